# Optimizing a Trainium2 kernel written in Bass

```python
import math
import jax, jax.numpy as jnp
from jax import lax
import numpy as np

D_MODEL = 2048
BATCH = 2
SEQ = 8192
DEPTH = 1
DEC_BATCH = 16
DEC_SEQ = 16
PAST_LEN = 4096

CHUNK = 64
FOX_HEADS = 8
FOX_HEAD_DIM = 128
FOX_WIDTH = FOX_HEADS * FOX_HEAD_DIM
FOX_SCALE = FOX_HEAD_DIM ** -0.5
Q_BLOCK = 128
FORGET_BIAS_INIT = 3.0
SGU_GROUPS = 8
SGU_CHUNK = 128
SGU_WIDTH = 1024
SGU_GROUP_DIM = SGU_WIDTH // SGU_GROUPS
PEER_HEADS = 8
PEER_N_KEYS = 128
PEER_N_EXPERTS = PEER_N_KEYS * PEER_N_KEYS
PEER_TOPK = 16
PEER_KEY_DIM = 256
PEER_BLOCK = 128
IN_WIDTH = 3 * FOX_WIDTH + FOX_HEADS + 2 * SGU_WIDTH + 2 * D_MODEL
RMS_EPS = 1e-6
NEG_INF = -1e30

kernel_name = 'fox_sgu_peer_stream_step'


def rms_norm(x, g):
    xf = x.astype(jnp.float32)
    y = xf * lax.rsqrt(jnp.mean(xf * xf, axis=-1, keepdims=True) + RMS_EPS)
    return (y * g.astype(jnp.float32)).astype(x.dtype)


def mixer_inputs(h, w_in, b_forget, q_norm_g, k_norm_g, v_norm_g):
    b, s, _ = h.shape
    z = h @ w_in
    o1 = FOX_WIDTH
    o2 = 2 * FOX_WIDTH
    o3 = 3 * FOX_WIDTH
    o4 = o3 + FOX_HEADS
    o5 = o4 + SGU_WIDTH
    o6 = o5 + SGU_WIDTH
    o7 = o6 + D_MODEL
    q, k, v, f, u_s, v_s, gate_a, gate_b = jnp.split(z, [o1, o2, o3, o4, o5, o6, o7], axis=-1)
    q = rms_norm(q.reshape(b, s, FOX_HEADS, FOX_HEAD_DIM), q_norm_g)
    k = rms_norm(k.reshape(b, s, FOX_HEADS, FOX_HEAD_DIM), k_norm_g)
    v = v.reshape(b, s, FOX_HEADS, FOX_HEAD_DIM)
    logf = jax.nn.log_sigmoid((f + b_forget).astype(jnp.float32))
    u_s = jax.nn.gelu(u_s)
    v_s = rms_norm(jax.nn.gelu(v_s), v_norm_g)
    return q, k, v, logf, u_s, v_s, gate_a, gate_b


def fox_block(q, fq, qpos, k, v, fk, kpos):
    logits = jnp.einsum('bqhd,bkhd->bhqk', q, k).astype(jnp.float32) * FOX_SCALE
    decay = jnp.transpose(fq, (0, 2, 1))[..., :, None] - jnp.transpose(fk, (0, 2, 1))[..., None, :]
    mask = kpos[None, :] <= qpos[:, None]
    logits = jnp.where(mask, logits + decay, NEG_INF)
    p = jax.nn.softmax(logits, axis=-1).astype(v.dtype)
    return jnp.einsum('bhqk,bkhd->bqhd', p, v)


def fox_prompt(q, k, v, logf):
    b, s = q.shape[0], q.shape[1]
    nb = s // Q_BLOCK
    fcum = jnp.cumsum(logf, axis=1)
    pos = jnp.arange(s)
    qb = q.reshape(b, nb, Q_BLOCK, FOX_HEADS, FOX_HEAD_DIM).swapaxes(0, 1)
    fb = fcum.reshape(b, nb, Q_BLOCK, FOX_HEADS).swapaxes(0, 1)
    pb = pos.reshape(nb, Q_BLOCK)
    out = lax.map(lambda a: fox_block(a[0], a[1], a[2], k, v, fcum, pos), (qb, fb, pb))
    return out.swapaxes(0, 1).reshape(b, s, FOX_WIDTH)


def fox_sample(q, k, v, logf, cache_k, cache_v, cache_logf):
    b, t = q.shape[0], q.shape[1]
    p = cache_k.shape[1]
    k_all = jnp.concatenate([cache_k.astype(k.dtype), k], axis=1)
    v_all = jnp.concatenate([cache_v.astype(v.dtype), v], axis=1)
    fcum = jnp.cumsum(jnp.concatenate([cache_logf.astype(jnp.float32), logf], axis=1), axis=1)
    kpos = jnp.arange(p + t)
    qpos = p + jnp.arange(t)
    out = fox_block(q, fcum[:, p:], qpos, k_all, v_all, fcum, kpos)
    return out.reshape(b, t, FOX_WIDTH)


def sgu_prompt(u, v, w_s, b_s):
    b, s, _ = u.shape
    nc = s // SGU_CHUNK
    vb = v.reshape(b, nc, SGU_CHUNK, SGU_GROUPS, SGU_GROUP_DIM)
    w = jnp.tril(w_s)
    mixed = jnp.einsum('gij,bnjgc->bnigc', w, vb) + jnp.transpose(b_s)[:, :, None].astype(v.dtype)
    return u * mixed.reshape(b, s, SGU_WIDTH)


def sgu_sample(u, v, w_s, b_s):
    b, t, _ = u.shape
    vb = v.reshape(b, t, SGU_GROUPS, SGU_GROUP_DIM)
    w = jnp.tril(w_s)[:, :t, :t]
    mixed = jnp.einsum('gij,bjgc->bigc', w, vb) + jnp.transpose(b_s[:, :t])[:, :, None].astype(v.dtype)
    return u * mixed.reshape(b, t, SGU_WIDTH)


def merge_branches(o_a, o_b, gate_a, gate_b, w_out_a, w_out_b, w_out):
    y = jax.nn.sigmoid(gate_a) * (o_a @ w_out_a) + jax.nn.sigmoid(gate_b) * (o_b @ w_out_b)
    return y @ w_out


def peer_ffn(xn, w_q, q_norm_g, sub_keys, table_u, table_v):
    b, s, d = xn.shape
    n = b * s
    xf = xn.reshape(n, d)
    pad = (-n) % PEER_BLOCK
    xf = jnp.pad(xf, ((0, pad), (0, 0)))
    blocks = xf.reshape(-1, PEER_BLOCK, d)

    def one_block(xb):
        q = rms_norm((xb @ w_q).reshape(-1, PEER_HEADS, PEER_KEY_DIM), q_norm_g)
        half = PEER_KEY_DIM // 2
        s1 = jnp.einsum('nhd,hkd->nhk', q[..., :half], sub_keys[:, 0]).astype(jnp.float32)
        s2 = jnp.einsum('nhd,hkd->nhk', q[..., half:], sub_keys[:, 1]).astype(jnp.float32)
        t1, i1 = lax.top_k(s1, PEER_TOPK)
        t2, i2 = lax.top_k(s2, PEER_TOPK)
        cand = (t1[..., :, None] + t2[..., None, :]).reshape(-1, PEER_HEADS, PEER_TOPK * PEER_TOPK)
        cidx = (i1[..., :, None] * PEER_N_KEYS + i2[..., None, :]).reshape(-1, PEER_HEADS, PEER_TOPK * PEER_TOPK)
        st, sel = lax.top_k(cand, PEER_TOPK)
        eidx = jnp.take_along_axis(cidx, sel, axis=-1)
        g = jax.nn.softmax(st, axis=-1)
        u_sel = jnp.take(table_u, eidx, axis=0)
        a = jax.nn.gelu(jnp.einsum('nd,nhkd->nhk', xb, u_sel).astype(jnp.float32))
        v_sel = jnp.take(table_v, eidx, axis=0)
        return jnp.einsum('nhk,nhkd->nd', (g * a).astype(xb.dtype), v_sel)

    y = lax.map(one_block, blocks).reshape(-1, d)[:n]
    return y.reshape(b, s, d)


def setup_inputs(seed: int = 0) -> dict:
    key = jax.random.key(seed)
    ks = jax.random.split(key, 22)
    L = DEPTH

    def nrm(k, shape, scale):
        return jax.random.normal(k, shape, jnp.float32) * scale

    return {
        'x_prompt': nrm(ks[0], (BATCH, SEQ, D_MODEL), 1.0),
        'x_sample': nrm(ks[1], (DEC_BATCH, DEC_SEQ, D_MODEL), 1.0),
        'cache_k': nrm(ks[2], (L, DEC_BATCH, PAST_LEN, FOX_HEADS, FOX_HEAD_DIM), 1.0),
        'cache_v': nrm(ks[3], (L, DEC_BATCH, PAST_LEN, FOX_HEADS, FOX_HEAD_DIM), 1.0),
        'cache_logf': jax.nn.log_sigmoid(FORGET_BIAS_INIT + nrm(ks[4], (L, DEC_BATCH, PAST_LEN, FOX_HEADS), 1.0)),
        'norm_mix_g': 1.0 + nrm(ks[5], (L, D_MODEL), 0.02),
        'w_in': nrm(ks[6], (L, D_MODEL, IN_WIDTH), D_MODEL ** -0.5),
        'b_forget': FORGET_BIAS_INIT + nrm(ks[7], (L, FOX_HEADS), 0.5),
        'q_norm_g': 1.0 + nrm(ks[8], (L, FOX_HEAD_DIM), 0.02),
        'k_norm_g': 1.0 + nrm(ks[9], (L, FOX_HEAD_DIM), 0.02),
        'v_norm_g': 1.0 + nrm(ks[10], (L, SGU_WIDTH), 0.02),
        'w_spatial': nrm(ks[11], (L, SGU_GROUPS, SGU_CHUNK, SGU_CHUNK), SGU_CHUNK ** -0.5),
        'b_spatial': 1.0 + nrm(ks[12], (L, SGU_GROUPS, SGU_CHUNK), 0.1),
        'w_out_a': nrm(ks[13], (L, FOX_WIDTH, D_MODEL), FOX_WIDTH ** -0.5),
        'w_out_b': nrm(ks[14], (L, SGU_WIDTH, D_MODEL), SGU_WIDTH ** -0.5),
        'w_out': nrm(ks[15], (L, D_MODEL, D_MODEL), D_MODEL ** -0.5),
        'norm_ffn_g': 1.0 + nrm(ks[16], (L, D_MODEL), 0.02),
        'w_peer_q': nrm(ks[17], (L, D_MODEL, PEER_HEADS * PEER_KEY_DIM), D_MODEL ** -0.5),
        'peer_q_norm_g': 1.0 + nrm(ks[18], (L, PEER_KEY_DIM), 0.02),
        'peer_sub_keys': nrm(ks[19], (L, PEER_HEADS, 2, PEER_N_KEYS, PEER_KEY_DIM // 2), (PEER_KEY_DIM // 2) ** -0.5),
        'peer_u': nrm(ks[20], (L, PEER_N_EXPERTS, D_MODEL), D_MODEL ** -0.5),
        'peer_v': nrm(ks[21], (L, PEER_N_EXPERTS, D_MODEL), 0.1),
    }


def reference(x_prompt, x_sample, cache_k, cache_v, cache_logf, norm_mix_g, w_in, b_forget,
              q_norm_g, k_norm_g, v_norm_g, w_spatial, b_spatial, w_out_a, w_out_b, w_out,
              norm_ffn_g, w_peer_q, peer_q_norm_g, peer_sub_keys, peer_u, peer_v):
    xp = x_prompt
    xs = x_sample
    kp_l, vp_l, fp_l = [], [], []
    ks_l, vs_l, fs_l, us_l = [], [], [], []
    for l in range(DEPTH):
        h = rms_norm(xp, norm_mix_g[l])
        q, k, v, logf, u_s, v_s, ga, gb = mixer_inputs(h, w_in[l], b_forget[l], q_norm_g[l], k_norm_g[l], v_norm_g[l])
        o_a = fox_prompt(q, k, v, logf)
        o_b = sgu_prompt(u_s, v_s, w_spatial[l], b_spatial[l])
        xp = xp + merge_branches(o_a, o_b, ga, gb, w_out_a[l], w_out_b[l], w_out[l])
        xp = xp + peer_ffn(rms_norm(xp, norm_ffn_g[l]), w_peer_q[l], peer_q_norm_g[l], peer_sub_keys[l], peer_u[l], peer_v[l])
        kp_l.append(k)
        vp_l.append(v)
        fp_l.append(logf)

        h = rms_norm(xs, norm_mix_g[l])
        q, k, v, logf, u_s, v_s, ga, gb = mixer_inputs(h, w_in[l], b_forget[l], q_norm_g[l], k_norm_g[l], v_norm_g[l])
        o_a = fox_sample(q, k, v, logf, cache_k[l], cache_v[l], cache_logf[l])
        o_b = sgu_sample(u_s, v_s, w_spatial[l], b_spatial[l])
        xs = xs + merge_branches(o_a, o_b, ga, gb, w_out_a[l], w_out_b[l], w_out[l])
        xs = xs + peer_ffn(rms_norm(xs, norm_ffn_g[l]), w_peer_q[l], peer_q_norm_g[l], peer_sub_keys[l], peer_u[l], peer_v[l])
        ks_l.append(k)
        vs_l.append(v)
        fs_l.append(logf)
        us_l.append(v_s)
    return (xp, xs, jnp.stack(kp_l), jnp.stack(vp_l), jnp.stack(fp_l),
            jnp.stack(ks_l), jnp.stack(vs_l), jnp.stack(fs_l), jnp.stack(us_l))
```

```python
import numpy as np
from contextlib import ExitStack
import concourse.bass as bass
import concourse.mybir as mybir
from concourse.bass_utils import run_bass_kernel_spmd

F32 = mybir.dt.float32
BF16 = mybir.dt.bfloat16
I32 = mybir.dt.int32
U32 = mybir.dt.uint32
ALU = mybir.AluOpType
AF = mybir.ActivationFunctionType
AX = mybir.AxisListType

D = 2048
KC = 16
H = 8
HD = 128
O_Q, O_K, O_V, O_F, O_U, O_VS, O_GA, O_GB = 0, 1024, 2048, 3072, 3080, 4104, 5128, 7176
INW = 9224
RMS_EPS = 1e-6
SCALE = HD ** -0.5
GELU_C = 0.7978845608028654
TOPK = 16
NEG = -1e30
ENGS = ("pe", "act", "dve", "pool", "sp")
SAME_ENG_SYNC = True
GSZ = 3
NGB = 6


class T:
    __slots__ = ("lw", "rd", "slot")

    def __init__(self):
        self.lw = None
        self.rd = []
        self.slot = None


class Prog:
    def __init__(self, nc, es):
        self.nc = nc
        self.es = es
        self.ops = {e: [] for e in ENGS}
        self.cnt = {e: 0 for e in ENGS}
        self.esem = {e: es.enter_context(nc.semaphore("sem_" + e)) for e in ENGS}
        self.slots = []
        self.nobar = set()

    def slot_of(self, t):
        if t.slot is None:
            sem = self.es.enter_context(self.nc.semaphore("ds%d" % len(self.slots)))
            t.slot = [sem, 0]
            self.slots.append(t.slot)
        return t.slot

    def _deps(self, reads, writes, extra):
        toks = list(extra)
        for t in reads:
            if t.lw is not None:
                toks.append(t.lw)
        for t in writes:
            if t.lw is not None:
                toks.append(t.lw)
            toks.extend(t.rd)
        return toks

    def _upd(self, reads, writes, tok):
        for t in reads:
            t.rd.append(tok)
        for t in writes:
            t.lw = tok
            t.rd = []

    def op(self, eng, fn, reads=(), writes=(), extra=()):
        toks = self._deps(reads, writes, extra)
        self.cnt[eng] += 1
        tok = ("e", eng, self.cnt[eng])
        self.ops[eng].append((toks, fn, self.esem[eng], 1))
        self._upd(reads, writes, tok)
        return tok

    def dma(self, q, fn, reads=(), writes=(), slot_t=None, extra=()):
        toks = self._deps(reads, writes, extra)
        st = slot_t if slot_t is not None else (writes[0] if writes else reads[0])
        slot = self.slot_of(st)
        slot[1] += 16
        tok = ("s", slot[0], slot[1])
        self.ops[q].append((toks, fn, slot[0], 16))
        self._upd(reads, writes, tok)
        return tok

    def barrier(self, final=False):
        toks = [("e", e, self.cnt[e]) for e in ENGS if self.cnt[e] > 0]
        toks += [("s", s[0], s[1]) for s in self.slots if s[1] > 0 and (final or id(s) not in self.nobar)]
        for e in ENGS:
            self.ops[e].append((list(toks), None, None, 0))

    def emit(self):
        nc = self.nc
        self.barrier(final=True)
        with nc.Block() as blk:
            def replay(engname, e):
                waited = {}
                for toks, fn, sem, inc in self.ops[engname]:
                    for tk in toks:
                        if tk[0] == "e":
                            if tk[1] == engname and (engname == "pe" or not SAME_ENG_SYNC):
                                continue
                            s, v = self.esem[tk[1]], tk[2]
                        else:
                            s, v = tk[1], tk[2]
                        key = id(s)
                        if waited.get(key, 0) >= v:
                            continue
                        waited[key] = v
                        e.wait_ge(s, v)
                    if fn is not None:
                        fn(e).then_inc(sem, inc)

            @blk.tensor
            def _(e):
                replay("pe", e)

            @blk.scalar
            def _(e):
                replay("act", e)

            @blk.vector
            def _(e):
                replay("dve", e)

            @blk.gpsimd
            def _(e):
                replay("pool", e)

            @blk.sync
            def _(e):
                replay("sp", e)


class Buf:
    __slots__ = ("ap", "t")

    def __init__(self, ap):
        self.ap = ap
        self.t = T()


class Arena:
    def __init__(self, ap_f32, nfl):
        self.base = ap_f32
        self.n = nfl
        self.off = 0

    def mark(self):
        return self.off

    def release(self, m):
        self.off = m

    def alloc(self, shape, dt=F32):
        ne = int(np.prod(shape))
        esz = {F32: 4, BF16: 2, I32: 4, U32: 4}[dt]
        nfl = (ne * esz + 3) // 4
        nfl = (nfl + 1) // 2 * 2
        assert self.off + nfl <= self.n, "arena overflow: need %d have %d" % (self.off + nfl, self.n)
        v = self.base[:, self.off:self.off + nfl]
        self.off += nfl
        if dt != F32:
            v = v.bitcast(dt)
        v = v[:, 0:ne]
        if len(shape) == 2:
            v = v.rearrange("p (a b) -> p a b", a=shape[0])
        elif len(shape) == 3:
            v = v.rearrange("p (a b c) -> p a b c", a=shape[0], b=shape[1])
        return Buf(v)


class Cfg:
    def __init__(self, seq, past, nk):
        self.SEQ = seq
        self.PAST = past
        self.NK = nk
        self.NE = nk * nk
        self.NT = seq // 128
        assert self.NT % 8 == 0
        self.NOWN = self.NT // 4
        self.NPT = past // 128
        assert past % 128 == 0


def own_tiles(cfg, j):
    out = []
    for m in range(cfg.NT // 8):
        out += [8 * m + j, 8 * m + 7 - j]
    return out


def slot_lo(i):
    m, e = divmod(i, 2)
    return 8 * m + (0 if e == 0 else 4)


def build(cfg):
    nc = bass.Bass("TRN2", target_bir_lowering=False)
    SEQ, PAST, NK, NE, NT, NOWN, NPT = cfg.SEQ, cfg.PAST, cfg.NK, cfg.NE, cfg.NT, cfg.NOWN, cfg.NPT
    NROW = NOWN * 128
    es = ExitStack()
    with es:
        def din(n, s, d=F32):
            return nc.dram_tensor(n, list(s), d, kind="ExternalInput").ap()

        def dout(n, s, d=F32):
            return nc.dram_tensor(n, list(s), d, kind="ExternalOutput").ap()

        xb = din("xb", [SEQ, D])
        xo = din("xo", [NROW, D])
        xs = din("xs", [2, 16, D])
        ck = din("ck", [2, PAST, 1024])
        cv = din("cv", [2, PAST, 1024])
        clf = din("clf", [2, PAST, 8])
        norm_mix_g = din("norm_mix_g", [1, D])
        w_in = din("w_in", [D, INW])
        b_forget = din("b_forget", [1, 8])
        q_norm_g = din("q_norm_g", [1, 128])
        k_norm_g = din("k_norm_g", [1, 128])
        v_norm_g = din("v_norm_g", [1, 1024])
        w_spatial = din("w_spatial", [8, 128, 128])
        b_spatial = din("b_spatial", [8, 128])
        w_out_a = din("w_out_a", [1024, D])
        w_out_b = din("w_out_b", [1024, D])
        w_out = din("w_out", [D, D])
        norm_ffn_g = din("norm_ffn_g", [1, D])
        w_peer_q = din("w_peer_q", [D, D])
        peer_q_norm_g = din("peer_q_norm_g", [1, 256])
        sub_keys = din("peer_sub_keys", [8, 2, NK, 128])
        peer_u = din("peer_u", [NE, D])
        peer_v = din("peer_v", [NE, D])
        c_ident = din("c_ident", [128, 128])
        c_tri = din("c_tri", [128, 128])
        c_e127 = din("c_e127", [128, 128])
        c_triblk = din("c_triblk", [128, 128])
        c_iota = din("c_iota", [128, 256])
        c_sel = din("c_sel", [1, NOWN * NT])
        c_mk = din("c_mk", [128, NOWN * 4 * 128])
        c_pen = din("c_pen", [1, NOWN * 4])

        y_o = dout("y", [NROW, D])
        ys_o = dout("ys", [2, 16, D])
        ko_o = dout("ko", [NROW, 1024])
        vo_o = dout("vo", [NROW, 1024])
        lfo_o = dout("lfo", [NROW, 8])
        kso_o = dout("kso", [2, 16, 1024])
        vso_o = dout("vso", [2, 16, 1024])
        lfs_o = dout("lfs", [2, 16, 8])
        sgv_o = dout("sgv", [2, 16, 1024])

        KTs = nc.dram_tensor("KTs", [H, 128, SEQ], BF16, kind="Internal").ap()
        Vs = nc.dram_tensor("Vs", [H, SEQ, 128], BF16, kind="Internal").ap()
        OAs = nc.dram_tensor("OAs", [NROW + 128, 1024], BF16, kind="Internal").ap()
        UVb = nc.dram_tensor("UVb", [NE, 2 * D], BF16, kind="Internal").ap()
        Ub = UVb[:, 0:D]
        Vb = UVb[:, D:2 * D]
        t_conv = T()

        NFL = 52500
        arena_t = es.enter_context(nc.sbuf_tensor("arena", [128, NFL], F32))
        A = Arena(arena_t[:, :], NFL)
        pacc = es.enter_context(nc.psum_tensor("pacc", [128, 2048], F32))
        pw = es.enter_context(nc.psum_tensor("pw", [128, 2048], F32))
        PB = [Buf(pacc[:, b * 512:(b + 1) * 512]) for b in range(4)] + [Buf(pw[:, b * 512:(b + 1) * 512]) for b in range(4)]
        P = Prog(nc, es)

        def ld(q, dst, src_ap, dst_ap=None, extra=()):
            d_ap = dst.ap if dst_ap is None else dst_ap
            return P.dma(q, lambda e, d_ap=d_ap, s=src_ap: e.dma_start(out=d_ap, in_=s), writes=[dst.t], extra=extra)

        def ld_nc(q, dst, src_ap, dst_ap=None):
            d_ap = dst.ap if dst_ap is None else dst_ap
            return P.dma(q, lambda e, d_ap=d_ap, s=src_ap: e.dma_start(out=d_ap, in_=s, allow_slow_non_contiguous=True), writes=[dst.t])

        def st(q, dst_ap, src, src_ap=None, dst_t=None, extra=()):
            s_ap = src.ap if src_ap is None else src_ap
            w = [dst_t] if dst_t is not None else []
            return P.dma(q, lambda e, d=dst_ap, s=s_ap: e.dma_start(out=d, in_=s), reads=[src.t], writes=w, slot_t=src.t, extra=extra)

        def mm(out_ap, out_b, lhsT_ap, lhsT_b, rhs_ap, rhs_b, start, stop):
            return P.op("pe", lambda e, o=out_ap, l=lhsT_ap, r=rhs_ap, s=start, p=stop: e.matmul(out=o, lhsT=l, rhs=r, start=s, stop=p),
                        reads=[lhsT_b.t, rhs_b.t], writes=[out_b.t])

        def tp(out_ap, out_b, in_ap, in_b, ident, n=128):
            return P.op("pe", lambda e, o=out_ap, i=in_ap, idn=ident.ap[0:n, 0:n]: e.transpose(out=o, in_=i, identity=idn),
                        reads=[in_b.t, ident.t], writes=[out_b.t])

        def act(out_ap, in_ap, func, reads, writes, **kw):
            return P.op("act", lambda e, o=out_ap, i=in_ap, f=func, kw=kw: e.activation(out=o, in_=i, func=f, **kw),
                        reads=[b.t for b in reads], writes=[b.t for b in writes])

        def dve(fn, reads, writes):
            return P.op("dve", fn, reads=[b.t for b in reads], writes=[b.t for b in writes])

        def pool(fn, reads, writes):
            return P.op("pool", fn, reads=[b.t for b in reads], writes=[b.t for b in writes])

        def tt(eng, out_ap, in0, in1, op, reads, writes):
            return P.op(eng, lambda e, o=out_ap, a=in0, b=in1, op=op: e.tensor_tensor(out=o, in0=a, in1=b, op=op),
                        reads=[b.t for b in reads], writes=[b.t for b in writes])

        def ts(eng, out_ap, in0, s1, s2, op0, op1, reads, writes):
            if op1 is None:
                return P.op(eng, lambda e, o=out_ap, a=in0, s1=s1, op0=op0: e.tensor_scalar(out=o, in0=a, scalar1=s1, scalar2=None, op0=op0),
                            reads=[b.t for b in reads], writes=[b.t for b in writes])
            return P.op(eng, lambda e, o=out_ap, a=in0, s1=s1, s2=s2, op0=op0, op1=op1: e.tensor_scalar(out=o, in0=a, scalar1=s1, scalar2=s2, op0=op0, op1=op1),
                        reads=[b.t for b in reads], writes=[b.t for b in writes])

        def stt(out_ap, in0, scalar, in1, op0, op1, reads, writes, accum=None):
            if accum is None:
                return dve(lambda e, o=out_ap, a=in0, s=scalar, b=in1, op0=op0, op1=op1: e.scalar_tensor_tensor(out=o, in0=a, scalar=s, in1=b, op0=op0, op1=op1), reads, writes)
            return dve(lambda e, o=out_ap, a=in0, s=scalar, b=in1, op0=op0, op1=op1, ac=accum: e.scalar_tensor_tensor(out=o, in0=a, scalar=s, in1=b, op0=op0, op1=op1, accum_out=ac), reads, writes)

        def cp(eng, out_ap, in_ap, reads, writes):
            return P.op(eng, lambda e, o=out_ap, i=in_ap: e.tensor_copy(out=o, in_=i), reads=[b.t for b in reads], writes=[b.t for b in writes])

        def memset(eng, buf, ap, val):
            return P.op(eng, lambda e, a=ap, v=val: e.memset(a, v), writes=[buf.t])

        ident_f = A.alloc([128]); ld("sp", ident_f, c_ident)
        ident_b = A.alloc([128], BF16); ld("pool", ident_b, c_ident)
        tri_f = A.alloc([128]); ld("sp", tri_f, c_tri)
        tri_b = A.alloc([128], BF16); ld("pool", tri_b, c_tri)
        e127 = A.alloc([128]); ld("sp", e127, c_e127)
        triblk = A.alloc([128]); ld("sp", triblk, c_triblk)
        iota = A.alloc([256]); ld("sp", iota, c_iota)
        gmix = A.alloc([D]); ld("sp", gmix, norm_mix_g.partition_broadcast(128))
        gffn = A.alloc([D]); ld("sp", gffn, norm_ffn_g.partition_broadcast(128))
        gq = A.alloc([128]); ld("sp", gq, q_norm_g.partition_broadcast(128))
        gk = A.alloc([128]); ld("sp", gk, k_norm_g.partition_broadcast(128))
        gv = A.alloc([1024]); ld("sp", gv, v_norm_g.partition_broadcast(128))
        bfg = A.alloc([8]); ld("sp", bfg, b_forget.partition_broadcast(128))
        gpq = A.alloc([256]); ld("sp", gpq, peer_q_norm_g.partition_broadcast(128))
        zero1 = A.alloc([2]); memset("dve", zero1, zero1.ap, 0.0)
        m_persist = A.mark()

        def front(rows, nr, xf, hb, hT, junk, stat, gain, pbs, zero_pad=False):
            if zero_pad:
                memset("pool", xf, xf.ap[0:nr, :], 0.0)
            for (src, p0, n) in rows:
                ld("pool", xf, src, xf.ap[p0:p0 + n, :])
            rstd_of(xf, nr, D, junk, stat)
            stt(hb.ap[0:nr, :], xf.ap[0:nr, :], stat.ap[0:nr, 2:3], gain.ap[0:nr, :], ALU.mult, ALU.mult, [xf, stat, gain], [hb])
            transpose16(hb, nr, hT, pbs)

        def rstd_of(xf, nr, n, junk, stat):
            act(junk.ap[0:nr, 0:n], xf.ap[0:nr, 0:n], AF.Square, [xf], [junk, stat], accum_out=stat.ap[0:nr, 0:1])
            act(stat.ap[0:nr, 1:2], stat.ap[0:nr, 0:1], AF.Ln, [stat, eps_b], [stat], scale=1.0 / n, bias=eps_b.ap[0:nr, 0:1])
            act(stat.ap[0:nr, 2:3], stat.ap[0:nr, 1:2], AF.Exp, [stat], [stat], scale=-0.5)

        def transpose16(hb, nr, hT, pbs, nchunk=KC):
            for half in range((nchunk + 7) // 8):
                pb = pbs[half % len(pbs)]
                pv = pb.ap.bitcast(BF16)
                n8 = min(8, nchunk - half * 8)
                for c in range(n8):
                    kc = half * 8 + c
                    tp(pv[:, c * 128:c * 128 + nr], pb, hb.ap[0:nr, kc * 128:(kc + 1) * 128], hb, ident_b, nr)
                src = pv[:, 0:n8 * 128].rearrange("p (a b) -> p a b", a=n8)[:, :, 0:nr]
                dst = hT.ap[:, half * 8:half * 8 + n8, 0:nr]
                if half % 2 == 0:
                    cp("dve", dst, src, [pb], [hT])
                else:
                    act(dst, src, AF.Copy, [pb], [hT])

        eps_b = A.alloc([2]); memset("dve", eps_b, eps_b.ap, RMS_EPS)
        one_b = A.alloc([2]); memset("dve", one_b, one_b.ap, 1.0)
        m_persist = A.mark()

        def load_w(wt, w_dram, k_rows, c0, ncols, wap=None):
            kc = k_rows // 128
            src = w_dram[:, c0:c0 + ncols].rearrange("(kc p) n -> p kc n", p=128)
            dst = wt.ap[:, 0:kc, 0:ncols] if wap is None else wap
            return ld("pool", wt, src, dst)

        m_ab = A.mark()
        NFK = A.alloc([NT, 8])
        CE = A.alloc([NT, 8])
        mkb = A.alloc([NOWN * 4, 128], BF16); ld("pool", mkb, c_mk, mkb.ap.rearrange("p a b -> p (a b)"))
        CEsel = A.alloc([NOWN, 8])
        QT = A.alloc([H, NROW], BF16)
        QTs = A.alloc([H, 48], BF16)
        kTs_new = A.alloc([H, 48], BF16)
        vb_new = A.alloc([H, 132], BF16)
        nl_new = A.alloc([8])
        m_wa = A.mark()
        WA = A.alloc([KC, 2056], BF16)
        load_w(WA, w_in, D, O_K, 2056)

        xf1 = A.alloc([D])
        xf2 = [xf1, xf1]
        hb = A.alloc([D], BF16)
        hT2 = [A.alloc([KC, 128], BF16) for _ in range(2)]
        stat = A.alloc([4])
        k_sb = A.alloc([1024])
        tmpf = A.alloc([1024])
        junk = Buf(tmpf.ap.bitcast(BF16)); junk.t = tmpf.t
        kn = A.alloc([1024])
        v_sb = A.alloc([1024])
        kT_st = [A.alloc([H, 128], BF16) for _ in range(2)]
        vb_st = [A.alloc([1024], BF16) for _ in range(2)]
        ss8 = A.alloc([8]); r8 = A.alloc([8]); l8 = A.alloc([8])
        f_sb = A.alloc([8]); lf_sb = A.alloc([8]); fc_sb = A.alloc([8]); ce_prev = A.alloc([8])
        memset("dve", ce_prev, ce_prev.ap, 0.0)
        selb = A.alloc([NOWN, NT]); ld("sp", selb, c_sel.partition_broadcast(128), selb.ap.rearrange("p a b -> p (a b)"))

        def qknorm(src, nr, gain, dst, scr):
            s3 = src.ap[0:nr, :].rearrange("p (h d) -> p h d", h=H)
            tt("dve", scr.ap[0:nr, :], src.ap[0:nr, :], src.ap[0:nr, :], ALU.mult, [src], [scr])
            dve(lambda e, o=ss8.ap[0:nr, :], i=scr.ap[0:nr, :].rearrange("p (h d) -> p h d", h=H): e.tensor_reduce(out=o, in_=i, axis=AX.X, op=ALU.add), [scr], [ss8])
            act(l8.ap[0:nr, :], ss8.ap[0:nr, :], AF.Ln, [ss8, eps_b], [l8], scale=1.0 / HD, bias=eps_b.ap[0:nr, 0:1])
            act(r8.ap[0:nr, :], l8.ap[0:nr, :], AF.Exp, [l8], [r8], scale=-0.5)
            tt("dve", scr.ap[0:nr, :].rearrange("p (h d) -> p h d", h=H), s3, r8.ap[0:nr, :].unsqueeze(2).to_broadcast([nr, H, HD]), ALU.mult, [src, r8], [scr])
            tt("dve", dst.ap[0:nr, :].rearrange("p (h d) -> p h d", h=H), scr.ap[0:nr, :].rearrange("p (h d) -> p h d", h=H),
               gain.ap[0:nr, :].unsqueeze(1).to_broadcast([nr, H, HD]), ALU.mult, [scr, gain], [dst])

        def logf_of(nr):
            tt("dve", f_sb.ap[0:nr, :], f_sb.ap[0:nr, :], bfg.ap[0:nr, :], ALU.add, [f_sb, bfg], [f_sb])
            act(lf_sb.ap[0:nr, :], f_sb.ap[0:nr, :], AF.Exp, [f_sb], [lf_sb], scale=-1.0)
            ts("dve", lf_sb.ap[0:nr, :], lf_sb.ap[0:nr, :], 1.0, None, ALU.add, None, [lf_sb], [lf_sb])
            act(lf_sb.ap[0:nr, :], lf_sb.ap[0:nr, :], AF.Ln, [lf_sb], [lf_sb])
            ts("dve", lf_sb.ap[0:nr, :], lf_sb.ap[0:nr, :], -1.0, None, ALU.mult, None, [lf_sb], [lf_sb])

        kn2 = [kn, A.alloc([1024])]
        lf2 = [lf_sb, A.alloc([8])]

        def logf_of2(nr, lf):
            tt("dve", f_sb.ap[0:nr, :], f_sb.ap[0:nr, :], bfg.ap[0:nr, :], ALU.add, [f_sb, bfg], [f_sb])
            act(lf.ap[0:nr, :], f_sb.ap[0:nr, :], AF.Exp, [f_sb], [lf], scale=-1.0)
            ts("dve", lf.ap[0:nr, :], lf.ap[0:nr, :], 1.0, None, ALU.add, None, [lf], [lf])
            act(lf.ap[0:nr, :], lf.ap[0:nr, :], AF.Ln, [lf], [lf])
            ts("dve", lf.ap[0:nr, :], lf.ap[0:nr, :], -1.0, None, ALU.mult, None, [lf], [lf])

        def kvf_tile(it, rows, nr, owned, out_row0=None, sample=False):
            xf = xf2[it % 2]; hT = hT2[it % 2]
            knc = kn2[it % 2]; lf = lf2[it % 2]
            front(rows, nr, xf, hb, hT, junk, stat, gmix, [PB[4], PB[5]], zero_pad=sample)
            for blk in range(4):
                for kc in range(KC):
                    mm(PB[blk].ap[0:nr, :], PB[blk], hT.ap[:, kc, 0:nr], hT, WA.ap[:, kc, blk * 512:(blk + 1) * 512], WA, kc == 0, kc == KC - 1)
            for kc in range(KC):
                mm(PB[6].ap[0:nr, 0:8], PB[6], hT.ap[:, kc, 0:nr], hT, WA.ap[:, kc, 2048:2056], WA, kc == 0, kc == KC - 1)
            act(k_sb.ap[0:nr, 0:512], PB[0].ap[0:nr, :], AF.Copy, [PB[0]], [k_sb])
            act(k_sb.ap[0:nr, 512:1024], PB[1].ap[0:nr, :], AF.Copy, [PB[1]], [k_sb])
            cp("dve", f_sb.ap[0:nr, :], PB[6].ap[0:nr, 0:8], [PB[6]], [f_sb])
            kT = kT_st[it % 2]; vb = vb_st[it % 2]
            if owned:
                act(v_sb.ap[0:nr, 0:512], PB[2].ap[0:nr, :], AF.Copy, [PB[2]], [v_sb])
                act(v_sb.ap[0:nr, 512:1024], PB[3].ap[0:nr, :], AF.Copy, [PB[3]], [v_sb])
            else:
                act(vb.ap[:, 0:512], PB[2].ap[:, :], AF.Copy, [PB[2]], [vb])
                act(vb.ap[:, 512:1024], PB[3].ap[:, :], AF.Copy, [PB[3]], [vb])
            qknorm(k_sb, nr, gk, knc, tmpf)
            logf_of2(nr, lf)
            t0 = it * 128
            if owned:
                if not sample:
                    st("sp", ko_o[out_row0:out_row0 + 128, :], knc)
                    st("sp", vo_o[out_row0:out_row0 + 128, :], v_sb)
                    st("sp", lfo_o[out_row0:out_row0 + 128, :], lf)
                else:
                    for s in range(2):
                        st("sp", kso_o[s], knc, knc.ap[32 * s:32 * s + 16, :])
                        st("sp", vso_o[s], v_sb, v_sb.ap[32 * s:32 * s + 16, :])
                        st("sp", lfs_o[s], lf, lf.ap[32 * s:32 * s + 16, :])
                    memset("pool", vb_new, vb_new.ap[0:48, :, 128:129], 1.0)
                    cp("dve", vb_new.ap[0:48, :, 0:128], v_sb.ap[0:48, :].rearrange("p (h d) -> p h d", h=H), [v_sb], [vb_new])
            else:
                st("sp", Vs[:, t0:t0 + 128, :].rearrange("h t d -> t h d"), vb, vb.ap.rearrange("p (h d) -> p h d", h=H))

            def stage2():
                if owned:
                    if sample:
                        for h in range(H):
                            tp(PB[7].ap[:, h * 48:h * 48 + 48], PB[7], knc.ap[0:48, h * 128:(h + 1) * 128], knc, ident_f, 48)
                        cp("dve", kTs_new.ap.rearrange("p h t -> p (h t)"), PB[7].ap[:, 0:H * 48], [PB[7]], [kTs_new])
                        mm(PB[6].ap[0:48, 8:16], PB[6], triblk.ap[0:48, 0:48], triblk, lf.ap[0:48, :], lf, True, True)
                        ts("dve", nl_new.ap[0:48, :], PB[6].ap[0:48, 8:16], -1.0, None, ALU.mult, None, [PB[6]], [nl_new])
                    return
                for half in range(2):
                    for c in range(4):
                        h = half * 4 + c
                        tp(PB[7].ap[:, c * 128:(c + 1) * 128], PB[7], knc.ap[:, h * 128:(h + 1) * 128], knc, ident_f)
                    cp("dve", kT.ap[:, half * 4:half * 4 + 4, :].rearrange("p h t -> p (h t)"), PB[7].ap[:, :], [PB[7]], [kT])
                st("sp", KTs[:, :, t0:t0 + 128].rearrange("h d t -> d h t"), kT)
                mm(PB[6].ap[:, 8:16], PB[6], tri_f.ap, tri_f, lf.ap, lf, True, False)
                mm(PB[6].ap[:, 8:16], PB[6], e127.ap, e127, ce_prev.ap, ce_prev, False, True)
                cp("dve", fc_sb.ap, PB[6].ap[:, 8:16], [PB[6]], [fc_sb])
                ts("dve", NFK.ap[:, it, :], fc_sb.ap, -1.0, None, ALU.mult, None, [fc_sb], [NFK])
                mm(PB[6].ap[:, 16:24], PB[6], e127.ap, e127, fc_sb.ap, fc_sb, True, True)
                cp("dve", CE.ap[:, it, :], PB[6].ap[:, 16:24], [PB[6]], [CE])
                cp("dve", ce_prev.ap, fc_sb.ap, [fc_sb], [ce_prev])
            return stage2

        conv_tok = []
        P.nobar.add(id(P.slot_of(t_conv)))
        CR = NE // NT
        prev2 = None
        ce_toks = []
        for t in range(NT):
            s2 = kvf_tile(t, [(xb[t * 128:(t + 1) * 128, :], 0, 128)], 128, owned=False)
            if prev2 is not None:
                prev2()
            prev2 = s2
            for (dst_, src_) in ((Ub, peer_u), (Vb, peer_v)):
                conv_tok.append(P.dma("pool", lambda e, d=dst_[t * CR:(t + 1) * CR, :], s_=src_[t * CR:(t + 1) * CR, :]: e.dma_start(out=d, in_=s_), slot_t=t_conv,
                                      extra=[ce_toks[-3]] if len(ce_toks) >= 3 else []))
            if CE.t.lw is not None:
                ce_toks.append(CE.t.lw)
        prev2()
        selscr = A.alloc([8, NT])
        for i in range(NOWN):
            tt("dve", selscr.ap, CE.ap.rearrange("p t h -> p h t"), selb.ap[:, i, :].unsqueeze(1).to_broadcast([128, 8, NT]), ALU.mult, [CE, selb], [selscr])
            dve(lambda e, o=CEsel.ap[:, i, :], i_=selscr.ap: e.tensor_reduce(out=o, in_=i_, axis=AX.X, op=ALU.add), [selscr], [CEsel])
        for i in range(NOWN):
            kvf_tile(i, [(xo[i * 128:(i + 1) * 128, :], 0, 128)], 128, owned=True, out_row0=i * 128)()
        kvf_tile(NOWN, [(xs[0], 0, 16), (xs[1], 32, 16)], 48, owned=True, sample=True)()
        P.barrier()
        load_w(WA, w_in, D, O_Q, 1024, WA.ap[:, :, 0:1024])
        for i in range(NOWN + 1):
            sample = (i == NOWN)
            nr = 48 if sample else 128
            rows = [(xs[0], 0, 16), (xs[1], 32, 16)] if sample else [(xo[i * 128:(i + 1) * 128, :], 0, 128)]
            xf = xf2[i % 2]; hT = hT2[i % 2]
            front(rows, nr, xf, hb, hT, junk, stat, gmix, [PB[4], PB[5]], zero_pad=sample)
            for blk in range(2):
                for kc in range(KC):
                    mm(PB[blk].ap[0:nr, :], PB[blk], hT.ap[:, kc, 0:nr], hT, WA.ap[:, kc, blk * 512:(blk + 1) * 512], WA, kc == 0, kc == KC - 1)
            act(k_sb.ap[0:nr, 0:512], PB[0].ap[0:nr, :], AF.Copy, [PB[0]], [k_sb])
            act(k_sb.ap[0:nr, 512:1024], PB[1].ap[0:nr, :], AF.Copy, [PB[1]], [k_sb])
            qknorm(k_sb, nr, gq, kn, tmpf)
            if sample:
                for h in range(H):
                    tp(PB[7].ap[:, h * 48:h * 48 + 48], PB[7], kn.ap[0:48, h * 128:(h + 1) * 128], kn, ident_f, 48)
                cp("dve", QTs.ap.rearrange("p h t -> p (h t)"), PB[7].ap[:, 0:H * 48], [PB[7]], [QTs])
            else:
                for half in range(2):
                    pb = PB[6 + half]
                    for c in range(4):
                        h = half * 4 + c
                        tp(pb.ap[:, c * 128:(c + 1) * 128], pb, kn.ap[:, h * 128:(h + 1) * 128], kn, ident_f)
                    dst = QT.ap[:, half * 4:half * 4 + 4, i * 128:(i + 1) * 128]
                    src = pb.ap.rearrange("p (h t) -> p h t", h=4)
                    if half == 0:
                        cp("dve", dst, src, [pb], [QT])
                    else:
                        act(dst, src, AF.Copy, [pb], [QT])
        P.barrier()
        A.release(m_wa)
        KTh = [A.alloc([SEQ], BF16) for _ in range(2)]
        Vh = [A.alloc([NT, 130], BF16) for _ in range(2)]
        for b in Vh:
            memset("pool", b, b.ap[:, :, 128:129], 1.0)
        BS = min(4, NOWN)
        NBLK = NOWN // BS
        TS = 4 * BS
        PTw = [A.alloc([BS * 128], BF16) for _ in range(4)]
        onesrow = A.alloc([128], BF16)
        memset("dve", onesrow, onesrow.ap, 0.0)
        memset("dve", onesrow, onesrow.ap[0:1, :], 1.0)
        drow2 = [A.alloc([BS * 128], BF16) for _ in range(2)]
        for d_ in drow2:
            memset("dve", d_, d_.ap, 0.0)
        dsl = A.alloc([BS])
        penb = A.alloc([NOWN * 4]); ld("sp", penb, c_pen.partition_broadcast(128))
        bb2 = [A.alloc([NT]) for _ in range(2)]
        bu2 = [A.alloc([4]) for _ in range(2)]
        o_st = [A.alloc([128], BF16) for _ in range(2)]
        rden = A.alloc([2])
        ucount = 0
        units = []

        def mk_bulk(h, n, t, pre):
            kth = KTh[h % 2]; vh = Vh[h % 2]
            s0 = BS * n
            drow = drow2[(h * NBLK + n) % 2]; bb = bb2[(h * NBLK + n) % 2]
            f = min(ii for ii in range(BS) if slot_lo(s0 + ii) > t)
            c0 = f * 128; Wd = BS * 128 - c0
            return dict(h=h, n=n, kind="bulk", t=t, f=f, c0=c0, Wd=Wd, pre=pre)

        for h in range(H):
            for n in range(NBLK):
                s0 = BS * n
                ntb = slot_lo(s0 + BS - 1)
                first = True
                for t in range(ntb):
                    u = mk_bulk(h, n, t, first); first = False
                    units.append(u)
                for ii in range(BS):
                    for r in range(4):
                        units.append(dict(h=h, n=n, kind="diag", ii=ii, r=r, pre=first, pre_slot=(r == 0)))
                        first = False
        started = {}

        def stage1(u, idx):
            h, n = u["h"], u["n"]
            kth = KTh[h % 2]; vh = Vh[h % 2]
            s0 = BS * n; tR = TS * (n + 1) - 1
            drow = drow2[(h * NBLK + n) % 2]; bb = bb2[(h * NBLK + n) % 2]
            if u["pre"]:
                if n == 0:
                    ld("sp", kth, KTs[h])
                    ld("sp", vh, Vs[h].rearrange("(t p) d -> p t d", p=128), vh.ap[:, :, 0:128])
                tt("dve", dsl.ap[0:1, :], CEsel.ap[0:1, s0:s0 + BS, h], CE.ap[0:1, tR, h:h + 1].to_broadcast([1, BS]), ALU.subtract, [CEsel, CE], [dsl])
                ts("dve", drow.ap[0:1, :].rearrange("p (a b) -> p a b", a=BS), dsl.ap[0:1, :].unsqueeze(2).to_broadcast([1, BS, 128]), 1.0 / SCALE, None, ALU.mult, None, [dsl], [drow])
                ntb = slot_lo(s0 + BS - 1)
                if ntb > 0:
                    ts("dve", bb.ap[:, 0:ntb], NFK.ap[:, 0:ntb, h], CE.ap[:, tR, h:h + 1], None, ALU.add, None, [NFK, CE], [bb])
            psb = PB[4 + idx % 4]; pt = PTw[idx % 4]
            if u["kind"] == "bulk":
                t, f, c0, Wd = u["t"], u["f"], u["c0"], u["Wd"]
                mm(psb.ap[:, c0:c0 + Wd], psb, kth.ap[:, t * 128:(t + 1) * 128], kth, QT.ap[:, h, (s0 + f) * 128:(s0 + BS) * 128], QT, True, False)
                mm(psb.ap[:, c0:c0 + Wd], psb, onesrow.ap, onesrow, drow.ap[:, c0:c0 + Wd], drow, False, True)
                act(pt.ap[:, c0:c0 + Wd], psb.ap[:, c0:c0 + Wd], AF.Exp, [psb, bb], [pt], scale=SCALE, bias=bb.ap[:, t:t + 1])
            else:
                ii, r = u["ii"], u["r"]
                i = s0 + ii; lo = slot_lo(i); t = lo + r
                bu = bu2[(h * NOWN + i) % 2]
                if u["pre_slot"]:
                    tt("dve", bu.ap, NFK.ap[:, lo:lo + 4, h], penb.ap[:, i * 4:i * 4 + 4], ALU.add, [NFK, penb], [bu])
                    ts("dve", bu.ap, bu.ap, CE.ap[:, tR, h:h + 1], None, ALU.add, None, [bu, CE], [bu])
                mm(psb.ap[:, 0:128], psb, kth.ap[:, t * 128:(t + 1) * 128], kth, QT.ap[:, h, i * 128:(i + 1) * 128], QT, True, False)
                mm(psb.ap[:, 0:128], psb, onesrow.ap, onesrow, drow.ap[:, ii * 128:(ii + 1) * 128], drow, False, True)
                act(pt.ap[:, 0:128], psb.ap[:, 0:128], AF.Exp, [psb, bu], [pt], scale=SCALE, bias=bu.ap[:, r:r + 1])
                tt("pool", pt.ap[:, 0:128], pt.ap[:, 0:128], mkb.ap[:, i * 4 + r, :], ALU.mult, [pt, mkb], [pt])

        def stage2(u, idx):
            h, n = u["h"], u["n"]
            vh = Vh[h % 2]
            s0 = BS * n
            pt = PTw[idx % 4]
            if u["kind"] == "bulk":
                t, f = u["t"], u["f"]
                for ii in range(f, BS):
                    key = (h, n, ii)
                    mm(PB[ii].ap[:, 0:129], PB[ii], pt.ap[:, ii * 128:(ii + 1) * 128], pt, vh.ap[:, t, 0:129], vh, key not in started, False)
                    started[key] = True
            else:
                ii, r = u["ii"], u["r"]
                i = s0 + ii; lo = slot_lo(i); t = lo + r
                key = (h, n, ii)
                mm(PB[ii].ap[:, 0:129], PB[ii], pt.ap[:, 0:128], pt, vh.ap[:, t, 0:129], vh, key not in started, r == 3)
                started[key] = True
                if r == 3:
                    po = PB[ii]
                    ob = o_st[(h * NOWN + i) % 2]
                    dve(lambda e, o=rden.ap[:, 0:1], i_=po.ap[:, 128:129]: e.reciprocal(out=o, in_=i_), [po], [rden])
                    ts("dve", ob.ap, po.ap[:, 0:128], rden.ap[:, 0:1], None, ALU.mult, None, [po, rden], [ob])
                    st("sp", OAs[i * 128:(i + 1) * 128, h * 128:(h + 1) * 128], ob)

        SKEW = 2
        for idx in range(len(units) + SKEW):
            if idx < len(units):
                stage1(units[idx], idx)
            if idx - SKEW >= 0:
                stage2(units[idx - SKEW], idx - SKEW)
        P.barrier()
        A.release(m_wa)
        ckf = [A.alloc([NPT, 128]) for _ in range(2)]
        KTc = [A.alloc([NPT * 128], BF16) for _ in range(2)]
        Vc = [A.alloc([NPT, 130], BF16) for _ in range(2)]
        for b in Vc:
            memset("pool", b, b.ap[:, :, 128:129], 1.0)
        clf_sb = A.alloc([2, NPT, 8])
        nfk_c = A.alloc([2, NPT, 8])
        bias_c = A.alloc([2, NPT, 8])
        ce_c = A.alloc([8]); fcc = A.alloc([8]); cend = A.alloc([8])
        PTs = [A.alloc([16], BF16) for _ in range(4)]
        oas_st = A.alloc([1024], BF16)
        memset("pool", oas_st, oas_st.ap, 0.0)
        rden2 = A.alloc([2])
        for s in range(2):
            ld("sp", clf_sb, clf[s].rearrange("(t p) h -> p t h", p=128), clf_sb.ap[:, s, :, :])
            memset("dve", ce_c, ce_c.ap, 0.0)
            for t in range(NPT):
                mm(PB[1].ap[:, 0:8], PB[1], tri_f.ap, tri_f, clf_sb.ap[:, s, t, :], clf_sb, True, False)
                mm(PB[1].ap[:, 0:8], PB[1], e127.ap, e127, ce_c.ap, ce_c, False, True)
                cp("dve", fcc.ap, PB[1].ap[:, 0:8], [PB[1]], [fcc])
                ts("dve", nfk_c.ap[:, s, t, :], fcc.ap, -1.0, None, ALU.mult, None, [fcc], [nfk_c])
                cp("dve", ce_c.ap, fcc.ap, [fcc], [ce_c])
            mm(PB[1].ap[:, 8:16], PB[1], e127.ap, e127, fcc.ap, fcc, True, True)
            cp("dve", cend.ap, PB[1].ap[:, 8:16], [PB[1]], [cend])
            tt("dve", bias_c.ap[:, s, :, :], nfk_c.ap[:, s, :, :], cend.ap.unsqueeze(1).to_broadcast([128, NPT, 8]), ALU.add, [nfk_c, cend], [bias_c])
        ucount = 0
        for s in range(2):
            r0 = 32 * s
            for h in range(H):
                u = s * H + h
                cf = ckf[u % 2]; ktc = KTc[u % 2]; vc = Vc[u % 2]
                ld("sp", cf, ck[s].rearrange("(t p) (h d) -> p t h d", p=128, h=H)[:, :, h, :])
                ld("pool", vc, cv[s].rearrange("(t p) (h d) -> p t h d", p=128, h=H)[:, :, h, :], vc.ap[:, :, 0:128])
                for g4 in range((NPT + 3) // 4):
                    pb = PB[2 + g4 % 2]
                    n4 = min(4, NPT - g4 * 4)
                    for c in range(n4):
                        t = g4 * 4 + c
                        tp(pb.ap[:, c * 128:(c + 1) * 128], pb, cf.ap[:, t, :], cf, ident_f)
                    if g4 % 2 == 0:
                        cp("dve", ktc.ap[:, g4 * 512:g4 * 512 + n4 * 128], pb.ap[:, 0:n4 * 128], [pb], [ktc])
                    else:
                        act(ktc.ap[:, g4 * 512:g4 * 512 + n4 * 128], pb.ap[:, 0:n4 * 128], AF.Copy, [pb], [ktc])
                po = PB[0]
                qT = QTs.ap[:, h, r0:r0 + 16]
                for t in range(NPT):
                    psb = PB[4 + ucount % 4]; pt = PTs[ucount % 4]; ucount += 1
                    mm(psb.ap[:, 0:16], psb, ktc.ap[:, t * 128:(t + 1) * 128], ktc, qT, QTs, True, True)
                    act(pt.ap, psb.ap[:, 0:16], AF.Exp, [psb, bias_c], [pt], scale=SCALE, bias=bias_c.ap[:, s, t, h:h + 1])
                    mm(po.ap[r0:r0 + 16, 0:129], po, pt.ap, pt, vc.ap[:, t, 0:129], vc, t == 0, False)
                psb = PB[4 + ucount % 4]; pt = PTs[ucount % 4]; ucount += 1
                mm(psb.ap[r0:r0 + 16, 0:16], psb, kTs_new.ap[:, h, r0:r0 + 16], kTs_new, qT, QTs, True, True)
                act(pt.ap[r0:r0 + 16, :], psb.ap[r0:r0 + 16, 0:16], AF.Exp, [psb, nl_new], [pt], scale=SCALE, bias=nl_new.ap[r0:r0 + 16, h:h + 1])
                tt("pool", pt.ap[r0:r0 + 16, :], pt.ap[r0:r0 + 16, :], tri_b.ap[r0:r0 + 16, r0:r0 + 16], ALU.mult, [pt, tri_b], [pt])
                mm(po.ap[r0:r0 + 16, 0:129], po, pt.ap[r0:r0 + 16, :], pt, vb_new.ap[r0:r0 + 16, h, 0:129], vb_new, False, True)
                dve(lambda e, o=rden2.ap[r0:r0 + 16, 0:1], i_=po.ap[r0:r0 + 16, 128:129]: e.reciprocal(out=o, in_=i_), [po], [rden2])
                ts("dve", oas_st.ap[r0:r0 + 16, h * 128:(h + 1) * 128], po.ap[r0:r0 + 16, 0:128], rden2.ap[r0:r0 + 16, 0:1], None, ALU.mult, None, [po, rden2], [oas_st])
        st("sp", OAs[NROW:NROW + 48, :], oas_st, oas_st.ap[0:48, :])
        P.barrier()

        A.release(m_ab)
        trilWT = A.alloc([H, 128], BF16)
        bsT = A.alloc([H])
        WS_f = A.alloc([H, 48])
        WS = A.alloc([H, 48], BF16)
        bsS = A.alloc([H])
        skT = A.alloc([16, NK])
        m_c = A.mark()
        wsp_f = A.alloc([H, 128])
        sk_f = A.alloc([16, 128])
        ld("sp", wsp_f, w_spatial.rearrange("g i j -> i g j"))
        for g in range(H):
            pb = PB[4 + g % 2]
            tp(pb.ap[:, 0:128], pb, wsp_f.ap[:, g, :], wsp_f, ident_f)
            tt("dve", trilWT.ap[:, g, :], pb.ap[:, 0:128], tri_f.ap, ALU.mult, [pb, tri_f], [trilWT])
        ld_nc("sp", bsT, b_spatial.rearrange("g i -> i g"))
        memset("dve", WS_f, WS_f.ap, 0.0)
        for s_ in range(2):
            for g in range(H):
                ld_nc("sp", WS_f, w_spatial[g, 0:16, 0:16].rearrange("i j -> j i"), WS_f.ap[32 * s_:32 * s_ + 16, g, 32 * s_:32 * s_ + 16])
        tt("dve", WS.ap[0:48, :, :], WS_f.ap[0:48, :, :], tri_f.ap[0:48, 0:48].unsqueeze(1).to_broadcast([48, H, 48]), ALU.mult, [WS_f, tri_f], [WS])
        memset("dve", bsS, bsS.ap, 0.0)
        for s_ in range(2):
            ld_nc("sp", bsS, b_spatial[:, 0:16].rearrange("g i -> i g"), bsS.ap[32 * s_:32 * s_ + 16, :])
        if NK < 128:
            memset("dve", sk_f, sk_f.ap, 0.0)
        ld("sp", sk_f, sub_keys.rearrange("h two k d -> k (h two) d"), sk_f.ap[0:NK, :, :])
        for g in range(16):
            pb = PB[4 + g % 2]
            tp(pb.ap[:, 0:128], pb, sk_f.ap[:, g, :], sk_f, ident_f)
            cp("dve", skT.ap[:, g, :], pb.ap[:, 0:NK], [pb], [skT])
        P.barrier()
        A.release(m_c)

        class TB:
            pass
        tbs = []
        for g in range(GSZ):
            b = TB()
            b.xf = A.alloc([D])
            b.hT = A.alloc([KC, 128], BF16)
            r1 = A.alloc([2560])
            b.q = Buf(r1.ap[:, 0:2048]); b.q.t = r1.t
            rb = r1.ap.bitcast(BF16)
            b.y = Buf(rb[:, 0:2048]); b.y.t = r1.t
            b.us = Buf(rb[:, 2048:3072]); b.us.t = r1.t
            b.obT = Buf(rb[:, 3072:4096].rearrange("p (a b) -> p a b", a=8)); b.obT.t = r1.t
            b.oaT = Buf(rb[:, 4096:5120].rearrange("p (a b) -> p a b", a=8)); b.oaT.t = r1.t
            b.stat = A.alloc([4])
            tbs.append(b)
        m_r2 = A.mark()
        Wb = [A.alloc([KC, 512], BF16) for _ in range(3)]
        Wb0lo = Buf(Wb[0].ap[:, 0:8, :]); Wb0lo.t = Wb[0].t
        Wb0hi = Buf(Wb[0].ap[:, 8:16, :]); Wb0hi.t = Wb[0].t
        junkc = A.alloc([D], BF16)
        hbc = A.alloc([D], BF16)
        xs_sb = A.alloc([512]); sq_sb = A.alloc([512]); u_sb = A.alloc([512])
        vg = A.alloc([1024]); vsn = A.alloc([1024]); vsnb = A.alloc([1024], BF16); ob_sb = A.alloc([1024], BF16)
        oa_sb = A.alloc([1024], BF16)
        sga = A.alloc([512]); sgb = A.alloc([512]); y1 = A.alloc([512])
        A.release(m_r2)
        junkd = A.alloc([D], BF16)
        xn2 = [A.alloc([D], BF16) for _ in range(2)]
        qn = A.alloc([D]); yout = A.alloc([D])
        qnT = A.alloc([16, 128])
        sc = A.alloc([16, NK])
        wk2 = [A.alloc([NK]) for _ in range(2)]
        t16 = A.alloc([16, 16])
        i16u = A.alloc([16, 16], U32)
        i16f = A.alloc([16, 16])
        cand2 = [A.alloc([256]) for _ in range(2)]
        cwk2 = [A.alloc([256]) for _ in range(2)]
        st16 = A.alloc([H, 16]); selu = A.alloc([H, 16], U32)
        selA = A.alloc([H, 16], U32); selB = A.alloc([H, 16], U32); aF = A.alloc([H, 16]); bF = A.alloc([H, 16]); ea = A.alloc([H, 16]); eb = A.alloc([H, 16])
        eidf = A.alloc([H * 16]); eidi2 = [A.alloc([H * 16], I32) for _ in range(2)]
        gat2 = [A.alloc([H, 16]) for _ in range(2)]; zs = A.alloc([H]); ss8c = A.alloc([8]); r8c = A.alloc([8]); l8c = A.alloc([8])
        acol = [A.alloc([2]) for _ in range(4)]; wcol = [A.alloc([2]) for _ in range(4)]
        dg = [A.alloc([128], BF16) for _ in range(4)]
        GBUV = [A.alloc([2 * D], BF16) for _ in range(4)]
        gj = A.alloc([D], BF16)
        gcu = [0]; gcv = [0]
        _breg = {}

        def breg(e):
            if "r" not in _breg:
                _breg["r"] = e.to_reg(NE - 1)
            return _breg["r"]

        def gelu_from(pb, nr, dst_ap, dst):
            act(xs_sb.ap[0:nr, :], pb.ap[0:nr, :], AF.Copy, [pb], [xs_sb])
            tt("dve", sq_sb.ap[0:nr, :], xs_sb.ap[0:nr, :], xs_sb.ap[0:nr, :], ALU.mult, [xs_sb], [sq_sb])
            tt("dve", sq_sb.ap[0:nr, :], sq_sb.ap[0:nr, :], xs_sb.ap[0:nr, :], ALU.mult, [sq_sb, xs_sb], [sq_sb])
            stt(u_sb.ap[0:nr, :], sq_sb.ap[0:nr, :], 0.044715, xs_sb.ap[0:nr, :], ALU.mult, ALU.add, [sq_sb, xs_sb], [u_sb])
            act(u_sb.ap[0:nr, :], u_sb.ap[0:nr, :], AF.Sigmoid, [u_sb], [u_sb], scale=2.0 * GELU_C)
            tt("dve", dst_ap, xs_sb.ap[0:nr, :], u_sb.ap[0:nr, :], ALU.mult, [xs_sb, u_sb], [dst])

        def proj(pb, nr, hT, wt, kc_n, ncols=512):
            for kc in range(kc_n):
                mm(pb.ap[0:nr, 0:ncols], pb, hT.ap[:, kc, 0:nr], hT, wt.ap[:, kc, 0:ncols], wt, kc == 0, kc == kc_n - 1)

        def group(tiles):
            ng = len(tiles)
            for g, tl in enumerate(tiles):
                b = tbs[g]
                front(tl["rows"], tl["nr"], b.xf, hbc, b.hT, junkc, b.stat, gmix, [PB[4], PB[5]], zero_pad=tl["sample"])
                ld("sp", oa_sb, OAs[tl["oa_row0"]:tl["oa_row0"] + tl["nr"], :], oa_sb.ap[0:tl["nr"], :])
                transpose16(oa_sb, tl["nr"], b.oaT, [PB[6]], nchunk=8)
            for blk in range(2):
                wt = Wb[blk]
                load_w(wt, w_in, D, O_U + blk * 512, 512)
                for g, tl in enumerate(tiles):
                    b = tbs[g]; nr = tl["nr"]; pb = PB[g % 4]
                    proj(pb, nr, b.hT, wt, KC)
                    gelu_from(pb, nr, b.us.ap[0:nr, blk * 512:(blk + 1) * 512], b.us)
            wv = [Wb[2], Wb[0]]
            for blk in range(2):
                load_w(wv[blk], w_in, D, O_VS + blk * 512, 512)
            for g, tl in enumerate(tiles):
                b = tbs[g]; nr = tl["nr"]
                for blk in range(2):
                    pb = PB[blk]
                    proj(pb, nr, b.hT, wv[blk], KC)
                    gelu_from(pb, nr, vg.ap[0:nr, blk * 512:(blk + 1) * 512], vg)
                rstd_of(vg, nr, 1024, junkc, b.stat)
                stt(vsn.ap[0:nr, :], vg.ap[0:nr, :], b.stat.ap[0:nr, 2:3], gv.ap[0:nr, :], ALU.mult, ALU.mult, [vg, b.stat, gv], [vsn])
                if tl["sample"]:
                    for s in range(2):
                        st("sp", sgv_o[s], vsn, vsn.ap[32 * s:32 * s + 16, :])
                cp("pool", vsnb.ap[0:nr, :], vsn.ap[0:nr, :], [vsn], [vsnb])
                for gg in range(H):
                    pb = PB[2 + gg // 4]
                    lhs = WS.ap[0:48, gg, :] if tl["sample"] else trilWT.ap[:, gg, :]
                    lb = WS if tl["sample"] else trilWT
                    mm(pb.ap[0:nr, (gg % 4) * 128:(gg % 4 + 1) * 128], pb, lhs, lb, vsnb.ap[0:nr, gg * 128:(gg + 1) * 128], vsnb, True, True)
                bs_ = bsS if tl["sample"] else bsT
                for gg in range(H):
                    pb = PB[2 + gg // 4]
                    stt(ob_sb.ap[0:nr, gg * 128:(gg + 1) * 128], pb.ap[0:nr, (gg % 4) * 128:(gg % 4 + 1) * 128], bs_.ap[0:nr, gg:gg + 1],
                        b.us.ap[0:nr, gg * 128:(gg + 1) * 128], ALU.add, ALU.mult, [pb, bs_, b.us], [ob_sb])
                transpose16(ob_sb, nr, b.obT, [PB[6]], nchunk=8)
            for blk in range(4):
                load_w(Wb[0], w_out_a, 1024, blk * 512, 512, Wb[0].ap[:, 0:8, :])
                load_w(Wb[0], w_out_b, 1024, blk * 512, 512, Wb[0].ap[:, 8:16, :])
                load_w(Wb[1], w_in, D, O_GA + blk * 512, 512)
                load_w(Wb[2], w_in, D, O_GB + blk * 512, 512)
                for g, tl in enumerate(tiles):
                    b = tbs[g]; nr = tl["nr"]
                    proj(PB[0], nr, b.oaT, Wb0lo, 8)
                    proj(PB[1], nr, b.obT, Wb0hi, 8)
                    proj(PB[2], nr, b.hT, Wb[1], KC)
                    proj(PB[3], nr, b.hT, Wb[2], KC)
                    act(sga.ap[0:nr, :], PB[2].ap[0:nr, :], AF.Sigmoid, [PB[2]], [sga])
                    act(sgb.ap[0:nr, :], PB[3].ap[0:nr, :], AF.Sigmoid, [PB[3]], [sgb])
                    tt("dve", y1.ap[0:nr, :], PB[0].ap[0:nr, :], sga.ap[0:nr, :], ALU.mult, [PB[0], sga], [y1])
                    tt("dve", sgb.ap[0:nr, :], PB[1].ap[0:nr, :], sgb.ap[0:nr, :], ALU.mult, [PB[1], sgb], [sgb])
                    tt("dve", b.y.ap[0:nr, blk * 512:(blk + 1) * 512], y1.ap[0:nr, :], sgb.ap[0:nr, :], ALU.add, [y1, sgb], [b.y])
            for g, tl in enumerate(tiles):
                b = tbs[g]
                transpose16(b.y, tl["nr"], b.hT, [PB[4], PB[5]])
            for blk in range(4):
                wt = Wb[blk % 3]
                load_w(wt, w_out, D, blk * 512, 512)
                for g, tl in enumerate(tiles):
                    b = tbs[g]; nr = tl["nr"]; pb = PB[g % 4]
                    proj(pb, nr, b.hT, wt, KC)
                    tt("dve", b.xf.ap[0:nr, blk * 512:(blk + 1) * 512], b.xf.ap[0:nr, blk * 512:(blk + 1) * 512], pb.ap[0:nr, :], ALU.add, [b.xf, pb], [b.xf])
            for g, tl in enumerate(tiles):
                b = tbs[g]; nr = tl["nr"]
                rstd_of(b.xf, nr, D, junkc, b.stat)
                stt(hbc.ap[0:nr, :], b.xf.ap[0:nr, :], b.stat.ap[0:nr, 2:3], gffn.ap[0:nr, :], ALU.mult, ALU.mult, [b.xf, b.stat, gffn], [hbc])
                transpose16(hbc, nr, b.hT, [PB[4], PB[5]])
            for blk in range(4):
                wt = Wb[blk % 3]
                load_w(wt, w_peer_q, D, blk * 512, 512)
                for g, tl in enumerate(tiles):
                    b = tbs[g]; nr = tl["nr"]; pb = PB[g % 4]
                    proj(pb, nr, b.hT, wt, KC)
                    act(b.q.ap[0:nr, blk * 512:(blk + 1) * 512], pb.ap[0:nr, :], AF.Copy, [pb], [b.q])
            P.barrier()
            def routing(g):
                tl = tiles[g]; b = tbs[g]; nr = tl["nr"]; eidi = eidi2[g % 2]; xn = xn2[g % 2]; gat = gat2[g % 2]
                rstd_of(b.xf, nr, D, junkd, b.stat)
                stt(xn.ap[0:nr, :], b.xf.ap[0:nr, :], b.stat.ap[0:nr, 2:3], gffn.ap[0:nr, :], ALU.mult, ALU.mult, [b.xf, b.stat, gffn], [xn])
                q3 = b.q.ap[0:nr, :].rearrange("p (h d) -> p h d", h=H)
                qn3 = qn.ap[0:nr, :].rearrange("p (h d) -> p h d", h=H)
                tt("dve", qn.ap[0:nr, :], b.q.ap[0:nr, :], b.q.ap[0:nr, :], ALU.mult, [b.q], [qn])
                dve(lambda e, o=ss8c.ap[0:nr, :], i_=qn3: e.tensor_reduce(out=o, in_=i_, axis=AX.X, op=ALU.add), [qn], [ss8c])
                act(l8c.ap[0:nr, :], ss8c.ap[0:nr, :], AF.Ln, [ss8c, eps_b], [l8c], scale=1.0 / 256, bias=eps_b.ap[0:nr, 0:1])
                act(r8c.ap[0:nr, :], l8c.ap[0:nr, :], AF.Exp, [l8c], [r8c], scale=-0.5)
                tt("dve", qn3, q3, r8c.ap[0:nr, :].unsqueeze(2).to_broadcast([nr, H, 256]), ALU.mult, [b.q, r8c], [qn])
                tt("dve", qn3, qn3, gpq.ap[0:nr, :].unsqueeze(1).to_broadcast([nr, H, 256]), ALU.mult, [qn, gpq], [qn])
                yield
                for half in range(4):
                    pb = PB[4 + half % 2]
                    for c in range(4):
                        gg = half * 4 + c
                        tp(pb.ap[:, c * 128:c * 128 + nr], pb, qn.ap[0:nr, gg * 128:(gg + 1) * 128], qn, ident_f, nr)
                    dst = qnT.ap[:, half * 4:half * 4 + 4, 0:nr]
                    src = pb.ap.rearrange("p (a b) -> p a b", a=4)[:, :, 0:nr]
                    if half % 2 == 0:
                        cp("dve", dst, src, [pb], [qnT])
                    else:
                        act(dst, src, AF.Copy, [pb], [qnT])
                    yield
                gpb = 512 // NK
                for gg in range(16):
                    pb = PB[6 + (gg // gpb) % 2]
                    col = (gg % gpb) * NK
                    mm(pb.ap[0:nr, col:col + NK], pb, qnT.ap[:, gg, 0:nr], qnT, skT.ap[:, gg, :], skT, True, True)
                    if gg % gpb == gpb - 1 or gg == 15:
                        g0 = (gg // gpb) * gpb
                        ng_ = gg - g0 + 1
                        cp("dve", sc.ap[0:nr, g0:g0 + ng_, :].rearrange("p a b -> p (a b)"), pb.ap[0:nr, 0:ng_ * NK], [pb], [sc])
                        yield
                for gg in range(16):
                    wk = wk2[gg % 2]
                    dve(lambda e, o=t16.ap[0:nr, gg, 0:8], i_=sc.ap[0:nr, gg, :]: e.max(out=o, in_=i_), [sc], [t16])
                    dve(lambda e, o=i16u.ap[0:nr, gg, 0:8], m=t16.ap[0:nr, gg, 0:8], v=sc.ap[0:nr, gg, :]: e.max_index(out=o, in_max=m, in_values=v), [sc, t16], [i16u])
                    dve(lambda e, o=wk.ap[0:nr, :], m=t16.ap[0:nr, gg, 0:8], v=sc.ap[0:nr, gg, :]: e.match_replace(out=o, in_to_replace=m, in_values=v, imm_value=NEG), [sc, t16], [wk])
                    dve(lambda e, o=t16.ap[0:nr, gg, 8:16], i_=wk.ap[0:nr, :]: e.max(out=o, in_=i_), [wk], [t16])
                    dve(lambda e, o=i16u.ap[0:nr, gg, 8:16], m=t16.ap[0:nr, gg, 8:16], v=wk.ap[0:nr, :]: e.max_index(out=o, in_max=m, in_values=v), [wk, t16], [i16u])
                    yield
                cp("dve", i16f.ap[0:nr, :, :], i16u.ap[0:nr, :, :], [i16u], [i16f])
                t4 = t16.ap[0:nr, :, :].rearrange("p (h two) k -> p h two k", two=2)
                i4 = i16f.ap[0:nr, :, :].rearrange("p (h two) k -> p h two k", two=2)
                for hh in range(H):
                    cand = cand2[hh % 2]; cwk = cwk2[hh % 2]
                    c3 = cand.ap[0:nr, :].rearrange("p (a b) -> p a b", a=16)
                    tt("dve", c3, t4[:, hh, 0, :].unsqueeze(2).to_broadcast([nr, 16, 16]), t4[:, hh, 1, :].unsqueeze(1).to_broadcast([nr, 16, 16]), ALU.add, [t16], [cand])
                    dve(lambda e, o=st16.ap[0:nr, hh, 0:8], i_=cand.ap[0:nr, :]: e.max(out=o, in_=i_), [cand], [st16])
                    dve(lambda e, o=selu.ap[0:nr, hh, 0:8], m=st16.ap[0:nr, hh, 0:8], v=cand.ap[0:nr, :]: e.max_index(out=o, in_max=m, in_values=v), [cand, st16], [selu])
                    dve(lambda e, o=cwk.ap[0:nr, :], m=st16.ap[0:nr, hh, 0:8], v=cand.ap[0:nr, :]: e.match_replace(out=o, in_to_replace=m, in_values=v, imm_value=NEG), [cand, st16], [cwk])
                    dve(lambda e, o=st16.ap[0:nr, hh, 8:16], i_=cwk.ap[0:nr, :]: e.max(out=o, in_=i_), [cwk], [st16])
                    dve(lambda e, o=selu.ap[0:nr, hh, 8:16], m=st16.ap[0:nr, hh, 8:16], v=cwk.ap[0:nr, :]: e.max_index(out=o, in_max=m, in_values=v), [cwk, st16], [selu])
                    yield
                ts("dve", selA.ap[0:nr, :, :], selu.ap[0:nr, :, :], 4, None, ALU.logical_shift_right, None, [selu], [selA])
                ts("dve", selB.ap[0:nr, :, :], selu.ap[0:nr, :, :], 15, None, ALU.bitwise_and, None, [selu], [selB])
                cp("dve", aF.ap[0:nr, :, :], selA.ap[0:nr, :, :], [selA], [aF])
                cp("dve", bF.ap[0:nr, :, :], selB.ap[0:nr, :, :], [selB], [bF])
                yield
                oh4 = qn.ap[0:nr, :].rearrange("p (h r c) -> p h r c", h=H, r=16)
                io4 = iota.ap[0:nr, 0:16].unsqueeze(1).unsqueeze(1).to_broadcast([nr, H, 16, 16])
                for which, (xF, eX) in enumerate(((aF, ea), (bF, eb))):
                    tt("dve", oh4, xF.ap[0:nr, :, :].unsqueeze(3).to_broadcast([nr, H, 16, 16]), io4, ALU.is_equal, [xF, iota], [qn])
                    yield
                    tt("dve", oh4, oh4, i4[:, :, which, :].unsqueeze(2).to_broadcast([nr, H, 16, 16]), ALU.mult, [qn, i16f], [qn])
                    yield
                    dve(lambda e, o=eX.ap[0:nr, :, :], i_=oh4: e.tensor_reduce(out=o, in_=i_, axis=AX.X, op=ALU.add), [qn], [eX])
                    yield
                stt(eidf.ap[0:nr, :], ea.ap[0:nr, :, :].rearrange("p h k -> p (h k)"), float(NK), eb.ap[0:nr, :, :].rearrange("p h k -> p (h k)"), ALU.mult, ALU.add, [ea, eb], [eidf])
                cp("dve", eidi.ap[0:nr, :], eidf.ap[0:nr, :], [eidf], [eidi])
                tt("dve", gat.ap[0:nr, :, :], st16.ap[0:nr, :, :], st16.ap[0:nr, :, 0:1].to_broadcast([nr, H, 16]), ALU.subtract, [st16], [gat])
                act(gat.ap[0:nr, :, :], gat.ap[0:nr, :, :], AF.Exp, [gat], [gat])
                dve(lambda e, o=zs.ap[0:nr, :], i_=gat.ap[0:nr, :, :]: e.tensor_reduce(out=o, in_=i_, axis=AX.X, op=ALU.add), [gat], [zs])
                dve(lambda e, o=zs.ap[0:nr, :], i_=zs.ap[0:nr, :]: e.reciprocal(out=o, in_=i_), [zs], [zs])
                tt("dve", gat.ap[0:nr, :, :], gat.ap[0:nr, :, :], zs.ap[0:nr, :].unsqueeze(2).to_broadcast([nr, H, 16]), ALU.mult, [gat, zs], [gat])
            def slots(g, s_from, s_to):
                tl = tiles[g]; b = tbs[g]; nr = tl["nr"]; eidi = eidi2[g % 2]; xn = xn2[g % 2]; gat = gat2[g % 2]
                gflat = gat.ap[0:nr, :, :].rearrange("p h k -> p (h k)")
                for sl in range(s_from, s_to):
                    k = gcu[0] % 4; gcu[0] += 1
                    gb = GBUV[k]; ac = acol[k]; wc = wcol[k]; d_ = dg[k]
                    P.dma("pool", lambda e, o=gb.ap[0:nr, :], ix=eidi.ap[0:nr, sl:sl + 1]: e.indirect_dma_start(
                        out=o, out_offset=None, in_=UVb, in_offset=bass.IndirectOffsetOnAxis(ap=ix, axis=0), bounds_check=breg(e), oob_is_err=False),
                        reads=[eidi.t], writes=[gb.t], extra=conv_tok[-1:])
                    stt(gj.ap[0:nr, :], gb.ap[0:nr, 0:D], 1.0, xn.ap[0:nr, :], ALU.mult, ALU.mult, [gb, xn], [gj, ac], accum=ac.ap[0:nr, 0:1])
                    act(wc.ap[0:nr, 0:1], ac.ap[0:nr, 0:1], AF.Square, [ac], [wc])
                    act(wc.ap[0:nr, 0:1], wc.ap[0:nr, 0:1], AF.Identity, [wc, one_b], [wc], scale=0.044715, bias=one_b.ap[0:nr, 0:1])
                    act(wc.ap[0:nr, 0:1], wc.ap[0:nr, 0:1], AF.Copy, [wc, ac], [wc], scale=ac.ap[0:nr, 0:1])
                    act(wc.ap[0:nr, 0:1], wc.ap[0:nr, 0:1], AF.Sigmoid, [wc], [wc], scale=2.0 * GELU_C)
                    act(wc.ap[0:nr, 0:1], wc.ap[0:nr, 0:1], AF.Copy, [wc, ac], [wc], scale=ac.ap[0:nr, 0:1])
                    act(wc.ap[0:nr, 1:2], wc.ap[0:nr, 0:1], AF.Copy, [wc, gat], [wc], scale=gflat[:, sl:sl + 1])
                    act(d_.ap[0:nr, 0:nr], ident_b.ap[0:nr, 0:nr], AF.Copy, [ident_b, wc], [d_], scale=wc.ap[0:nr, 1:2])
                    for blk in range(4):
                        mm(PB[blk].ap[0:nr, :], PB[blk], d_.ap[0:nr, 0:nr], d_, gb.ap[0:nr, D + blk * 512:D + (blk + 1) * 512], gb, sl == 0, sl == H * 16 - 1)
                if s_to == H * 16:
                    for blk in range(4):
                        tt("dve", yout.ap[0:nr, blk * 512:(blk + 1) * 512], b.xf.ap[0:nr, blk * 512:(blk + 1) * 512], PB[blk].ap[0:nr, :], ALU.add, [b.xf, PB[blk]], [yout])
                    for (dst, p0, n) in tl["outs"]:
                        st("sp", dst, yout, yout.ap[p0:p0 + n, :])

            for _ in routing(0):
                pass
            for g in range(ng):
                rgen = routing(g + 1) if g + 1 < ng else None
                for sl in range(H * 16):
                    slots(g, sl, sl + 1)
                    if rgen is not None and sl % 3 == 2:
                        try:
                            next(rgen)
                        except StopIteration:
                            rgen = None
                if rgen is not None:
                    for _ in rgen:
                        pass
            P.barrier()

        alltiles = []
        for i in range(NOWN):
            alltiles.append(dict(rows=[(xo[i * 128:(i + 1) * 128, :], 0, 128)], nr=128, oa_row0=i * 128, sample=False,
                                 outs=[(y_o[i * 128:(i + 1) * 128, :], 0, 128)]))
        for g0 in range(0, NOWN, GSZ):
            group(alltiles[g0:g0 + GSZ])
        group([dict(rows=[(xs[0], 0, 16), (xs[1], 32, 16)], nr=48, oa_row0=NROW, sample=True,
                    outs=[(ys_o[0], 0, 16), (ys_o[1], 32, 16)])])
        P.emit()
    return nc


_NC_CACHE = {}


def _consts(cfg, j):
    ident = np.eye(128, dtype=np.float32)
    k = np.arange(128)
    tri = (k[:, None] <= k[None, :]).astype(np.float32)
    e127 = np.zeros((128, 128), np.float32); e127[127, :] = 1.0
    triblk = tri * ((k[:, None] // 32) == (k[None, :] // 32)).astype(np.float32)
    iota = np.tile(np.arange(256, dtype=np.float32), (128, 1))
    own = own_tiles(cfg, j)
    sel = np.zeros((cfg.NOWN, cfg.NT), np.float32)
    mk = np.zeros((128, cfg.NOWN * 4, 128), np.float32)
    pen = np.zeros((1, cfg.NOWN * 4), np.float32)
    for i, Tt in enumerate(own):
        sel[i, Tt] = 1.0
        lo = slot_lo(i)
        for r in range(4):
            t = lo + r
            if t < Tt:
                mk[:, i * 4 + r, :] = 1.0
            elif t == Tt:
                mk[:, i * 4 + r, :] = tri
            else:
                pen[0, i * 4 + r] = NEG
    return dict(c_ident=ident, c_tri=tri, c_e127=e127, c_triblk=triblk, c_iota=iota,
                c_sel=sel.reshape(1, -1), c_mk=np.ascontiguousarray(mk.reshape(128, -1)), c_pen=pen)


def kernel(x_prompt, x_sample, cache_k, cache_v, cache_logf, norm_mix_g, w_in, b_forget,
           q_norm_g, k_norm_g, v_norm_g, w_spatial, b_spatial, w_out_a, w_out_b, w_out,
           norm_ffn_g, w_peer_q, peer_q_norm_g, peer_sub_keys, peer_u, peer_v):
    f = lambda a: np.ascontiguousarray(np.asarray(a, dtype=np.float32))
    x_prompt = f(x_prompt); x_sample = f(x_sample)
    B, SEQ, _ = x_prompt.shape
    DB, DS, _ = x_sample.shape
    PAST = cache_k.shape[2]
    NK = peer_sub_keys.shape[3]
    assert B == 2 and DB == 16 and DS == 16
    cfg = Cfg(SEQ, PAST, NK)
    key = (SEQ, PAST, NK)
    if key not in _NC_CACHE:
        _NC_CACHE[key] = build(cfg)
    nc = _NC_CACHE[key]
    ck = f(cache_k)[0].reshape(16, PAST, 1024)
    cv = f(cache_v)[0].reshape(16, PAST, 1024)
    clf = f(cache_logf)[0]
    shared = dict(
        norm_mix_g=f(norm_mix_g).reshape(1, D), w_in=f(w_in)[0], b_forget=f(b_forget).reshape(1, 8),
        q_norm_g=f(q_norm_g).reshape(1, 128), k_norm_g=f(k_norm_g).reshape(1, 128), v_norm_g=f(v_norm_g).reshape(1, 1024),
        w_spatial=f(w_spatial)[0], b_spatial=f(b_spatial)[0], w_out_a=f(w_out_a)[0], w_out_b=f(w_out_b)[0],
        w_out=f(w_out)[0], norm_ffn_g=f(norm_ffn_g).reshape(1, D), w_peer_q=f(w_peer_q)[0],
        peer_q_norm_g=f(peer_q_norm_g).reshape(1, 256), peer_sub_keys=f(peer_sub_keys)[0],
        peer_u=f(peer_u)[0], peer_v=f(peer_v)[0])
    in_maps = []
    owns = []
    for c in range(8):
        b, j = divmod(c, 4)
        own = own_tiles(cfg, j)
        owns.append(own)
        xo = np.concatenate([x_prompt[b, t * 128:(t + 1) * 128] for t in own], axis=0)
        m = dict(shared)
        m.update(xb=x_prompt[b], xo=xo, xs=x_sample[2 * c:2 * c + 2], ck=ck[2 * c:2 * c + 2], cv=cv[2 * c:2 * c + 2],
                 clf=clf[2 * c:2 * c + 2])
        m.update(_consts(cfg, j))
        in_maps.append(m)
    res = run_bass_kernel_spmd(nc, in_maps, core_ids=list(range(8))).results
    y_p = np.zeros((2, SEQ, D), np.float32)
    k_p = np.zeros((1, 2, SEQ, 8, 128), np.float32)
    v_p = np.zeros((1, 2, SEQ, 8, 128), np.float32)
    lf_p = np.zeros((1, 2, SEQ, 8), np.float32)
    y_s = np.zeros((16, 16, D), np.float32)
    k_s = np.zeros((1, 16, 16, 8, 128), np.float32)
    v_s = np.zeros((1, 16, 16, 8, 128), np.float32)
    lf_s = np.zeros((1, 16, 16, 8), np.float32)
    sg_s = np.zeros((1, 16, 16, 1024), np.float32)
    for c in range(8):
        b, j = divmod(c, 4)
        r = res[c]
        for i, t in enumerate(owns[c]):
            sl = slice(t * 128, (t + 1) * 128)
            y_p[b, sl] = r["y"][i * 128:(i + 1) * 128]
            k_p[0, b, sl] = r["ko"][i * 128:(i + 1) * 128].reshape(128, 8, 128)
            v_p[0, b, sl] = r["vo"][i * 128:(i + 1) * 128].reshape(128, 8, 128)
            lf_p[0, b, sl] = r["lfo"][i * 128:(i + 1) * 128]
        y_s[2 * c:2 * c + 2] = r["ys"]
        k_s[0, 2 * c:2 * c + 2] = r["kso"].reshape(2, 16, 8, 128)
        v_s[0, 2 * c:2 * c + 2] = r["vso"].reshape(2, 16, 8, 128)
        lf_s[0, 2 * c:2 * c + 2] = r["lfs"]
        sg_s[0, 2 * c:2 * c + 2] = r["sgv"]
    return (y_p, y_s, k_p, v_p, lf_p, k_s, v_s, lf_s, sg_s)
```

```python
import numpy as np
from contextlib import ExitStack
import concourse.bass as bass
import concourse.mybir as mybir
from concourse.bass_utils import run_bass_kernel_spmd

F32 = mybir.dt.float32
BF16 = mybir.dt.bfloat16
I32 = mybir.dt.int32
U32 = mybir.dt.uint32
ALU = mybir.AluOpType
AF = mybir.ActivationFunctionType
AX = mybir.AxisListType

D = 2048
KC = 16
H = 8
HD = 128
O_Q, O_K, O_V, O_F, O_U, O_VS, O_GA, O_GB = 0, 1024, 2048, 3072, 3080, 4104, 5128, 7176
INW = 9224
RMS_EPS = 1e-6
SCALE = HD ** -0.5
GELU_C = 0.7978845608028654
TOPK = 16
NEG = -1e30
ENGS = ("pe", "act", "dve", "pool", "sp")
SAME_ENG_SYNC = True
GSZ = 3
NGB = 6


class T:
    __slots__ = ("lw", "rd", "slot")

    def __init__(self):
        self.lw = None
        self.rd = []
        self.slot = None


class Prog:
    def __init__(self, nc, es):
        self.nc = nc
        self.es = es
        self.ops = {e: [] for e in ENGS}
        self.cnt = {e: 0 for e in ENGS}
        self.esem = {e: es.enter_context(nc.semaphore("sem_" + e)) for e in ENGS}
        self.slots = []
        self.nobar = set()

    def slot_of(self, t):
        if t.slot is None:
            sem = self.es.enter_context(self.nc.semaphore("ds%d" % len(self.slots)))
            t.slot = [sem, 0]
            self.slots.append(t.slot)
        return t.slot

    def _deps(self, reads, writes, extra):
        toks = list(extra)
        for t in reads:
            if t.lw is not None:
                toks.append(t.lw)
        for t in writes:
            if t.lw is not None:
                toks.append(t.lw)
            toks.extend(t.rd)
        return toks

    def _upd(self, reads, writes, tok):
        for t in reads:
            t.rd.append(tok)
        for t in writes:
            t.lw = tok
            t.rd = []

    def op(self, eng, fn, reads=(), writes=(), extra=()):
        toks = self._deps(reads, writes, extra)
        self.cnt[eng] += 1
        tok = ("e", eng, self.cnt[eng])
        self.ops[eng].append((toks, fn, self.esem[eng], 1))
        self._upd(reads, writes, tok)
        return tok

    def dma(self, q, fn, reads=(), writes=(), slot_t=None, extra=()):
        toks = self._deps(reads, writes, extra)
        st = slot_t if slot_t is not None else (writes[0] if writes else reads[0])
        slot = self.slot_of(st)
        slot[1] += 16
        tok = ("s", slot[0], slot[1])
        self.ops[q].append((toks, fn, slot[0], 16))
        self._upd(reads, writes, tok)
        return tok

    def barrier(self, final=False):
        toks = [("e", e, self.cnt[e]) for e in ENGS if self.cnt[e] > 0]
        toks += [("s", s[0], s[1]) for s in self.slots if s[1] > 0 and (final or id(s) not in self.nobar)]
        for e in ENGS:
            self.ops[e].append((list(toks), None, None, 0))

    def emit(self):
        nc = self.nc
        self.barrier(final=True)
        with nc.Block() as blk:
            def replay(engname, e):
                waited = {}
                for toks, fn, sem, inc in self.ops[engname]:
                    for tk in toks:
                        if tk[0] == "e":
                            if tk[1] == engname and (engname == "pe" or not SAME_ENG_SYNC):
                                continue
                            s, v = self.esem[tk[1]], tk[2]
                        else:
                            s, v = tk[1], tk[2]
                        key = id(s)
                        if waited.get(key, 0) >= v:
                            continue
                        waited[key] = v
                        e.wait_ge(s, v)
                    if fn is not None:
                        fn(e).then_inc(sem, inc)

            @blk.tensor
            def _(e):
                replay("pe", e)

            @blk.scalar
            def _(e):
                replay("act", e)

            @blk.vector
            def _(e):
                replay("dve", e)

            @blk.gpsimd
            def _(e):
                replay("pool", e)

            @blk.sync
            def _(e):
                replay("sp", e)


class Buf:
    __slots__ = ("ap", "t")

    def __init__(self, ap):
        self.ap = ap
        self.t = T()


class Arena:
    def __init__(self, ap_f32, nfl):
        self.base = ap_f32
        self.n = nfl
        self.off = 0

    def mark(self):
        return self.off

    def release(self, m):
        self.off = m

    def alloc(self, shape, dt=F32):
        ne = int(np.prod(shape))
        esz = {F32: 4, BF16: 2, I32: 4, U32: 4}[dt]
        nfl = (ne * esz + 3) // 4
        nfl = (nfl + 1) // 2 * 2
        assert self.off + nfl <= self.n, "arena overflow: need %d have %d" % (self.off + nfl, self.n)
        v = self.base[:, self.off:self.off + nfl]
        self.off += nfl
        if dt != F32:
            v = v.bitcast(dt)
        v = v[:, 0:ne]
        if len(shape) == 2:
            v = v.rearrange("p (a b) -> p a b", a=shape[0])
        elif len(shape) == 3:
            v = v.rearrange("p (a b c) -> p a b c", a=shape[0], b=shape[1])
        return Buf(v)


class Cfg:
    def __init__(self, seq, past, nk):
        self.SEQ = seq
        self.PAST = past
        self.NK = nk
        self.NE = nk * nk
        self.NT = seq // 128
        assert self.NT % 8 == 0
        self.NOWN = self.NT // 4
        self.NPT = past // 128
        assert past % 128 == 0


def own_tiles(cfg, j):
    out = []
    for m in range(cfg.NT // 8):
        out += [8 * m + j, 8 * m + 7 - j]
    return out


def slot_lo(i):
    m, e = divmod(i, 2)
    return 8 * m + (0 if e == 0 else 4)


def build(cfg):
    nc = bass.Bass("TRN2", target_bir_lowering=False)
    SEQ, PAST, NK, NE, NT, NOWN, NPT = cfg.SEQ, cfg.PAST, cfg.NK, cfg.NE, cfg.NT, cfg.NOWN, cfg.NPT
    NROW = NOWN * 128
    es = ExitStack()
    with es:
        def din(n, s, d=F32):
            return nc.dram_tensor(n, list(s), d, kind="ExternalInput").ap()

        def dout(n, s, d=F32):
            return nc.dram_tensor(n, list(s), d, kind="ExternalOutput").ap()

        xb = din("xb", [SEQ, D])
        xo = din("xo", [NROW, D])
        xs = din("xs", [2, 16, D])
        ck = din("ck", [2, PAST, 1024])
        cv = din("cv", [2, PAST, 1024])
        clf = din("clf", [2, PAST, 8])
        norm_mix_g = din("norm_mix_g", [1, D])
        w_in = din("w_in", [D, INW])
        b_forget = din("b_forget", [1, 8])
        q_norm_g = din("q_norm_g", [1, 128])
        k_norm_g = din("k_norm_g", [1, 128])
        v_norm_g = din("v_norm_g", [1, 1024])
        w_spatial = din("w_spatial", [8, 128, 128])
        b_spatial = din("b_spatial", [8, 128])
        w_out_a = din("w_out_a", [1024, D])
        w_out_b = din("w_out_b", [1024, D])
        w_out = din("w_out", [D, D])
        norm_ffn_g = din("norm_ffn_g", [1, D])
        w_peer_q = din("w_peer_q", [D, D])
        peer_q_norm_g = din("peer_q_norm_g", [1, 256])
        sub_keys = din("peer_sub_keys", [8, 2, NK, 128])
        peer_u = din("peer_u", [NE, D])
        peer_v = din("peer_v", [NE, D])
        c_ident = din("c_ident", [128, 128])
        c_tri = din("c_tri", [128, 128])
        c_e127 = din("c_e127", [128, 128])
        c_triblk = din("c_triblk", [128, 128])
        c_iota = din("c_iota", [128, 256])
        c_sel = din("c_sel", [1, NOWN * NT])
        c_mk = din("c_mk", [128, NOWN * 4 * 128])
        c_pen = din("c_pen", [1, NOWN * 4])

        y_o = dout("y", [NROW, D])
        ys_o = dout("ys", [2, 16, D])
        ko_o = dout("ko", [NROW, 1024])
        vo_o = dout("vo", [NROW, 1024])
        lfo_o = dout("lfo", [NROW, 8])
        kso_o = dout("kso", [2, 16, 1024])
        vso_o = dout("vso", [2, 16, 1024])
        lfs_o = dout("lfs", [2, 16, 8])
        sgv_o = dout("sgv", [2, 16, 1024])

        KTs = nc.dram_tensor("KTs", [H, 128, SEQ], BF16, kind="Internal").ap()
        Vs = nc.dram_tensor("Vs", [H, SEQ, 128], BF16, kind="Internal").ap()
        OAs = nc.dram_tensor("OAs", [NROW + 128, 1024], BF16, kind="Internal").ap()
        UVb = nc.dram_tensor("UVb", [NE, 2 * D], BF16, kind="Internal").ap()
        Ub = UVb[:, 0:D]
        Vb = UVb[:, D:2 * D]
        t_conv = T()

        NFL = 52500
        arena_t = es.enter_context(nc.sbuf_tensor("arena", [128, NFL], F32))
        A = Arena(arena_t[:, :], NFL)
        pacc = es.enter_context(nc.psum_tensor("pacc", [128, 2048], F32))
        pw = es.enter_context(nc.psum_tensor("pw", [128, 2048], F32))
        PB = [Buf(pacc[:, b * 512:(b + 1) * 512]) for b in range(4)] + [Buf(pw[:, b * 512:(b + 1) * 512]) for b in range(4)]
        P = Prog(nc, es)

        def ld(q, dst, src_ap, dst_ap=None, extra=()):
            d_ap = dst.ap if dst_ap is None else dst_ap
            return P.dma(q, lambda e, d_ap=d_ap, s=src_ap: e.dma_start(out=d_ap, in_=s), writes=[dst.t], extra=extra)

        def ld_nc(q, dst, src_ap, dst_ap=None):
            d_ap = dst.ap if dst_ap is None else dst_ap
            return P.dma(q, lambda e, d_ap=d_ap, s=src_ap: e.dma_start(out=d_ap, in_=s, allow_slow_non_contiguous=True), writes=[dst.t])

        def st(q, dst_ap, src, src_ap=None, dst_t=None, extra=()):
            s_ap = src.ap if src_ap is None else src_ap
            w = [dst_t] if dst_t is not None else []
            return P.dma(q, lambda e, d=dst_ap, s=s_ap: e.dma_start(out=d, in_=s), reads=[src.t], writes=w, slot_t=src.t, extra=extra)

        def mm(out_ap, out_b, lhsT_ap, lhsT_b, rhs_ap, rhs_b, start, stop):
            return P.op("pe", lambda e, o=out_ap, l=lhsT_ap, r=rhs_ap, s=start, p=stop: e.matmul(out=o, lhsT=l, rhs=r, start=s, stop=p),
                        reads=[lhsT_b.t, rhs_b.t], writes=[out_b.t])

        def tp(out_ap, out_b, in_ap, in_b, ident, n=128):
            return P.op("pe", lambda e, o=out_ap, i=in_ap, idn=ident.ap[0:n, 0:n]: e.transpose(out=o, in_=i, identity=idn),
                        reads=[in_b.t, ident.t], writes=[out_b.t])

        def act(out_ap, in_ap, func, reads, writes, **kw):
            return P.op("act", lambda e, o=out_ap, i=in_ap, f=func, kw=kw: e.activation(out=o, in_=i, func=f, **kw),
                        reads=[b.t for b in reads], writes=[b.t for b in writes])

        def dve(fn, reads, writes):
            return P.op("dve", fn, reads=[b.t for b in reads], writes=[b.t for b in writes])

        def pool(fn, reads, writes):
            return P.op("pool", fn, reads=[b.t for b in reads], writes=[b.t for b in writes])

        def tt(eng, out_ap, in0, in1, op, reads, writes):
            return P.op(eng, lambda e, o=out_ap, a=in0, b=in1, op=op: e.tensor_tensor(out=o, in0=a, in1=b, op=op),
                        reads=[b.t for b in reads], writes=[b.t for b in writes])

        def ts(eng, out_ap, in0, s1, s2, op0, op1, reads, writes):
            if op1 is None:
                return P.op(eng, lambda e, o=out_ap, a=in0, s1=s1, op0=op0: e.tensor_scalar(out=o, in0=a, scalar1=s1, scalar2=None, op0=op0),
                            reads=[b.t for b in reads], writes=[b.t for b in writes])
            return P.op(eng, lambda e, o=out_ap, a=in0, s1=s1, s2=s2, op0=op0, op1=op1: e.tensor_scalar(out=o, in0=a, scalar1=s1, scalar2=s2, op0=op0, op1=op1),
                        reads=[b.t for b in reads], writes=[b.t for b in writes])

        def stt(out_ap, in0, scalar, in1, op0, op1, reads, writes, accum=None):
            if accum is None:
                return dve(lambda e, o=out_ap, a=in0, s=scalar, b=in1, op0=op0, op1=op1: e.scalar_tensor_tensor(out=o, in0=a, scalar=s, in1=b, op0=op0, op1=op1), reads, writes)
            return dve(lambda e, o=out_ap, a=in0, s=scalar, b=in1, op0=op0, op1=op1, ac=accum: e.scalar_tensor_tensor(out=o, in0=a, scalar=s, in1=b, op0=op0, op1=op1, accum_out=ac), reads, writes)

        def cp(eng, out_ap, in_ap, reads, writes):
            return P.op(eng, lambda e, o=out_ap, i=in_ap: e.tensor_copy(out=o, in_=i), reads=[b.t for b in reads], writes=[b.t for b in writes])

        def memset(eng, buf, ap, val):
            return P.op(eng, lambda e, a=ap, v=val: e.memset(a, v), writes=[buf.t])

        ident_f = A.alloc([128]); ld("sp", ident_f, c_ident)
        ident_b = A.alloc([128], BF16); ld("pool", ident_b, c_ident)
        tri_f = A.alloc([128]); ld("sp", tri_f, c_tri)
        tri_b = A.alloc([128], BF16); ld("pool", tri_b, c_tri)
        e127 = A.alloc([128]); ld("sp", e127, c_e127)
        triblk = A.alloc([128]); ld("sp", triblk, c_triblk)
        iota = A.alloc([256]); ld("sp", iota, c_iota)
        gmix = A.alloc([D]); ld("sp", gmix, norm_mix_g.partition_broadcast(128))
        gffn = A.alloc([D]); ld("sp", gffn, norm_ffn_g.partition_broadcast(128))
        gq = A.alloc([128]); ld("sp", gq, q_norm_g.partition_broadcast(128))
        gk = A.alloc([128]); ld("sp", gk, k_norm_g.partition_broadcast(128))
        gv = A.alloc([1024]); ld("sp", gv, v_norm_g.partition_broadcast(128))
        bfg = A.alloc([8]); ld("sp", bfg, b_forget.partition_broadcast(128))
        gpq = A.alloc([256]); ld("sp", gpq, peer_q_norm_g.partition_broadcast(128))
        zero1 = A.alloc([2]); memset("dve", zero1, zero1.ap, 0.0)
        m_persist = A.mark()

        def front(rows, nr, xf, hb, hT, junk, stat, gain, pbs, zero_pad=False):
            if zero_pad:
                memset("pool", xf, xf.ap[0:nr, :], 0.0)
            for (src, p0, n) in rows:
                ld("pool", xf, src, xf.ap[p0:p0 + n, :])
            rstd_of(xf, nr, D, junk, stat)
            stt(hb.ap[0:nr, :], xf.ap[0:nr, :], stat.ap[0:nr, 2:3], gain.ap[0:nr, :], ALU.mult, ALU.mult, [xf, stat, gain], [hb])
            transpose16(hb, nr, hT, pbs)

        def rstd_of(xf, nr, n, junk, stat):
            act(junk.ap[0:nr, 0:n], xf.ap[0:nr, 0:n], AF.Square, [xf], [junk, stat], accum_out=stat.ap[0:nr, 0:1])
            act(stat.ap[0:nr, 1:2], stat.ap[0:nr, 0:1], AF.Ln, [stat, eps_b], [stat], scale=1.0 / n, bias=eps_b.ap[0:nr, 0:1])
            act(stat.ap[0:nr, 2:3], stat.ap[0:nr, 1:2], AF.Exp, [stat], [stat], scale=-0.5)

        def transpose16(hb, nr, hT, pbs, nchunk=KC):
            for half in range((nchunk + 7) // 8):
                pb = pbs[half % len(pbs)]
                pv = pb.ap.bitcast(BF16)
                n8 = min(8, nchunk - half * 8)
                for c in range(n8):
                    kc = half * 8 + c
                    tp(pv[:, c * 128:c * 128 + nr], pb, hb.ap[0:nr, kc * 128:(kc + 1) * 128], hb, ident_b, nr)
                src = pv[:, 0:n8 * 128].rearrange("p (a b) -> p a b", a=n8)[:, :, 0:nr]
                dst = hT.ap[:, half * 8:half * 8 + n8, 0:nr]
                if half % 2 == 0:
                    cp("dve", dst, src, [pb], [hT])
                else:
                    act(dst, src, AF.Copy, [pb], [hT])

        eps_b = A.alloc([2]); memset("dve", eps_b, eps_b.ap, RMS_EPS)
        one_b = A.alloc([2]); memset("dve", one_b, one_b.ap, 1.0)
        m_persist = A.mark()

        def load_w(wt, w_dram, k_rows, c0, ncols, wap=None):
            kc = k_rows // 128
            src = w_dram[:, c0:c0 + ncols].rearrange("(kc p) n -> p kc n", p=128)
            dst = wt.ap[:, 0:kc, 0:ncols] if wap is None else wap
            return ld("pool", wt, src, dst)

        m_ab = A.mark()
        NFK = A.alloc([NT, 8])
        CE = A.alloc([NT, 8])
        mkb = A.alloc([NOWN * 4, 128], BF16); ld("pool", mkb, c_mk, mkb.ap.rearrange("p a b -> p (a b)"))
        CEsel = A.alloc([NOWN, 8])
        QT = A.alloc([H, NROW], BF16)
        QTs = A.alloc([H, 48], BF16)
        kTs_new = A.alloc([H, 48], BF16)
        vb_new = A.alloc([H, 132], BF16)
        nl_new = A.alloc([8])
        m_wa = A.mark()
        WA = A.alloc([KC, 2056], BF16)
        load_w(WA, w_in, D, O_K, 2056)

        xf1 = A.alloc([D])
        xf2 = [xf1, xf1]
        hb = A.alloc([D], BF16)
        hT2 = [A.alloc([KC, 128], BF16) for _ in range(2)]
        stat = A.alloc([4])
        k_sb = A.alloc([1024])
        tmpf = A.alloc([1024])
        junk = Buf(tmpf.ap.bitcast(BF16)); junk.t = tmpf.t
        kn = A.alloc([1024])
        v_sb = A.alloc([1024])
        kT_st = [A.alloc([H, 128], BF16) for _ in range(2)]
        vb_st = [A.alloc([1024], BF16) for _ in range(2)]
        ss8 = A.alloc([8]); r8 = A.alloc([8]); l8 = A.alloc([8])
        f_sb = A.alloc([8]); lf_sb = A.alloc([8]); fc_sb = A.alloc([8]); ce_prev = A.alloc([8])
        memset("dve", ce_prev, ce_prev.ap, 0.0)
        selb = A.alloc([NOWN, NT]); ld("sp", selb, c_sel.partition_broadcast(128), selb.ap.rearrange("p a b -> p (a b)"))

        def qknorm(src, nr, gain, dst, scr):
            s3 = src.ap[0:nr, :].rearrange("p (h d) -> p h d", h=H)
            tt("dve", scr.ap[0:nr, :], src.ap[0:nr, :], src.ap[0:nr, :], ALU.mult, [src], [scr])
            dve(lambda e, o=ss8.ap[0:nr, :], i=scr.ap[0:nr, :].rearrange("p (h d) -> p h d", h=H): e.tensor_reduce(out=o, in_=i, axis=AX.X, op=ALU.add), [scr], [ss8])
            act(l8.ap[0:nr, :], ss8.ap[0:nr, :], AF.Ln, [ss8, eps_b], [l8], scale=1.0 / HD, bias=eps_b.ap[0:nr, 0:1])
            act(r8.ap[0:nr, :], l8.ap[0:nr, :], AF.Exp, [l8], [r8], scale=-0.5)
            tt("dve", scr.ap[0:nr, :].rearrange("p (h d) -> p h d", h=H), s3, r8.ap[0:nr, :].unsqueeze(2).to_broadcast([nr, H, HD]), ALU.mult, [src, r8], [scr])
            tt("dve", dst.ap[0:nr, :].rearrange("p (h d) -> p h d", h=H), scr.ap[0:nr, :].rearrange("p (h d) -> p h d", h=H),
               gain.ap[0:nr, :].unsqueeze(1).to_broadcast([nr, H, HD]), ALU.mult, [scr, gain], [dst])

        def logf_of(nr):
            tt("dve", f_sb.ap[0:nr, :], f_sb.ap[0:nr, :], bfg.ap[0:nr, :], ALU.add, [f_sb, bfg], [f_sb])
            act(lf_sb.ap[0:nr, :], f_sb.ap[0:nr, :], AF.Exp, [f_sb], [lf_sb], scale=-1.0)
            ts("dve", lf_sb.ap[0:nr, :], lf_sb.ap[0:nr, :], 1.0, None, ALU.add, None, [lf_sb], [lf_sb])
            act(lf_sb.ap[0:nr, :], lf_sb.ap[0:nr, :], AF.Ln, [lf_sb], [lf_sb])
            ts("dve", lf_sb.ap[0:nr, :], lf_sb.ap[0:nr, :], -1.0, None, ALU.mult, None, [lf_sb], [lf_sb])

        kn2 = [kn, A.alloc([1024])]
        lf2 = [lf_sb, A.alloc([8])]

        def logf_of2(nr, lf):
            tt("dve", f_sb.ap[0:nr, :], f_sb.ap[0:nr, :], bfg.ap[0:nr, :], ALU.add, [f_sb, bfg], [f_sb])
            act(lf.ap[0:nr, :], f_sb.ap[0:nr, :], AF.Exp, [f_sb], [lf], scale=-1.0)
            ts("dve", lf.ap[0:nr, :], lf.ap[0:nr, :], 1.0, None, ALU.add, None, [lf], [lf])
            act(lf.ap[0:nr, :], lf.ap[0:nr, :], AF.Ln, [lf], [lf])
            ts("dve", lf.ap[0:nr, :], lf.ap[0:nr, :], -1.0, None, ALU.mult, None, [lf], [lf])

        def kvf_tile(it, rows, nr, owned, out_row0=None, sample=False):
            xf = xf2[it % 2]; hT = hT2[it % 2]
            knc = kn2[it % 2]; lf = lf2[it % 2]
            front(rows, nr, xf, hb, hT, junk, stat, gmix, [PB[4], PB[5]], zero_pad=sample)
            for blk in range(4):
                for kc in range(KC):
                    mm(PB[blk].ap[0:nr, :], PB[blk], hT.ap[:, kc, 0:nr], hT, WA.ap[:, kc, blk * 512:(blk + 1) * 512], WA, kc == 0, kc == KC - 1)
            for kc in range(KC):
                mm(PB[6].ap[0:nr, 0:8], PB[6], hT.ap[:, kc, 0:nr], hT, WA.ap[:, kc, 2048:2056], WA, kc == 0, kc == KC - 1)
            act(k_sb.ap[0:nr, 0:512], PB[0].ap[0:nr, :], AF.Copy, [PB[0]], [k_sb])
            act(k_sb.ap[0:nr, 512:1024], PB[1].ap[0:nr, :], AF.Copy, [PB[1]], [k_sb])
            cp("dve", f_sb.ap[0:nr, :], PB[6].ap[0:nr, 0:8], [PB[6]], [f_sb])
            kT = kT_st[it % 2]; vb = vb_st[it % 2]
            if owned:
                act(v_sb.ap[0:nr, 0:512], PB[2].ap[0:nr, :], AF.Copy, [PB[2]], [v_sb])
                act(v_sb.ap[0:nr, 512:1024], PB[3].ap[0:nr, :], AF.Copy, [PB[3]], [v_sb])
            else:
                act(vb.ap[:, 0:512], PB[2].ap[:, :], AF.Copy, [PB[2]], [vb])
                act(vb.ap[:, 512:1024], PB[3].ap[:, :], AF.Copy, [PB[3]], [vb])
            qknorm(k_sb, nr, gk, knc, tmpf)
            logf_of2(nr, lf)
            t0 = it * 128
            if owned:
                if not sample:
                    st("sp", ko_o[out_row0:out_row0 + 128, :], knc)
                    st("sp", vo_o[out_row0:out_row0 + 128, :], v_sb)
                    st("sp", lfo_o[out_row0:out_row0 + 128, :], lf)
                else:
                    for s in range(2):
                        st("sp", kso_o[s], knc, knc.ap[32 * s:32 * s + 16, :])
                        st("sp", vso_o[s], v_sb, v_sb.ap[32 * s:32 * s + 16, :])
                        st("sp", lfs_o[s], lf, lf.ap[32 * s:32 * s + 16, :])
                    memset("pool", vb_new, vb_new.ap[0:48, :, 128:129], 1.0)
                    cp("dve", vb_new.ap[0:48, :, 0:128], v_sb.ap[0:48, :].rearrange("p (h d) -> p h d", h=H), [v_sb], [vb_new])
            else:
                st("sp", Vs[:, t0:t0 + 128, :].rearrange("h t d -> t h d"), vb, vb.ap.rearrange("p (h d) -> p h d", h=H))

            def stage2():
                if owned:
                    if sample:
                        for h in range(H):
                            tp(PB[7].ap[:, h * 48:h * 48 + 48], PB[7], knc.ap[0:48, h * 128:(h + 1) * 128], knc, ident_f, 48)
                        cp("dve", kTs_new.ap.rearrange("p h t -> p (h t)"), PB[7].ap[:, 0:H * 48], [PB[7]], [kTs_new])
                        mm(PB[6].ap[0:48, 8:16], PB[6], triblk.ap[0:48, 0:48], triblk, lf.ap[0:48, :], lf, True, True)
                        ts("dve", nl_new.ap[0:48, :], PB[6].ap[0:48, 8:16], -1.0, None, ALU.mult, None, [PB[6]], [nl_new])
                    return
                for half in range(2):
                    for c in range(4):
                        h = half * 4 + c
                        tp(PB[7].ap[:, c * 128:(c + 1) * 128], PB[7], knc.ap[:, h * 128:(h + 1) * 128], knc, ident_f)
                    cp("dve", kT.ap[:, half * 4:half * 4 + 4, :].rearrange("p h t -> p (h t)"), PB[7].ap[:, :], [PB[7]], [kT])
                st("sp", KTs[:, :, t0:t0 + 128].rearrange("h d t -> d h t"), kT)
                mm(PB[6].ap[:, 8:16], PB[6], tri_f.ap, tri_f, lf.ap, lf, True, False)
                mm(PB[6].ap[:, 8:16], PB[6], e127.ap, e127, ce_prev.ap, ce_prev, False, True)
                cp("dve", fc_sb.ap, PB[6].ap[:, 8:16], [PB[6]], [fc_sb])
                ts("dve", NFK.ap[:, it, :], fc_sb.ap, -1.0, None, ALU.mult, None, [fc_sb], [NFK])
                mm(PB[6].ap[:, 16:24], PB[6], e127.ap, e127, fc_sb.ap, fc_sb, True, True)
                cp("dve", CE.ap[:, it, :], PB[6].ap[:, 16:24], [PB[6]], [CE])
                cp("dve", ce_prev.ap, fc_sb.ap, [fc_sb], [ce_prev])
            return stage2

        conv_tok = []
        P.nobar.add(id(P.slot_of(t_conv)))
        CR = NE // NT
        prev2 = None
        ce_toks = []
        for t in range(NT):
            s2 = kvf_tile(t, [(xb[t * 128:(t + 1) * 128, :], 0, 128)], 128, owned=False)
            if prev2 is not None:
                prev2()
            prev2 = s2
            for (dst_, src_) in ((Ub, peer_u), (Vb, peer_v)):
                conv_tok.append(P.dma("pool", lambda e, d=dst_[t * CR:(t + 1) * CR, :], s_=src_[t * CR:(t + 1) * CR, :]: e.dma_start(out=d, in_=s_), slot_t=t_conv,
                                      extra=[ce_toks[-3]] if len(ce_toks) >= 3 else []))
            if CE.t.lw is not None:
                ce_toks.append(CE.t.lw)
        prev2()
        selscr = A.alloc([8, NT])
        for i in range(NOWN):
            tt("dve", selscr.ap, CE.ap.rearrange("p t h -> p h t"), selb.ap[:, i, :].unsqueeze(1).to_broadcast([128, 8, NT]), ALU.mult, [CE, selb], [selscr])
            dve(lambda e, o=CEsel.ap[:, i, :], i_=selscr.ap: e.tensor_reduce(out=o, in_=i_, axis=AX.X, op=ALU.add), [selscr], [CEsel])
        for i in range(NOWN):
            kvf_tile(i, [(xo[i * 128:(i + 1) * 128, :], 0, 128)], 128, owned=True, out_row0=i * 128)()
        kvf_tile(NOWN, [(xs[0], 0, 16), (xs[1], 32, 16)], 48, owned=True, sample=True)()
        P.barrier()
        load_w(WA, w_in, D, O_Q, 1024, WA.ap[:, :, 0:1024])
        for i in range(NOWN + 1):
            sample = (i == NOWN)
            nr = 48 if sample else 128
            rows = [(xs[0], 0, 16), (xs[1], 32, 16)] if sample else [(xo[i * 128:(i + 1) * 128, :], 0, 128)]
            xf = xf2[i % 2]; hT = hT2[i % 2]
            front(rows, nr, xf, hb, hT, junk, stat, gmix, [PB[4], PB[5]], zero_pad=sample)
            for blk in range(2):
                for kc in range(KC):
                    mm(PB[blk].ap[0:nr, :], PB[blk], hT.ap[:, kc, 0:nr], hT, WA.ap[:, kc, blk * 512:(blk + 1) * 512], WA, kc == 0, kc == KC - 1)
            act(k_sb.ap[0:nr, 0:512], PB[0].ap[0:nr, :], AF.Copy, [PB[0]], [k_sb])
            act(k_sb.ap[0:nr, 512:1024], PB[1].ap[0:nr, :], AF.Copy, [PB[1]], [k_sb])
            qknorm(k_sb, nr, gq, kn, tmpf)
            if sample:
                for h in range(H):
                    tp(PB[7].ap[:, h * 48:h * 48 + 48], PB[7], kn.ap[0:48, h * 128:(h + 1) * 128], kn, ident_f, 48)
                cp("dve", QTs.ap.rearrange("p h t -> p (h t)"), PB[7].ap[:, 0:H * 48], [PB[7]], [QTs])
            else:
                for half in range(2):
                    pb = PB[6 + half]
                    for c in range(4):
                        h = half * 4 + c
                        tp(pb.ap[:, c * 128:(c + 1) * 128], pb, kn.ap[:, h * 128:(h + 1) * 128], kn, ident_f)
                    dst = QT.ap[:, half * 4:half * 4 + 4, i * 128:(i + 1) * 128]
                    src = pb.ap.rearrange("p (h t) -> p h t", h=4)
                    if half == 0:
                        cp("dve", dst, src, [pb], [QT])
                    else:
                        act(dst, src, AF.Copy, [pb], [QT])
        P.barrier()
        A.release(m_wa)
        KTh = [A.alloc([SEQ], BF16) for _ in range(2)]
        Vh = [A.alloc([NT, 130], BF16) for _ in range(2)]
        for b in Vh:
            memset("pool", b, b.ap[:, :, 128:129], 1.0)
        BS = min(4, NOWN)
        NBLK = NOWN // BS
        TS = 4 * BS
        PTw = [A.alloc([BS * 128], BF16) for _ in range(4)]
        onesrow = A.alloc([128], BF16)
        memset("dve", onesrow, onesrow.ap, 0.0)
        memset("dve", onesrow, onesrow.ap[0:1, :], 1.0)
        drow2 = [A.alloc([BS * 128], BF16) for _ in range(2)]
        for d_ in drow2:
            memset("dve", d_, d_.ap, 0.0)
        dsl = A.alloc([BS])
        penb = A.alloc([NOWN * 4]); ld("sp", penb, c_pen.partition_broadcast(128))
        bb2 = [A.alloc([NT]) for _ in range(2)]
        bu2 = [A.alloc([4]) for _ in range(2)]
        o_st = [A.alloc([128], BF16) for _ in range(2)]
        rden = A.alloc([2])
        ucount = 0
        units = []

        def mk_bulk(h, n, t, pre):
            kth = KTh[h % 2]; vh = Vh[h % 2]
            s0 = BS * n
            drow = drow2[(h * NBLK + n) % 2]; bb = bb2[(h * NBLK + n) % 2]
            f = min(ii for ii in range(BS) if slot_lo(s0 + ii) > t)
            c0 = f * 128; Wd = BS * 128 - c0
            return dict(h=h, n=n, kind="bulk", t=t, f=f, c0=c0, Wd=Wd, pre=pre)

        for h in range(H):
            for n in range(NBLK):
                s0 = BS * n
                ntb = slot_lo(s0 + BS - 1)
                first = True
                for t in range(ntb):
                    u = mk_bulk(h, n, t, first); first = False
                    units.append(u)
                for ii in range(BS):
                    for r in range(4):
                        units.append(dict(h=h, n=n, kind="diag", ii=ii, r=r, pre=first, pre_slot=(r == 0)))
                        first = False
        started = {}

        def stage1(u, idx):
            h, n = u["h"], u["n"]
            kth = KTh[h % 2]; vh = Vh[h % 2]
            s0 = BS * n; tR = TS * (n + 1) - 1
            drow = drow2[(h * NBLK + n) % 2]; bb = bb2[(h * NBLK + n) % 2]
            if u["pre"]:
                if n == 0:
                    ld("sp", kth, KTs[h])
                    ld("sp", vh, Vs[h].rearrange("(t p) d -> p t d", p=128), vh.ap[:, :, 0:128])
                tt("dve", dsl.ap[0:1, :], CEsel.ap[0:1, s0:s0 + BS, h], CE.ap[0:1, tR, h:h + 1].to_broadcast([1, BS]), ALU.subtract, [CEsel, CE], [dsl])
                ts("dve", drow.ap[0:1, :].rearrange("p (a b) -> p a b", a=BS), dsl.ap[0:1, :].unsqueeze(2).to_broadcast([1, BS, 128]), 1.0 / SCALE, None, ALU.mult, None, [dsl], [drow])
                ntb = slot_lo(s0 + BS - 1)
                if ntb > 0:
                    ts("dve", bb.ap[:, 0:ntb], NFK.ap[:, 0:ntb, h], CE.ap[:, tR, h:h + 1], None, ALU.add, None, [NFK, CE], [bb])
            psb = PB[4 + idx % 4]; pt = PTw[idx % 4]
            if u["kind"] == "bulk":
                t, f, c0, Wd = u["t"], u["f"], u["c0"], u["Wd"]
                mm(psb.ap[:, c0:c0 + Wd], psb, kth.ap[:, t * 128:(t + 1) * 128], kth, QT.ap[:, h, (s0 + f) * 128:(s0 + BS) * 128], QT, True, False)
                mm(psb.ap[:, c0:c0 + Wd], psb, onesrow.ap, onesrow, drow.ap[:, c0:c0 + Wd], drow, False, True)
                act(pt.ap[:, c0:c0 + Wd], psb.ap[:, c0:c0 + Wd], AF.Exp, [psb, bb], [pt], scale=SCALE, bias=bb.ap[:, t:t + 1])
            else:
                ii, r = u["ii"], u["r"]
                i = s0 + ii; lo = slot_lo(i); t = lo + r
                bu = bu2[(h * NOWN + i) % 2]
                if u["pre_slot"]:
                    tt("dve", bu.ap, NFK.ap[:, lo:lo + 4, h], penb.ap[:, i * 4:i * 4 + 4], ALU.add, [NFK, penb], [bu])
                    ts("dve", bu.ap, bu.ap, CE.ap[:, tR, h:h + 1], None, ALU.add, None, [bu, CE], [bu])
                mm(psb.ap[:, 0:128], psb, kth.ap[:, t * 128:(t + 1) * 128], kth, QT.ap[:, h, i * 128:(i + 1) * 128], QT, True, False)
                mm(psb.ap[:, 0:128], psb, onesrow.ap, onesrow, drow.ap[:, ii * 128:(ii + 1) * 128], drow, False, True)
                act(pt.ap[:, 0:128], psb.ap[:, 0:128], AF.Exp, [psb, bu], [pt], scale=SCALE, bias=bu.ap[:, r:r + 1])
                tt("pool", pt.ap[:, 0:128], pt.ap[:, 0:128], mkb.ap[:, i * 4 + r, :], ALU.mult, [pt, mkb], [pt])

        def stage2(u, idx):
            h, n = u["h"], u["n"]
            vh = Vh[h % 2]
            s0 = BS * n
            pt = PTw[idx % 4]
            if u["kind"] == "bulk":
                t, f = u["t"], u["f"]
                for ii in range(f, BS):
                    key = (h, n, ii)
                    mm(PB[ii].ap[:, 0:129], PB[ii], pt.ap[:, ii * 128:(ii + 1) * 128], pt, vh.ap[:, t, 0:129], vh, key not in started, False)
                    started[key] = True
            else:
                ii, r = u["ii"], u["r"]
                i = s0 + ii; lo = slot_lo(i); t = lo + r
                key = (h, n, ii)
                mm(PB[ii].ap[:, 0:129], PB[ii], pt.ap[:, 0:128], pt, vh.ap[:, t, 0:129], vh, key not in started, r == 3)
                started[key] = True
                if r == 3:
                    po = PB[ii]
                    ob = o_st[(h * NOWN + i) % 2]
                    dve(lambda e, o=rden.ap[:, 0:1], i_=po.ap[:, 128:129]: e.reciprocal(out=o, in_=i_), [po], [rden])
                    ts("dve", ob.ap, po.ap[:, 0:128], rden.ap[:, 0:1], None, ALU.mult, None, [po, rden], [ob])
                    st("sp", OAs[i * 128:(i + 1) * 128, h * 128:(h + 1) * 128], ob)

        SKEW = 2
        for idx in range(len(units) + SKEW):
            if idx < len(units):
                stage1(units[idx], idx)
            if idx - SKEW >= 0:
                stage2(units[idx - SKEW], idx - SKEW)
        P.barrier()
        A.release(m_wa)
        ckf = [A.alloc([NPT, 128]) for _ in range(2)]
        KTc = [A.alloc([NPT * 128], BF16) for _ in range(2)]
        Vc = [A.alloc([NPT, 130], BF16) for _ in range(2)]
        for b in Vc:
            memset("pool", b, b.ap[:, :, 128:129], 1.0)
        clf_sb = A.alloc([2, NPT, 8])
        nfk_c = A.alloc([2, NPT, 8])
        bias_c = A.alloc([2, NPT, 8])
        ce_c = A.alloc([8]); fcc = A.alloc([8]); cend = A.alloc([8])
        PTs = [A.alloc([16], BF16) for _ in range(4)]
        oas_st = A.alloc([1024], BF16)
        memset("pool", oas_st, oas_st.ap, 0.0)
        rden2 = A.alloc([2])
        for s in range(2):
            ld("sp", clf_sb, clf[s].rearrange("(t p) h -> p t h", p=128), clf_sb.ap[:, s, :, :])
            memset("dve", ce_c, ce_c.ap, 0.0)
            for t in range(NPT):
                mm(PB[1].ap[:, 0:8], PB[1], tri_f.ap, tri_f, clf_sb.ap[:, s, t, :], clf_sb, True, False)
                mm(PB[1].ap[:, 0:8], PB[1], e127.ap, e127, ce_c.ap, ce_c, False, True)
                cp("dve", fcc.ap, PB[1].ap[:, 0:8], [PB[1]], [fcc])
                ts("dve", nfk_c.ap[:, s, t, :], fcc.ap, -1.0, None, ALU.mult, None, [fcc], [nfk_c])
                cp("dve", ce_c.ap, fcc.ap, [fcc], [ce_c])
            mm(PB[1].ap[:, 8:16], PB[1], e127.ap, e127, fcc.ap, fcc, True, True)
            cp("dve", cend.ap, PB[1].ap[:, 8:16], [PB[1]], [cend])
            tt("dve", bias_c.ap[:, s, :, :], nfk_c.ap[:, s, :, :], cend.ap.unsqueeze(1).to_broadcast([128, NPT, 8]), ALU.add, [nfk_c, cend], [bias_c])
        ucount = 0
        for s in range(2):
            r0 = 32 * s
            for h in range(H):
                u = s * H + h
                cf = ckf[u % 2]; ktc = KTc[u % 2]; vc = Vc[u % 2]
                ld("sp", cf, ck[s].rearrange("(t p) (h d) -> p t h d", p=128, h=H)[:, :, h, :])
                ld("pool", vc, cv[s].rearrange("(t p) (h d) -> p t h d", p=128, h=H)[:, :, h, :], vc.ap[:, :, 0:128])
                for g4 in range((NPT + 3) // 4):
                    pb = PB[2 + g4 % 2]
                    n4 = min(4, NPT - g4 * 4)
                    for c in range(n4):
                        t = g4 * 4 + c
                        tp(pb.ap[:, c * 128:(c + 1) * 128], pb, cf.ap[:, t, :], cf, ident_f)
                    if g4 % 2 == 0:
                        cp("dve", ktc.ap[:, g4 * 512:g4 * 512 + n4 * 128], pb.ap[:, 0:n4 * 128], [pb], [ktc])
                    else:
                        act(ktc.ap[:, g4 * 512:g4 * 512 + n4 * 128], pb.ap[:, 0:n4 * 128], AF.Copy, [pb], [ktc])
                po = PB[0]
                qT = QTs.ap[:, h, r0:r0 + 16]
                for t in range(NPT):
                    psb = PB[4 + ucount % 4]; pt = PTs[ucount % 4]; ucount += 1
                    mm(psb.ap[:, 0:16], psb, ktc.ap[:, t * 128:(t + 1) * 128], ktc, qT, QTs, True, True)
                    act(pt.ap, psb.ap[:, 0:16], AF.Exp, [psb, bias_c], [pt], scale=SCALE, bias=bias_c.ap[:, s, t, h:h + 1])
                    mm(po.ap[r0:r0 + 16, 0:129], po, pt.ap, pt, vc.ap[:, t, 0:129], vc, t == 0, False)
                psb = PB[4 + ucount % 4]; pt = PTs[ucount % 4]; ucount += 1
                mm(psb.ap[r0:r0 + 16, 0:16], psb, kTs_new.ap[:, h, r0:r0 + 16], kTs_new, qT, QTs, True, True)
                act(pt.ap[r0:r0 + 16, :], psb.ap[r0:r0 + 16, 0:16], AF.Exp, [psb, nl_new], [pt], scale=SCALE, bias=nl_new.ap[r0:r0 + 16, h:h + 1])
                tt("pool", pt.ap[r0:r0 + 16, :], pt.ap[r0:r0 + 16, :], tri_b.ap[r0:r0 + 16, r0:r0 + 16], ALU.mult, [pt, tri_b], [pt])
                mm(po.ap[r0:r0 + 16, 0:129], po, pt.ap[r0:r0 + 16, :], pt, vb_new.ap[r0:r0 + 16, h, 0:129], vb_new, False, True)
                dve(lambda e, o=rden2.ap[r0:r0 + 16, 0:1], i_=po.ap[r0:r0 + 16, 128:129]: e.reciprocal(out=o, in_=i_), [po], [rden2])
                ts("dve", oas_st.ap[r0:r0 + 16, h * 128:(h + 1) * 128], po.ap[r0:r0 + 16, 0:128], rden2.ap[r0:r0 + 16, 0:1], None, ALU.mult, None, [po, rden2], [oas_st])
        st("sp", OAs[NROW:NROW + 48, :], oas_st, oas_st.ap[0:48, :])
        P.barrier()

        A.release(m_ab)
        trilWT = A.alloc([H, 128], BF16)
        bsT = A.alloc([H])
        WS_f = A.alloc([H, 48])
        WS = A.alloc([H, 48], BF16)
        bsS = A.alloc([H])
        skT = A.alloc([16, NK])
        m_c = A.mark()
        wsp_f = A.alloc([H, 128])
        sk_f = A.alloc([16, 128])
        ld("sp", wsp_f, w_spatial.rearrange("g i j -> i g j"))
        for g in range(H):
            pb = PB[4 + g % 2]
            tp(pb.ap[:, 0:128], pb, wsp_f.ap[:, g, :], wsp_f, ident_f)
            tt("dve", trilWT.ap[:, g, :], pb.ap[:, 0:128], tri_f.ap, ALU.mult, [pb, tri_f], [trilWT])
        ld_nc("sp", bsT, b_spatial.rearrange("g i -> i g"))
        memset("dve", WS_f, WS_f.ap, 0.0)
        for s_ in range(2):
            for g in range(H):
                ld_nc("sp", WS_f, w_spatial[g, 0:16, 0:16].rearrange("i j -> j i"), WS_f.ap[32 * s_:32 * s_ + 16, g, 32 * s_:32 * s_ + 16])
        tt("dve", WS.ap[0:48, :, :], WS_f.ap[0:48, :, :], tri_f.ap[0:48, 0:48].unsqueeze(1).to_broadcast([48, H, 48]), ALU.mult, [WS_f, tri_f], [WS])
        memset("dve", bsS, bsS.ap, 0.0)
        for s_ in range(2):
            ld_nc("sp", bsS, b_spatial[:, 0:16].rearrange("g i -> i g"), bsS.ap[32 * s_:32 * s_ + 16, :])
        if NK < 128:
            memset("dve", sk_f, sk_f.ap, 0.0)
        ld("sp", sk_f, sub_keys.rearrange("h two k d -> k (h two) d"), sk_f.ap[0:NK, :, :])
        for g in range(16):
            pb = PB[4 + g % 2]
            tp(pb.ap[:, 0:128], pb, sk_f.ap[:, g, :], sk_f, ident_f)
            cp("dve", skT.ap[:, g, :], pb.ap[:, 0:NK], [pb], [skT])
        P.barrier()
        A.release(m_c)

        class TB:
            pass
        tbs = []
        for g in range(GSZ):
            b = TB()
            b.xf = A.alloc([D])
            b.hT = A.alloc([KC, 128], BF16)
            r1 = A.alloc([2560])
            b.q = Buf(r1.ap[:, 0:2048]); b.q.t = r1.t
            rb = r1.ap.bitcast(BF16)
            b.y = Buf(rb[:, 0:2048]); b.y.t = r1.t
            b.us = Buf(rb[:, 2048:3072]); b.us.t = r1.t
            b.obT = Buf(rb[:, 3072:4096].rearrange("p (a b) -> p a b", a=8)); b.obT.t = r1.t
            b.oaT = Buf(rb[:, 4096:5120].rearrange("p (a b) -> p a b", a=8)); b.oaT.t = r1.t
            b.stat = A.alloc([4])
            tbs.append(b)
        m_r2 = A.mark()
        Wb = [A.alloc([KC, 512], BF16) for _ in range(3)]
        Wb0lo = Buf(Wb[0].ap[:, 0:8, :]); Wb0lo.t = Wb[0].t
        Wb0hi = Buf(Wb[0].ap[:, 8:16, :]); Wb0hi.t = Wb[0].t
        junkc = A.alloc([D], BF16)
        hbc = A.alloc([D], BF16)
        xs_sb = A.alloc([512]); sq_sb = A.alloc([512]); u_sb = A.alloc([512])
        vg = A.alloc([1024]); vsn = A.alloc([1024]); vsnb = A.alloc([1024], BF16); ob_sb = A.alloc([1024], BF16)
        oa_sb = A.alloc([1024], BF16)
        sga = A.alloc([512]); sgb = A.alloc([512]); y1 = A.alloc([512])
        A.release(m_r2)
        xn2 = [A.alloc([D], BF16) for _ in range(2)]
        qn = A.alloc([D])
        qnT = A.alloc([16, 128]); yout = Buf(qnT.ap.rearrange("p a b -> p (a b)")); yout.t = qnT.t
        sc = A.alloc([16, NK])
        wk2 = [A.alloc([NK]) for _ in range(2)]
        t16 = A.alloc([16, 16])
        i16u = A.alloc([16, 16], U32)
        i16f = A.alloc([16, 16])
        cand2 = [A.alloc([256]) for _ in range(2)]
        cwk2 = [A.alloc([256]) for _ in range(2)]
        st16 = A.alloc([H, 16]); selu = A.alloc([H, 16], U32)
        selA = A.alloc([H, 16], U32); selB = A.alloc([H, 16], U32); aF = A.alloc([H, 16]); bF = A.alloc([H, 16]); ea = A.alloc([H, 16]); eb = A.alloc([H, 16])
        eidf = A.alloc([H * 16]); eidi2 = [A.alloc([H * 16], I32) for _ in range(2)]
        gat2 = [A.alloc([H, 16]) for _ in range(2)]; zs = A.alloc([H]); ss8c = A.alloc([8]); r8c = A.alloc([8]); l8c = A.alloc([8])
        NUV = 6
        acol = [A.alloc([2]) for _ in range(NUV)]; wcol = [A.alloc([2]) for _ in range(NUV)]
        dg = [A.alloc([128], BF16) for _ in range(6)]
        GBUV = [A.alloc([2 * D], BF16) for _ in range(6)]
        gj = A.alloc([D], BF16); junkd = gj
        gcu = [0]; gcv = [0]
        _breg = {}

        def breg(e):
            if "r" not in _breg:
                _breg["r"] = e.to_reg(NE - 1)
            return _breg["r"]

        def gelu_from(pb, nr, dst_ap, dst):
            act(xs_sb.ap[0:nr, :], pb.ap[0:nr, :], AF.Copy, [pb], [xs_sb])
            tt("dve", sq_sb.ap[0:nr, :], xs_sb.ap[0:nr, :], xs_sb.ap[0:nr, :], ALU.mult, [xs_sb], [sq_sb])
            tt("dve", sq_sb.ap[0:nr, :], sq_sb.ap[0:nr, :], xs_sb.ap[0:nr, :], ALU.mult, [sq_sb, xs_sb], [sq_sb])
            stt(u_sb.ap[0:nr, :], sq_sb.ap[0:nr, :], 0.044715, xs_sb.ap[0:nr, :], ALU.mult, ALU.add, [sq_sb, xs_sb], [u_sb])
            act(u_sb.ap[0:nr, :], u_sb.ap[0:nr, :], AF.Sigmoid, [u_sb], [u_sb], scale=2.0 * GELU_C)
            tt("dve", dst_ap, xs_sb.ap[0:nr, :], u_sb.ap[0:nr, :], ALU.mult, [xs_sb, u_sb], [dst])

        def proj(pb, nr, hT, wt, kc_n, ncols=512):
            for kc in range(kc_n):
                mm(pb.ap[0:nr, 0:ncols], pb, hT.ap[:, kc, 0:nr], hT, wt.ap[:, kc, 0:ncols], wt, kc == 0, kc == kc_n - 1)

        def group(tiles):
            ng = len(tiles)
            for g, tl in enumerate(tiles):
                b = tbs[g]
                front(tl["rows"], tl["nr"], b.xf, hbc, b.hT, junkc, b.stat, gmix, [PB[4], PB[5]], zero_pad=tl["sample"])
                ld("sp", oa_sb, OAs[tl["oa_row0"]:tl["oa_row0"] + tl["nr"], :], oa_sb.ap[0:tl["nr"], :])
                transpose16(oa_sb, tl["nr"], b.oaT, [PB[6]], nchunk=8)
            for blk in range(2):
                wt = Wb[blk]
                load_w(wt, w_in, D, O_U + blk * 512, 512)
                for g, tl in enumerate(tiles):
                    b = tbs[g]; nr = tl["nr"]; pb = PB[g % 4]
                    proj(pb, nr, b.hT, wt, KC)
                    gelu_from(pb, nr, b.us.ap[0:nr, blk * 512:(blk + 1) * 512], b.us)
            wv = [Wb[2], Wb[0]]
            for blk in range(2):
                load_w(wv[blk], w_in, D, O_VS + blk * 512, 512)
            for g, tl in enumerate(tiles):
                b = tbs[g]; nr = tl["nr"]
                for blk in range(2):
                    pb = PB[blk]
                    proj(pb, nr, b.hT, wv[blk], KC)
                    gelu_from(pb, nr, vg.ap[0:nr, blk * 512:(blk + 1) * 512], vg)
                rstd_of(vg, nr, 1024, junkc, b.stat)
                stt(vsn.ap[0:nr, :], vg.ap[0:nr, :], b.stat.ap[0:nr, 2:3], gv.ap[0:nr, :], ALU.mult, ALU.mult, [vg, b.stat, gv], [vsn])
                if tl["sample"]:
                    for s in range(2):
                        st("sp", sgv_o[s], vsn, vsn.ap[32 * s:32 * s + 16, :])
                cp("pool", vsnb.ap[0:nr, :], vsn.ap[0:nr, :], [vsn], [vsnb])
                for gg in range(H):
                    pb = PB[2 + gg // 4]
                    lhs = WS.ap[0:48, gg, :] if tl["sample"] else trilWT.ap[:, gg, :]
                    lb = WS if tl["sample"] else trilWT
                    mm(pb.ap[0:nr, (gg % 4) * 128:(gg % 4 + 1) * 128], pb, lhs, lb, vsnb.ap[0:nr, gg * 128:(gg + 1) * 128], vsnb, True, True)
                bs_ = bsS if tl["sample"] else bsT
                for gg in range(H):
                    pb = PB[2 + gg // 4]
                    stt(ob_sb.ap[0:nr, gg * 128:(gg + 1) * 128], pb.ap[0:nr, (gg % 4) * 128:(gg % 4 + 1) * 128], bs_.ap[0:nr, gg:gg + 1],
                        b.us.ap[0:nr, gg * 128:(gg + 1) * 128], ALU.add, ALU.mult, [pb, bs_, b.us], [ob_sb])
                transpose16(ob_sb, nr, b.obT, [PB[6]], nchunk=8)
            for blk in range(4):
                load_w(Wb[0], w_out_a, 1024, blk * 512, 512, Wb[0].ap[:, 0:8, :])
                load_w(Wb[0], w_out_b, 1024, blk * 512, 512, Wb[0].ap[:, 8:16, :])
                load_w(Wb[1], w_in, D, O_GA + blk * 512, 512)
                load_w(Wb[2], w_in, D, O_GB + blk * 512, 512)
                for g, tl in enumerate(tiles):
                    b = tbs[g]; nr = tl["nr"]
                    proj(PB[0], nr, b.oaT, Wb0lo, 8)
                    proj(PB[1], nr, b.obT, Wb0hi, 8)
                    proj(PB[2], nr, b.hT, Wb[1], KC)
                    proj(PB[3], nr, b.hT, Wb[2], KC)
                    act(sga.ap[0:nr, :], PB[2].ap[0:nr, :], AF.Sigmoid, [PB[2]], [sga])
                    act(sgb.ap[0:nr, :], PB[3].ap[0:nr, :], AF.Sigmoid, [PB[3]], [sgb])
                    tt("dve", y1.ap[0:nr, :], PB[0].ap[0:nr, :], sga.ap[0:nr, :], ALU.mult, [PB[0], sga], [y1])
                    tt("dve", sgb.ap[0:nr, :], PB[1].ap[0:nr, :], sgb.ap[0:nr, :], ALU.mult, [PB[1], sgb], [sgb])
                    tt("dve", b.y.ap[0:nr, blk * 512:(blk + 1) * 512], y1.ap[0:nr, :], sgb.ap[0:nr, :], ALU.add, [y1, sgb], [b.y])
            for g, tl in enumerate(tiles):
                b = tbs[g]
                transpose16(b.y, tl["nr"], b.hT, [PB[4], PB[5]])
            for blk in range(4):
                wt = Wb[blk % 3]
                load_w(wt, w_out, D, blk * 512, 512)
                for g, tl in enumerate(tiles):
                    b = tbs[g]; nr = tl["nr"]; pb = PB[g % 4]
                    proj(pb, nr, b.hT, wt, KC)
                    tt("dve", b.xf.ap[0:nr, blk * 512:(blk + 1) * 512], b.xf.ap[0:nr, blk * 512:(blk + 1) * 512], pb.ap[0:nr, :], ALU.add, [b.xf, pb], [b.xf])
            for g, tl in enumerate(tiles):
                b = tbs[g]; nr = tl["nr"]
                rstd_of(b.xf, nr, D, junkc, b.stat)
                stt(hbc.ap[0:nr, :], b.xf.ap[0:nr, :], b.stat.ap[0:nr, 2:3], gffn.ap[0:nr, :], ALU.mult, ALU.mult, [b.xf, b.stat, gffn], [hbc])
                transpose16(hbc, nr, b.hT, [PB[4], PB[5]])
            for blk in range(4):
                wt = Wb[blk % 3]
                load_w(wt, w_peer_q, D, blk * 512, 512)
                for g, tl in enumerate(tiles):
                    b = tbs[g]; nr = tl["nr"]; pb = PB[g % 4]
                    proj(pb, nr, b.hT, wt, KC)
                    act(b.q.ap[0:nr, blk * 512:(blk + 1) * 512], pb.ap[0:nr, :], AF.Copy, [pb], [b.q])
            P.barrier()
            def routing(g):
                tl = tiles[g]; b = tbs[g]; nr = tl["nr"]; eidi = eidi2[g % 2]; xn = xn2[g % 2]; gat = gat2[g % 2]
                rstd_of(b.xf, nr, D, junkd, b.stat)
                stt(xn.ap[0:nr, :], b.xf.ap[0:nr, :], b.stat.ap[0:nr, 2:3], gffn.ap[0:nr, :], ALU.mult, ALU.mult, [b.xf, b.stat, gffn], [xn])
                q3 = b.q.ap[0:nr, :].rearrange("p (h d) -> p h d", h=H)
                qn3 = qn.ap[0:nr, :].rearrange("p (h d) -> p h d", h=H)
                tt("dve", qn.ap[0:nr, :], b.q.ap[0:nr, :], b.q.ap[0:nr, :], ALU.mult, [b.q], [qn])
                dve(lambda e, o=ss8c.ap[0:nr, :], i_=qn3: e.tensor_reduce(out=o, in_=i_, axis=AX.X, op=ALU.add), [qn], [ss8c])
                act(l8c.ap[0:nr, :], ss8c.ap[0:nr, :], AF.Ln, [ss8c, eps_b], [l8c], scale=1.0 / 256, bias=eps_b.ap[0:nr, 0:1])
                act(r8c.ap[0:nr, :], l8c.ap[0:nr, :], AF.Exp, [l8c], [r8c], scale=-0.5)
                tt("dve", qn3, q3, r8c.ap[0:nr, :].unsqueeze(2).to_broadcast([nr, H, 256]), ALU.mult, [b.q, r8c], [qn])
                tt("dve", qn3, qn3, gpq.ap[0:nr, :].unsqueeze(1).to_broadcast([nr, H, 256]), ALU.mult, [qn, gpq], [qn])
                yield
                for half in range(4):
                    pb = PB[4 + half % 2]
                    for c in range(4):
                        gg = half * 4 + c
                        tp(pb.ap[:, c * 128:c * 128 + nr], pb, qn.ap[0:nr, gg * 128:(gg + 1) * 128], qn, ident_f, nr)
                    dst = qnT.ap[:, half * 4:half * 4 + 4, 0:nr]
                    src = pb.ap.rearrange("p (a b) -> p a b", a=4)[:, :, 0:nr]
                    if half % 2 == 0:
                        cp("dve", dst, src, [pb], [qnT])
                    else:
                        act(dst, src, AF.Copy, [pb], [qnT])
                    yield
                gpb = 512 // NK
                for gg in range(16):
                    pb = PB[6 + (gg // gpb) % 2]
                    col = (gg % gpb) * NK
                    mm(pb.ap[0:nr, col:col + NK], pb, qnT.ap[:, gg, 0:nr], qnT, skT.ap[:, gg, :], skT, True, True)
                    if gg % gpb == gpb - 1 or gg == 15:
                        g0 = (gg // gpb) * gpb
                        ng_ = gg - g0 + 1
                        cp("dve", sc.ap[0:nr, g0:g0 + ng_, :].rearrange("p a b -> p (a b)"), pb.ap[0:nr, 0:ng_ * NK], [pb], [sc])
                        yield
                for gg in range(16):
                    wk = wk2[gg % 2]
                    dve(lambda e, o=t16.ap[0:nr, gg, 0:8], i_=sc.ap[0:nr, gg, :]: e.max(out=o, in_=i_), [sc], [t16])
                    dve(lambda e, o=i16u.ap[0:nr, gg, 0:8], m=t16.ap[0:nr, gg, 0:8], v=sc.ap[0:nr, gg, :]: e.max_index(out=o, in_max=m, in_values=v), [sc, t16], [i16u])
                    dve(lambda e, o=wk.ap[0:nr, :], m=t16.ap[0:nr, gg, 0:8], v=sc.ap[0:nr, gg, :]: e.match_replace(out=o, in_to_replace=m, in_values=v, imm_value=NEG), [sc, t16], [wk])
                    dve(lambda e, o=t16.ap[0:nr, gg, 8:16], i_=wk.ap[0:nr, :]: e.max(out=o, in_=i_), [wk], [t16])
                    dve(lambda e, o=i16u.ap[0:nr, gg, 8:16], m=t16.ap[0:nr, gg, 8:16], v=wk.ap[0:nr, :]: e.max_index(out=o, in_max=m, in_values=v), [wk, t16], [i16u])
                    yield
                cp("dve", i16f.ap[0:nr, :, :], i16u.ap[0:nr, :, :], [i16u], [i16f])
                t4 = t16.ap[0:nr, :, :].rearrange("p (h two) k -> p h two k", two=2)
                i4 = i16f.ap[0:nr, :, :].rearrange("p (h two) k -> p h two k", two=2)
                for hh in range(H):
                    cand = cand2[hh % 2]; cwk = cwk2[hh % 2]
                    c3 = cand.ap[0:nr, :].rearrange("p (a b) -> p a b", a=16)
                    tt("dve", c3, t4[:, hh, 0, :].unsqueeze(2).to_broadcast([nr, 16, 16]), t4[:, hh, 1, :].unsqueeze(1).to_broadcast([nr, 16, 16]), ALU.add, [t16], [cand])
                    dve(lambda e, o=st16.ap[0:nr, hh, 0:8], i_=cand.ap[0:nr, :]: e.max(out=o, in_=i_), [cand], [st16])
                    dve(lambda e, o=selu.ap[0:nr, hh, 0:8], m=st16.ap[0:nr, hh, 0:8], v=cand.ap[0:nr, :]: e.max_index(out=o, in_max=m, in_values=v), [cand, st16], [selu])
                    dve(lambda e, o=cwk.ap[0:nr, :], m=st16.ap[0:nr, hh, 0:8], v=cand.ap[0:nr, :]: e.match_replace(out=o, in_to_replace=m, in_values=v, imm_value=NEG), [cand, st16], [cwk])
                    dve(lambda e, o=st16.ap[0:nr, hh, 8:16], i_=cwk.ap[0:nr, :]: e.max(out=o, in_=i_), [cwk], [st16])
                    dve(lambda e, o=selu.ap[0:nr, hh, 8:16], m=st16.ap[0:nr, hh, 8:16], v=cwk.ap[0:nr, :]: e.max_index(out=o, in_max=m, in_values=v), [cwk, st16], [selu])
                    yield
                ts("dve", selA.ap[0:nr, :, :], selu.ap[0:nr, :, :], 4, None, ALU.logical_shift_right, None, [selu], [selA])
                ts("dve", selB.ap[0:nr, :, :], selu.ap[0:nr, :, :], 15, None, ALU.bitwise_and, None, [selu], [selB])
                cp("dve", aF.ap[0:nr, :, :], selA.ap[0:nr, :, :], [selA], [aF])
                cp("dve", bF.ap[0:nr, :, :], selB.ap[0:nr, :, :], [selB], [bF])
                yield
                oh4 = qn.ap[0:nr, :].rearrange("p (h r c) -> p h r c", h=H, r=16)
                io4 = iota.ap[0:nr, 0:16].unsqueeze(1).unsqueeze(1).to_broadcast([nr, H, 16, 16])
                for which, (xF, eX) in enumerate(((aF, ea), (bF, eb))):
                    tt("dve", oh4, xF.ap[0:nr, :, :].unsqueeze(3).to_broadcast([nr, H, 16, 16]), io4, ALU.is_equal, [xF, iota], [qn])
                    yield
                    tt("dve", oh4, oh4, i4[:, :, which, :].unsqueeze(2).to_broadcast([nr, H, 16, 16]), ALU.mult, [qn, i16f], [qn])
                    yield
                    dve(lambda e, o=eX.ap[0:nr, :, :], i_=oh4: e.tensor_reduce(out=o, in_=i_, axis=AX.X, op=ALU.add), [qn], [eX])
                    yield
                stt(eidf.ap[0:nr, :], ea.ap[0:nr, :, :].rearrange("p h k -> p (h k)"), float(NK), eb.ap[0:nr, :, :].rearrange("p h k -> p (h k)"), ALU.mult, ALU.add, [ea, eb], [eidf])
                cp("dve", eidi.ap[0:nr, :], eidf.ap[0:nr, :], [eidf], [eidi])
                tt("dve", gat.ap[0:nr, :, :], st16.ap[0:nr, :, :], st16.ap[0:nr, :, 0:1].to_broadcast([nr, H, 16]), ALU.subtract, [st16], [gat])
                act(gat.ap[0:nr, :, :], gat.ap[0:nr, :, :], AF.Exp, [gat], [gat])
                dve(lambda e, o=zs.ap[0:nr, :], i_=gat.ap[0:nr, :, :]: e.tensor_reduce(out=o, in_=i_, axis=AX.X, op=ALU.add), [gat], [zs])
                dve(lambda e, o=zs.ap[0:nr, :], i_=zs.ap[0:nr, :]: e.reciprocal(out=o, in_=i_), [zs], [zs])
                tt("dve", gat.ap[0:nr, :, :], gat.ap[0:nr, :, :], zs.ap[0:nr, :].unsqueeze(2).to_broadcast([nr, H, 16]), ALU.mult, [gat, zs], [gat])
            def slots(g, s_from, s_to):
                tl = tiles[g]; b = tbs[g]; nr = tl["nr"]; eidi = eidi2[g % 2]; xn = xn2[g % 2]; gat = gat2[g % 2]
                gflat = gat.ap[0:nr, :, :].rearrange("p h k -> p (h k)")
                for sl in range(s_from, s_to):
                    k = gcu[0] % 6; gcu[0] += 1
                    gb = GBUV[k]; ac = acol[k]; wc = wcol[k]; d_ = dg[k]
                    P.dma("pool", lambda e, o=gb.ap[0:nr, :], ix=eidi.ap[0:nr, sl:sl + 1]: e.indirect_dma_start(
                        out=o, out_offset=None, in_=UVb, in_offset=bass.IndirectOffsetOnAxis(ap=ix, axis=0), bounds_check=breg(e), oob_is_err=False),
                        reads=[eidi.t], writes=[gb.t], extra=conv_tok[-1:])
                    stt(gj.ap[0:nr, :], gb.ap[0:nr, 0:D], 1.0, xn.ap[0:nr, :], ALU.mult, ALU.mult, [gb, xn], [gj, ac], accum=ac.ap[0:nr, 0:1])
                    act(wc.ap[0:nr, 0:1], ac.ap[0:nr, 0:1], AF.Square, [ac], [wc])
                    act(wc.ap[0:nr, 0:1], wc.ap[0:nr, 0:1], AF.Identity, [wc, one_b], [wc], scale=0.044715, bias=one_b.ap[0:nr, 0:1])
                    act(wc.ap[0:nr, 0:1], wc.ap[0:nr, 0:1], AF.Copy, [wc, ac], [wc], scale=ac.ap[0:nr, 0:1])
                    act(wc.ap[0:nr, 0:1], wc.ap[0:nr, 0:1], AF.Sigmoid, [wc], [wc], scale=2.0 * GELU_C)
                    act(wc.ap[0:nr, 0:1], wc.ap[0:nr, 0:1], AF.Copy, [wc, ac], [wc], scale=ac.ap[0:nr, 0:1])
                    act(wc.ap[0:nr, 1:2], wc.ap[0:nr, 0:1], AF.Copy, [wc, gat], [wc], scale=gflat[:, sl:sl + 1])
                    act(d_.ap[0:nr, 0:nr], ident_b.ap[0:nr, 0:nr], AF.Copy, [ident_b, wc], [d_], scale=wc.ap[0:nr, 1:2])
                    for blk in range(4):
                        mm(PB[blk].ap[0:nr, :], PB[blk], d_.ap[0:nr, 0:nr], d_, gb.ap[0:nr, D + blk * 512:D + (blk + 1) * 512], gb, sl == 0, sl == H * 16 - 1)
                if s_to == H * 16:
                    for blk in range(4):
                        tt("dve", yout.ap[0:nr, blk * 512:(blk + 1) * 512], b.xf.ap[0:nr, blk * 512:(blk + 1) * 512], PB[blk].ap[0:nr, :], ALU.add, [b.xf, PB[blk]], [yout])
                    for (dst, p0, n) in tl["outs"]:
                        st("sp", dst, yout, yout.ap[p0:p0 + n, :])

            for _ in routing(0):
                pass
            for g in range(ng):
                rgen = routing(g + 1) if g + 1 < ng else None
                for sl in range(H * 16):
                    slots(g, sl, sl + 1)
                    if rgen is not None and sl % 2 == 1:
                        try:
                            next(rgen)
                        except StopIteration:
                            rgen = None
                if rgen is not None:
                    for _ in rgen:
                        pass
            P.barrier()

        alltiles = []
        for i in range(NOWN):
            alltiles.append(dict(rows=[(xo[i * 128:(i + 1) * 128, :], 0, 128)], nr=128, oa_row0=i * 128, sample=False,
                                 outs=[(y_o[i * 128:(i + 1) * 128, :], 0, 128)]))
        for g0 in range(0, NOWN, GSZ):
            group(alltiles[g0:g0 + GSZ])
        group([dict(rows=[(xs[0], 0, 16), (xs[1], 32, 16)], nr=48, oa_row0=NROW, sample=True,
                    outs=[(ys_o[0], 0, 16), (ys_o[1], 32, 16)])])
        P.emit()
    return nc


_NC_CACHE = {}


def _consts(cfg, j):
    ident = np.eye(128, dtype=np.float32)
    k = np.arange(128)
    tri = (k[:, None] <= k[None, :]).astype(np.float32)
    e127 = np.zeros((128, 128), np.float32); e127[127, :] = 1.0
    triblk = tri * ((k[:, None] // 32) == (k[None, :] // 32)).astype(np.float32)
    iota = np.tile(np.arange(256, dtype=np.float32), (128, 1))
    own = own_tiles(cfg, j)
    sel = np.zeros((cfg.NOWN, cfg.NT), np.float32)
    mk = np.zeros((128, cfg.NOWN * 4, 128), np.float32)
    pen = np.zeros((1, cfg.NOWN * 4), np.float32)
    for i, Tt in enumerate(own):
        sel[i, Tt] = 1.0
        lo = slot_lo(i)
        for r in range(4):
            t = lo + r
            if t < Tt:
                mk[:, i * 4 + r, :] = 1.0
            elif t == Tt:
                mk[:, i * 4 + r, :] = tri
            else:
                pen[0, i * 4 + r] = NEG
    return dict(c_ident=ident, c_tri=tri, c_e127=e127, c_triblk=triblk, c_iota=iota,
                c_sel=sel.reshape(1, -1), c_mk=np.ascontiguousarray(mk.reshape(128, -1)), c_pen=pen)


def kernel(x_prompt, x_sample, cache_k, cache_v, cache_logf, norm_mix_g, w_in, b_forget,
           q_norm_g, k_norm_g, v_norm_g, w_spatial, b_spatial, w_out_a, w_out_b, w_out,
           norm_ffn_g, w_peer_q, peer_q_norm_g, peer_sub_keys, peer_u, peer_v):
    f = lambda a: np.ascontiguousarray(np.asarray(a, dtype=np.float32))
    x_prompt = f(x_prompt); x_sample = f(x_sample)
    B, SEQ, _ = x_prompt.shape
    DB, DS, _ = x_sample.shape
    PAST = cache_k.shape[2]
    NK = peer_sub_keys.shape[3]
    assert B == 2 and DB == 16 and DS == 16
    cfg = Cfg(SEQ, PAST, NK)
    key = (SEQ, PAST, NK)
    if key not in _NC_CACHE:
        _NC_CACHE[key] = build(cfg)
    nc = _NC_CACHE[key]
    ck = f(cache_k)[0].reshape(16, PAST, 1024)
    cv = f(cache_v)[0].reshape(16, PAST, 1024)
    clf = f(cache_logf)[0]
    shared = dict(
        norm_mix_g=f(norm_mix_g).reshape(1, D), w_in=f(w_in)[0], b_forget=f(b_forget).reshape(1, 8),
        q_norm_g=f(q_norm_g).reshape(1, 128), k_norm_g=f(k_norm_g).reshape(1, 128), v_norm_g=f(v_norm_g).reshape(1, 1024),
        w_spatial=f(w_spatial)[0], b_spatial=f(b_spatial)[0], w_out_a=f(w_out_a)[0], w_out_b=f(w_out_b)[0],
        w_out=f(w_out)[0], norm_ffn_g=f(norm_ffn_g).reshape(1, D), w_peer_q=f(w_peer_q)[0],
        peer_q_norm_g=f(peer_q_norm_g).reshape(1, 256), peer_sub_keys=f(peer_sub_keys)[0],
        peer_u=f(peer_u)[0], peer_v=f(peer_v)[0])
    in_maps = []
    owns = []
    for c in range(8):
        b, j = divmod(c, 4)
        own = own_tiles(cfg, j)
        owns.append(own)
        xo = np.concatenate([x_prompt[b, t * 128:(t + 1) * 128] for t in own], axis=0)
        m = dict(shared)
        m.update(xb=x_prompt[b], xo=xo, xs=x_sample[2 * c:2 * c + 2], ck=ck[2 * c:2 * c + 2], cv=cv[2 * c:2 * c + 2],
                 clf=clf[2 * c:2 * c + 2])
        m.update(_consts(cfg, j))
        in_maps.append(m)
    res = run_bass_kernel_spmd(nc, in_maps, core_ids=list(range(8))).results
    y_p = np.zeros((2, SEQ, D), np.float32)
    k_p = np.zeros((1, 2, SEQ, 8, 128), np.float32)
    v_p = np.zeros((1, 2, SEQ, 8, 128), np.float32)
    lf_p = np.zeros((1, 2, SEQ, 8), np.float32)
    y_s = np.zeros((16, 16, D), np.float32)
    k_s = np.zeros((1, 16, 16, 8, 128), np.float32)
    v_s = np.zeros((1, 16, 16, 8, 128), np.float32)
    lf_s = np.zeros((1, 16, 16, 8), np.float32)
    sg_s = np.zeros((1, 16, 16, 1024), np.float32)
    for c in range(8):
        b, j = divmod(c, 4)
        r = res[c]
        for i, t in enumerate(owns[c]):
            sl = slice(t * 128, (t + 1) * 128)
            y_p[b, sl] = r["y"][i * 128:(i + 1) * 128]
            k_p[0, b, sl] = r["ko"][i * 128:(i + 1) * 128].reshape(128, 8, 128)
            v_p[0, b, sl] = r["vo"][i * 128:(i + 1) * 128].reshape(128, 8, 128)
            lf_p[0, b, sl] = r["lfo"][i * 128:(i + 1) * 128]
        y_s[2 * c:2 * c + 2] = r["ys"]
        k_s[0, 2 * c:2 * c + 2] = r["kso"].reshape(2, 16, 8, 128)
        v_s[0, 2 * c:2 * c + 2] = r["vso"].reshape(2, 16, 8, 128)
        lf_s[0, 2 * c:2 * c + 2] = r["lfs"]
        sg_s[0, 2 * c:2 * c + 2] = r["sgv"]
    return (y_p, y_s, k_p, v_p, lf_p, k_s, v_s, lf_s, sg_s)
```

```python
import numpy as np
from contextlib import ExitStack
import concourse.bass as bass
import concourse.mybir as mybir
from concourse.bass_utils import run_bass_kernel_spmd

F32 = mybir.dt.float32
BF16 = mybir.dt.bfloat16
I32 = mybir.dt.int32
U32 = mybir.dt.uint32
ALU = mybir.AluOpType
AF = mybir.ActivationFunctionType
AX = mybir.AxisListType

D = 2048
KC = 16
H = 8
HD = 128
O_Q, O_K, O_V, O_F, O_U, O_VS, O_GA, O_GB = 0, 1024, 2048, 3072, 3080, 4104, 5128, 7176
INW = 9224
RMS_EPS = 1e-6
SCALE = HD ** -0.5
GELU_C = 0.7978845608028654
TOPK = 16
NEG = -1e30
ENGS = ("pe", "act", "dve", "pool", "sp")
SAME_ENG_SYNC = True
GSZ = 3
NGB = 6


class T:
    __slots__ = ("lw", "rd", "slot")

    def __init__(self):
        self.lw = None
        self.rd = []
        self.slot = None


class Prog:
    def __init__(self, nc, es):
        self.nc = nc
        self.es = es
        self.ops = {e: [] for e in ENGS}
        self.cnt = {e: 0 for e in ENGS}
        self.esem = {e: es.enter_context(nc.semaphore("sem_" + e)) for e in ENGS}
        self.slots = []
        self.nobar = set()

    def slot_of(self, t):
        if t.slot is None:
            sem = self.es.enter_context(self.nc.semaphore("ds%d" % len(self.slots)))
            t.slot = [sem, 0]
            self.slots.append(t.slot)
        return t.slot

    def _deps(self, reads, writes, extra):
        toks = list(extra)
        for t in reads:
            if t.lw is not None:
                toks.append(t.lw)
        for t in writes:
            if t.lw is not None:
                toks.append(t.lw)
            toks.extend(t.rd)
        return toks

    def _upd(self, reads, writes, tok):
        for t in reads:
            t.rd.append(tok)
        for t in writes:
            t.lw = tok
            t.rd = []

    def op(self, eng, fn, reads=(), writes=(), extra=()):
        toks = self._deps(reads, writes, extra)
        self.cnt[eng] += 1
        tok = ("e", eng, self.cnt[eng])
        self.ops[eng].append((toks, fn, self.esem[eng], 1))
        self._upd(reads, writes, tok)
        return tok

    def dma(self, q, fn, reads=(), writes=(), slot_t=None, extra=()):
        toks = self._deps(reads, writes, extra)
        st = slot_t if slot_t is not None else (writes[0] if writes else reads[0])
        slot = self.slot_of(st)
        slot[1] += 16
        tok = ("s", slot[0], slot[1])
        self.ops[q].append((toks, fn, slot[0], 16))
        self._upd(reads, writes, tok)
        return tok

    def barrier(self, final=False):
        toks = [("e", e, self.cnt[e]) for e in ENGS if self.cnt[e] > 0]
        toks += [("s", s[0], s[1]) for s in self.slots if s[1] > 0 and (final or id(s) not in self.nobar)]
        for e in ENGS:
            self.ops[e].append((list(toks), None, None, 0))

    def emit(self):
        nc = self.nc
        self.barrier(final=True)
        with nc.Block() as blk:
            def replay(engname, e):
                waited = {}
                for toks, fn, sem, inc in self.ops[engname]:
                    for tk in toks:
                        if tk[0] == "e":
                            if tk[1] == engname and (engname == "pe" or not SAME_ENG_SYNC):
                                continue
                            s, v = self.esem[tk[1]], tk[2]
                        else:
                            s, v = tk[1], tk[2]
                        key = id(s)
                        if waited.get(key, 0) >= v:
                            continue
                        waited[key] = v
                        e.wait_ge(s, v)
                    if fn is not None:
                        fn(e).then_inc(sem, inc)

            @blk.tensor
            def _(e):
                replay("pe", e)

            @blk.scalar
            def _(e):
                replay("act", e)

            @blk.vector
            def _(e):
                replay("dve", e)

            @blk.gpsimd
            def _(e):
                replay("pool", e)

            @blk.sync
            def _(e):
                replay("sp", e)


class Buf:
    __slots__ = ("ap", "t")

    def __init__(self, ap):
        self.ap = ap
        self.t = T()


class Arena:
    def __init__(self, ap_f32, nfl):
        self.base = ap_f32
        self.n = nfl
        self.off = 0

    def mark(self):
        return self.off

    def release(self, m):
        self.off = m

    def alloc(self, shape, dt=F32):
        ne = int(np.prod(shape))
        esz = {F32: 4, BF16: 2, I32: 4, U32: 4}[dt]
        nfl = (ne * esz + 3) // 4
        nfl = (nfl + 1) // 2 * 2
        assert self.off + nfl <= self.n, "arena overflow: need %d have %d" % (self.off + nfl, self.n)
        v = self.base[:, self.off:self.off + nfl]
        self.off += nfl
        if dt != F32:
            v = v.bitcast(dt)
        v = v[:, 0:ne]
        if len(shape) == 2:
            v = v.rearrange("p (a b) -> p a b", a=shape[0])
        elif len(shape) == 3:
            v = v.rearrange("p (a b c) -> p a b c", a=shape[0], b=shape[1])
        return Buf(v)


class Cfg:
    def __init__(self, seq, past, nk):
        self.SEQ = seq
        self.PAST = past
        self.NK = nk
        self.NE = nk * nk
        self.NT = seq // 128
        assert self.NT % 8 == 0
        self.NOWN = self.NT // 4
        self.NPT = past // 128
        assert past % 128 == 0


def own_tiles(cfg, j):
    out = []
    for m in range(cfg.NT // 8):
        out += [8 * m + j, 8 * m + 7 - j]
    return out


def slot_lo(i):
    m, e = divmod(i, 2)
    return 8 * m + (0 if e == 0 else 4)


def build(cfg):
    nc = bass.Bass("TRN2", target_bir_lowering=False)
    SEQ, PAST, NK, NE, NT, NOWN, NPT = cfg.SEQ, cfg.PAST, cfg.NK, cfg.NE, cfg.NT, cfg.NOWN, cfg.NPT
    NROW = NOWN * 128
    es = ExitStack()
    with es:
        def din(n, s, d=F32):
            return nc.dram_tensor(n, list(s), d, kind="ExternalInput").ap()

        def dout(n, s, d=F32):
            return nc.dram_tensor(n, list(s), d, kind="ExternalOutput").ap()

        xb = din("xb", [SEQ, D])
        xo = din("xo", [NROW, D])
        xs = din("xs", [2, 16, D])
        ck = din("ck", [2, PAST, 1024])
        cv = din("cv", [2, PAST, 1024])
        clf = din("clf", [2, PAST, 8])
        norm_mix_g = din("norm_mix_g", [1, D])
        w_in = din("w_in", [D, INW])
        b_forget = din("b_forget", [1, 8])
        q_norm_g = din("q_norm_g", [1, 128])
        k_norm_g = din("k_norm_g", [1, 128])
        v_norm_g = din("v_norm_g", [1, 1024])
        w_spatial = din("w_spatial", [8, 128, 128])
        b_spatial = din("b_spatial", [8, 128])
        w_out_a = din("w_out_a", [1024, D])
        w_out_b = din("w_out_b", [1024, D])
        w_out = din("w_out", [D, D])
        norm_ffn_g = din("norm_ffn_g", [1, D])
        w_peer_q = din("w_peer_q", [D, D])
        peer_q_norm_g = din("peer_q_norm_g", [1, 256])
        sub_keys = din("peer_sub_keys", [8, 2, NK, 128])
        peer_u = din("peer_u", [NE, D])
        peer_v = din("peer_v", [NE, D])
        c_ident = din("c_ident", [128, 128])
        c_tri = din("c_tri", [128, 128])
        c_e127 = din("c_e127", [128, 128])
        c_triblk = din("c_triblk", [128, 128])
        c_iota = din("c_iota", [128, 256])
        c_sel = din("c_sel", [1, NOWN * NT])
        c_mk = din("c_mk", [128, NOWN * 4 * 128])
        c_pen = din("c_pen", [1, NOWN * 4])

        y_o = dout("y", [NROW, D])
        ys_o = dout("ys", [2, 16, D])
        ko_o = dout("ko", [NROW, 1024])
        vo_o = dout("vo", [NROW, 1024])
        lfo_o = dout("lfo", [NROW, 8])
        kso_o = dout("kso", [2, 16, 1024])
        vso_o = dout("vso", [2, 16, 1024])
        lfs_o = dout("lfs", [2, 16, 8])
        sgv_o = dout("sgv", [2, 16, 1024])

        KTs = nc.dram_tensor("KTs", [H, 128, SEQ], BF16, kind="Internal").ap()
        Vs = nc.dram_tensor("Vs", [H, SEQ, 128], BF16, kind="Internal").ap()
        OAs = nc.dram_tensor("OAs", [NROW + 128, 1024], BF16, kind="Internal").ap()
        UVb = nc.dram_tensor("UVb", [NE, 2 * D], BF16, kind="Internal").ap()
        Ub = UVb[:, 0:D]
        Vb = UVb[:, D:2 * D]
        t_conv = T()
        Wc_b = nc.dram_tensor("Wc_b", [D, INW - O_U], BF16, kind="Internal").ap()
        Woa_b = nc.dram_tensor("Woa_b", [1024, D], BF16, kind="Internal").ap()
        Wob_b = nc.dram_tensor("Wob_b", [1024, D], BF16, kind="Internal").ap()
        Wo_b = nc.dram_tensor("Wo_b", [D, D], BF16, kind="Internal").ap()
        Wpq_b = nc.dram_tensor("Wpq_b", [D, D], BF16, kind="Internal").ap()
        t_wconv = T()
        wconv_tok = []

        NFL = 52500
        arena_t = es.enter_context(nc.sbuf_tensor("arena", [128, NFL], F32))
        A = Arena(arena_t[:, :], NFL)
        pacc = es.enter_context(nc.psum_tensor("pacc", [128, 2048], F32))
        pw = es.enter_context(nc.psum_tensor("pw", [128, 2048], F32))
        PB = [Buf(pacc[:, b * 512:(b + 1) * 512]) for b in range(4)] + [Buf(pw[:, b * 512:(b + 1) * 512]) for b in range(4)]
        P = Prog(nc, es)

        def ld(q, dst, src_ap, dst_ap=None, extra=()):
            d_ap = dst.ap if dst_ap is None else dst_ap
            return P.dma(q, lambda e, d_ap=d_ap, s=src_ap: e.dma_start(out=d_ap, in_=s), writes=[dst.t], extra=extra)

        def ld_nc(q, dst, src_ap, dst_ap=None):
            d_ap = dst.ap if dst_ap is None else dst_ap
            return P.dma(q, lambda e, d_ap=d_ap, s=src_ap: e.dma_start(out=d_ap, in_=s, allow_slow_non_contiguous=True), writes=[dst.t])

        def st(q, dst_ap, src, src_ap=None, dst_t=None, extra=()):
            s_ap = src.ap if src_ap is None else src_ap
            w = [dst_t] if dst_t is not None else []
            return P.dma(q, lambda e, d=dst_ap, s=s_ap: e.dma_start(out=d, in_=s), reads=[src.t], writes=w, slot_t=src.t, extra=extra)

        def mm(out_ap, out_b, lhsT_ap, lhsT_b, rhs_ap, rhs_b, start, stop):
            return P.op("pe", lambda e, o=out_ap, l=lhsT_ap, r=rhs_ap, s=start, p=stop: e.matmul(out=o, lhsT=l, rhs=r, start=s, stop=p),
                        reads=[lhsT_b.t, rhs_b.t], writes=[out_b.t])

        def tp(out_ap, out_b, in_ap, in_b, ident, n=128):
            return P.op("pe", lambda e, o=out_ap, i=in_ap, idn=ident.ap[0:n, 0:n]: e.transpose(out=o, in_=i, identity=idn),
                        reads=[in_b.t, ident.t], writes=[out_b.t])

        def act(out_ap, in_ap, func, reads, writes, **kw):
            return P.op("act", lambda e, o=out_ap, i=in_ap, f=func, kw=kw: e.activation(out=o, in_=i, func=f, **kw),
                        reads=[b.t for b in reads], writes=[b.t for b in writes])

        def dve(fn, reads, writes):
            return P.op("dve", fn, reads=[b.t for b in reads], writes=[b.t for b in writes])

        def pool(fn, reads, writes):
            return P.op("pool", fn, reads=[b.t for b in reads], writes=[b.t for b in writes])

        def tt(eng, out_ap, in0, in1, op, reads, writes):
            return P.op(eng, lambda e, o=out_ap, a=in0, b=in1, op=op: e.tensor_tensor(out=o, in0=a, in1=b, op=op),
                        reads=[b.t for b in reads], writes=[b.t for b in writes])

        def ts(eng, out_ap, in0, s1, s2, op0, op1, reads, writes):
            if op1 is None:
                return P.op(eng, lambda e, o=out_ap, a=in0, s1=s1, op0=op0: e.tensor_scalar(out=o, in0=a, scalar1=s1, scalar2=None, op0=op0),
                            reads=[b.t for b in reads], writes=[b.t for b in writes])
            return P.op(eng, lambda e, o=out_ap, a=in0, s1=s1, s2=s2, op0=op0, op1=op1: e.tensor_scalar(out=o, in0=a, scalar1=s1, scalar2=s2, op0=op0, op1=op1),
                        reads=[b.t for b in reads], writes=[b.t for b in writes])

        def stt(out_ap, in0, scalar, in1, op0, op1, reads, writes, accum=None):
            if accum is None:
                return dve(lambda e, o=out_ap, a=in0, s=scalar, b=in1, op0=op0, op1=op1: e.scalar_tensor_tensor(out=o, in0=a, scalar=s, in1=b, op0=op0, op1=op1), reads, writes)
            return dve(lambda e, o=out_ap, a=in0, s=scalar, b=in1, op0=op0, op1=op1, ac=accum: e.scalar_tensor_tensor(out=o, in0=a, scalar=s, in1=b, op0=op0, op1=op1, accum_out=ac), reads, writes)

        def cp(eng, out_ap, in_ap, reads, writes):
            return P.op(eng, lambda e, o=out_ap, i=in_ap: e.tensor_copy(out=o, in_=i), reads=[b.t for b in reads], writes=[b.t for b in writes])

        def memset(eng, buf, ap, val):
            return P.op(eng, lambda e, a=ap, v=val: e.memset(a, v), writes=[buf.t])

        ident_f = A.alloc([128]); ld("sp", ident_f, c_ident)
        ident_b = A.alloc([128], BF16); ld("pool", ident_b, c_ident)
        tri_f = A.alloc([128]); ld("sp", tri_f, c_tri)
        tri_b = A.alloc([128], BF16); ld("pool", tri_b, c_tri)
        e127 = A.alloc([128]); ld("sp", e127, c_e127)
        triblk = A.alloc([128]); ld("sp", triblk, c_triblk)
        iota = A.alloc([256]); ld("sp", iota, c_iota)
        gmix = A.alloc([D]); ld("sp", gmix, norm_mix_g.partition_broadcast(128))
        gffn = A.alloc([D]); ld("sp", gffn, norm_ffn_g.partition_broadcast(128))
        gq = A.alloc([128]); ld("sp", gq, q_norm_g.partition_broadcast(128))
        gk = A.alloc([128]); ld("sp", gk, k_norm_g.partition_broadcast(128))
        gv = A.alloc([1024]); ld("sp", gv, v_norm_g.partition_broadcast(128))
        bfg = A.alloc([8]); ld("sp", bfg, b_forget.partition_broadcast(128))
        gpq = A.alloc([256]); ld("sp", gpq, peer_q_norm_g.partition_broadcast(128))
        zero1 = A.alloc([2]); memset("dve", zero1, zero1.ap, 0.0)
        m_persist = A.mark()

        def front(rows, nr, xf, hb, hT, junk, stat, gain, pbs, zero_pad=False):
            if zero_pad:
                memset("pool", xf, xf.ap[0:nr, :], 0.0)
            for (src, p0, n) in rows:
                ld("pool", xf, src, xf.ap[p0:p0 + n, :])
            rstd_of(xf, nr, D, junk, stat)
            stt(hb.ap[0:nr, :], xf.ap[0:nr, :], stat.ap[0:nr, 2:3], gain.ap[0:nr, :], ALU.mult, ALU.mult, [xf, stat, gain], [hb])
            transpose16(hb, nr, hT, pbs)

        def rstd_of(xf, nr, n, junk, stat):
            act(junk.ap[0:nr, 0:n], xf.ap[0:nr, 0:n], AF.Square, [xf], [junk, stat], accum_out=stat.ap[0:nr, 0:1])
            act(stat.ap[0:nr, 1:2], stat.ap[0:nr, 0:1], AF.Ln, [stat, eps_b], [stat], scale=1.0 / n, bias=eps_b.ap[0:nr, 0:1])
            act(stat.ap[0:nr, 2:3], stat.ap[0:nr, 1:2], AF.Exp, [stat], [stat], scale=-0.5)

        def transpose16(hb, nr, hT, pbs, nchunk=KC):
            for half in range((nchunk + 7) // 8):
                pb = pbs[half % len(pbs)]
                pv = pb.ap.bitcast(BF16)
                n8 = min(8, nchunk - half * 8)
                for c in range(n8):
                    kc = half * 8 + c
                    tp(pv[:, c * 128:c * 128 + nr], pb, hb.ap[0:nr, kc * 128:(kc + 1) * 128], hb, ident_b, nr)
                src = pv[:, 0:n8 * 128].rearrange("p (a b) -> p a b", a=n8)[:, :, 0:nr]
                dst = hT.ap[:, half * 8:half * 8 + n8, 0:nr]
                if half % 2 == 0:
                    cp("dve", dst, src, [pb], [hT])
                else:
                    act(dst, src, AF.Copy, [pb], [hT])

        eps_b = A.alloc([2]); memset("dve", eps_b, eps_b.ap, RMS_EPS)
        one_b = A.alloc([2]); memset("dve", one_b, one_b.ap, 1.0)
        m_persist = A.mark()

        def load_w(wt, w_dram, k_rows, c0, ncols, wap=None):
            kc = k_rows // 128
            src = w_dram[:, c0:c0 + ncols].rearrange("(kc p) n -> p kc n", p=128)
            dst = wt.ap[:, 0:kc, 0:ncols] if wap is None else wap
            return ld("pool", wt, src, dst)

        def load_wb(wt, w_bf, k_rows, c0, ncols, wap=None):
            kc = k_rows // 128
            src = w_bf[:, c0:c0 + ncols].rearrange("(kc p) n -> p kc n", p=128)
            dst = wt.ap[:, 0:kc, 0:ncols] if wap is None else wap
            return ld("sp", wt, src, dst, extra=wconv_tok[-1:])

        m_ab = A.mark()
        NFK = A.alloc([NT, 8])
        CE = A.alloc([NT, 8])
        mkb = A.alloc([NOWN * 4, 128], BF16); ld("pool", mkb, c_mk, mkb.ap.rearrange("p a b -> p (a b)"))
        CEsel = A.alloc([NOWN, 8])
        QT = A.alloc([H, NROW], BF16)
        QTs = A.alloc([H, 48], BF16)
        kTs_new = A.alloc([H, 48], BF16)
        vb_new = A.alloc([H, 132], BF16)
        nl_new = A.alloc([8])
        m_wa = A.mark()
        WA = A.alloc([KC, 2056], BF16)
        load_w(WA, w_in, D, O_K, 2056)

        xf1 = A.alloc([D])
        xf2 = [xf1, xf1]
        hb = A.alloc([D], BF16)
        hT2 = [A.alloc([KC, 128], BF16) for _ in range(2)]
        stat = A.alloc([4])
        k_sb = A.alloc([1024])
        tmpf = A.alloc([1024])
        junk = Buf(tmpf.ap.bitcast(BF16)); junk.t = tmpf.t
        kn = A.alloc([1024])
        v_sb = A.alloc([1024])
        kT_st = [A.alloc([H, 128], BF16) for _ in range(2)]
        vb_st = [A.alloc([1024], BF16) for _ in range(2)]
        ss8 = A.alloc([8]); r8 = A.alloc([8]); l8 = A.alloc([8])
        f_sb = A.alloc([8]); lf_sb = A.alloc([8]); fc_sb = A.alloc([8]); ce_prev = A.alloc([8])
        memset("dve", ce_prev, ce_prev.ap, 0.0)
        selb = A.alloc([NOWN, NT]); ld("sp", selb, c_sel.partition_broadcast(128), selb.ap.rearrange("p a b -> p (a b)"))

        def qknorm(src, nr, gain, dst, scr):
            s3 = src.ap[0:nr, :].rearrange("p (h d) -> p h d", h=H)
            tt("dve", scr.ap[0:nr, :], src.ap[0:nr, :], src.ap[0:nr, :], ALU.mult, [src], [scr])
            dve(lambda e, o=ss8.ap[0:nr, :], i=scr.ap[0:nr, :].rearrange("p (h d) -> p h d", h=H): e.tensor_reduce(out=o, in_=i, axis=AX.X, op=ALU.add), [scr], [ss8])
            act(l8.ap[0:nr, :], ss8.ap[0:nr, :], AF.Ln, [ss8, eps_b], [l8], scale=1.0 / HD, bias=eps_b.ap[0:nr, 0:1])
            act(r8.ap[0:nr, :], l8.ap[0:nr, :], AF.Exp, [l8], [r8], scale=-0.5)
            tt("dve", scr.ap[0:nr, :].rearrange("p (h d) -> p h d", h=H), s3, r8.ap[0:nr, :].unsqueeze(2).to_broadcast([nr, H, HD]), ALU.mult, [src, r8], [scr])
            tt("dve", dst.ap[0:nr, :].rearrange("p (h d) -> p h d", h=H), scr.ap[0:nr, :].rearrange("p (h d) -> p h d", h=H),
               gain.ap[0:nr, :].unsqueeze(1).to_broadcast([nr, H, HD]), ALU.mult, [scr, gain], [dst])

        def logf_of(nr):
            tt("dve", f_sb.ap[0:nr, :], f_sb.ap[0:nr, :], bfg.ap[0:nr, :], ALU.add, [f_sb, bfg], [f_sb])
            act(lf_sb.ap[0:nr, :], f_sb.ap[0:nr, :], AF.Exp, [f_sb], [lf_sb], scale=-1.0)
            ts("dve", lf_sb.ap[0:nr, :], lf_sb.ap[0:nr, :], 1.0, None, ALU.add, None, [lf_sb], [lf_sb])
            act(lf_sb.ap[0:nr, :], lf_sb.ap[0:nr, :], AF.Ln, [lf_sb], [lf_sb])
            ts("dve", lf_sb.ap[0:nr, :], lf_sb.ap[0:nr, :], -1.0, None, ALU.mult, None, [lf_sb], [lf_sb])

        kn2 = [kn, A.alloc([1024])]
        lf2 = [lf_sb, A.alloc([8])]

        def logf_of2(nr, lf):
            tt("dve", f_sb.ap[0:nr, :], f_sb.ap[0:nr, :], bfg.ap[0:nr, :], ALU.add, [f_sb, bfg], [f_sb])
            act(lf.ap[0:nr, :], f_sb.ap[0:nr, :], AF.Exp, [f_sb], [lf], scale=-1.0)
            ts("dve", lf.ap[0:nr, :], lf.ap[0:nr, :], 1.0, None, ALU.add, None, [lf], [lf])
            act(lf.ap[0:nr, :], lf.ap[0:nr, :], AF.Ln, [lf], [lf])
            ts("dve", lf.ap[0:nr, :], lf.ap[0:nr, :], -1.0, None, ALU.mult, None, [lf], [lf])

        def kvf_tile(it, rows, nr, owned, out_row0=None, sample=False):
            xf = xf2[it % 2]; hT = hT2[it % 2]
            knc = kn2[it % 2]; lf = lf2[it % 2]
            front(rows, nr, xf, hb, hT, junk, stat, gmix, [PB[4], PB[5]], zero_pad=sample)
            for blk in range(4):
                for kc in range(KC):
                    mm(PB[blk].ap[0:nr, :], PB[blk], hT.ap[:, kc, 0:nr], hT, WA.ap[:, kc, blk * 512:(blk + 1) * 512], WA, kc == 0, kc == KC - 1)
            for kc in range(KC):
                mm(PB[6].ap[0:nr, 0:8], PB[6], hT.ap[:, kc, 0:nr], hT, WA.ap[:, kc, 2048:2056], WA, kc == 0, kc == KC - 1)
            act(k_sb.ap[0:nr, 0:512], PB[0].ap[0:nr, :], AF.Copy, [PB[0]], [k_sb])
            act(k_sb.ap[0:nr, 512:1024], PB[1].ap[0:nr, :], AF.Copy, [PB[1]], [k_sb])
            cp("dve", f_sb.ap[0:nr, :], PB[6].ap[0:nr, 0:8], [PB[6]], [f_sb])
            kT = kT_st[it % 2]; vb = vb_st[it % 2]
            if owned:
                act(v_sb.ap[0:nr, 0:512], PB[2].ap[0:nr, :], AF.Copy, [PB[2]], [v_sb])
                act(v_sb.ap[0:nr, 512:1024], PB[3].ap[0:nr, :], AF.Copy, [PB[3]], [v_sb])
            else:
                act(vb.ap[:, 0:512], PB[2].ap[:, :], AF.Copy, [PB[2]], [vb])
                act(vb.ap[:, 512:1024], PB[3].ap[:, :], AF.Copy, [PB[3]], [vb])
            qknorm(k_sb, nr, gk, knc, tmpf)
            logf_of2(nr, lf)
            t0 = it * 128
            if owned:
                if not sample:
                    st("sp", ko_o[out_row0:out_row0 + 128, :], knc)
                    st("sp", vo_o[out_row0:out_row0 + 128, :], v_sb)
                    st("sp", lfo_o[out_row0:out_row0 + 128, :], lf)
                else:
                    for s in range(2):
                        st("sp", kso_o[s], knc, knc.ap[32 * s:32 * s + 16, :])
                        st("sp", vso_o[s], v_sb, v_sb.ap[32 * s:32 * s + 16, :])
                        st("sp", lfs_o[s], lf, lf.ap[32 * s:32 * s + 16, :])
                    memset("pool", vb_new, vb_new.ap[0:48, :, 128:129], 1.0)
                    cp("dve", vb_new.ap[0:48, :, 0:128], v_sb.ap[0:48, :].rearrange("p (h d) -> p h d", h=H), [v_sb], [vb_new])
            else:
                st("sp", Vs[:, t0:t0 + 128, :].rearrange("h t d -> t h d"), vb, vb.ap.rearrange("p (h d) -> p h d", h=H))

            def stage2():
                if owned:
                    if sample:
                        for h in range(H):
                            tp(PB[7].ap[:, h * 48:h * 48 + 48], PB[7], knc.ap[0:48, h * 128:(h + 1) * 128], knc, ident_f, 48)
                        cp("dve", kTs_new.ap.rearrange("p h t -> p (h t)"), PB[7].ap[:, 0:H * 48], [PB[7]], [kTs_new])
                        mm(PB[6].ap[0:48, 8:16], PB[6], triblk.ap[0:48, 0:48], triblk, lf.ap[0:48, :], lf, True, True)
                        ts("dve", nl_new.ap[0:48, :], PB[6].ap[0:48, 8:16], -1.0, None, ALU.mult, None, [PB[6]], [nl_new])
                    return
                for half in range(2):
                    for c in range(4):
                        h = half * 4 + c
                        tp(PB[7].ap[:, c * 128:(c + 1) * 128], PB[7], knc.ap[:, h * 128:(h + 1) * 128], knc, ident_f)
                    cp("dve", kT.ap[:, half * 4:half * 4 + 4, :].rearrange("p h t -> p (h t)"), PB[7].ap[:, :], [PB[7]], [kT])
                st("sp", KTs[:, :, t0:t0 + 128].rearrange("h d t -> d h t"), kT)
                mm(PB[6].ap[:, 8:16], PB[6], tri_f.ap, tri_f, lf.ap, lf, True, False)
                mm(PB[6].ap[:, 8:16], PB[6], e127.ap, e127, ce_prev.ap, ce_prev, False, True)
                cp("dve", fc_sb.ap, PB[6].ap[:, 8:16], [PB[6]], [fc_sb])
                ts("dve", NFK.ap[:, it, :], fc_sb.ap, -1.0, None, ALU.mult, None, [fc_sb], [NFK])
                mm(PB[6].ap[:, 16:24], PB[6], e127.ap, e127, fc_sb.ap, fc_sb, True, True)
                cp("dve", CE.ap[:, it, :], PB[6].ap[:, 16:24], [PB[6]], [CE])
                cp("dve", ce_prev.ap, fc_sb.ap, [fc_sb], [ce_prev])
            return stage2

        conv_tok = []
        P.nobar.add(id(P.slot_of(t_conv)))
        CR = NE // NT
        prev2 = None
        ce_toks = []
        for t in range(NT):
            s2 = kvf_tile(t, [(xb[t * 128:(t + 1) * 128, :], 0, 128)], 128, owned=False)
            if prev2 is not None:
                prev2()
            prev2 = s2
            for (dst_, src_) in ((Ub, peer_u), (Vb, peer_v)):
                conv_tok.append(P.dma("pool", lambda e, d=dst_[t * CR:(t + 1) * CR, :], s_=src_[t * CR:(t + 1) * CR, :]: e.dma_start(out=d, in_=s_), slot_t=t_conv,
                                      extra=[ce_toks[-3]] if len(ce_toks) >= 3 else []))
            if CE.t.lw is not None:
                ce_toks.append(CE.t.lw)
        prev2()
        selscr = A.alloc([8, NT])
        for i in range(NOWN):
            tt("dve", selscr.ap, CE.ap.rearrange("p t h -> p h t"), selb.ap[:, i, :].unsqueeze(1).to_broadcast([128, 8, NT]), ALU.mult, [CE, selb], [selscr])
            dve(lambda e, o=CEsel.ap[:, i, :], i_=selscr.ap: e.tensor_reduce(out=o, in_=i_, axis=AX.X, op=ALU.add), [selscr], [CEsel])
        for i in range(NOWN):
            kvf_tile(i, [(xo[i * 128:(i + 1) * 128, :], 0, 128)], 128, owned=True, out_row0=i * 128)()
        kvf_tile(NOWN, [(xs[0], 0, 16), (xs[1], 32, 16)], 48, owned=True, sample=True)()
        P.barrier()
        load_w(WA, w_in, D, O_Q, 1024, WA.ap[:, :, 0:1024])
        for i in range(NOWN + 1):
            sample = (i == NOWN)
            nr = 48 if sample else 128
            rows = [(xs[0], 0, 16), (xs[1], 32, 16)] if sample else [(xo[i * 128:(i + 1) * 128, :], 0, 128)]
            xf = xf2[i % 2]; hT = hT2[i % 2]
            front(rows, nr, xf, hb, hT, junk, stat, gmix, [PB[4], PB[5]], zero_pad=sample)
            for blk in range(2):
                for kc in range(KC):
                    mm(PB[blk].ap[0:nr, :], PB[blk], hT.ap[:, kc, 0:nr], hT, WA.ap[:, kc, blk * 512:(blk + 1) * 512], WA, kc == 0, kc == KC - 1)
            act(k_sb.ap[0:nr, 0:512], PB[0].ap[0:nr, :], AF.Copy, [PB[0]], [k_sb])
            act(k_sb.ap[0:nr, 512:1024], PB[1].ap[0:nr, :], AF.Copy, [PB[1]], [k_sb])
            qknorm(k_sb, nr, gq, kn, tmpf)
            if sample:
                for h in range(H):
                    tp(PB[7].ap[:, h * 48:h * 48 + 48], PB[7], kn.ap[0:48, h * 128:(h + 1) * 128], kn, ident_f, 48)
                cp("dve", QTs.ap.rearrange("p h t -> p (h t)"), PB[7].ap[:, 0:H * 48], [PB[7]], [QTs])
            else:
                for half in range(2):
                    pb = PB[6 + half]
                    for c in range(4):
                        h = half * 4 + c
                        tp(pb.ap[:, c * 128:(c + 1) * 128], pb, kn.ap[:, h * 128:(h + 1) * 128], kn, ident_f)
                    dst = QT.ap[:, half * 4:half * 4 + 4, i * 128:(i + 1) * 128]
                    src = pb.ap.rearrange("p (h t) -> p h t", h=4)
                    if half == 0:
                        cp("dve", dst, src, [pb], [QT])
                    else:
                        act(dst, src, AF.Copy, [pb], [QT])
        P.barrier()
        A.release(m_wa)
        P.nobar.add(id(P.slot_of(t_wconv)))
        for (dst_, src_, rows_) in ((Wc_b, w_in[:, O_U:INW], D), (Woa_b, w_out_a, 1024), (Wob_b, w_out_b, 1024), (Wo_b, w_out, D), (Wpq_b, w_peer_q, D)):
            for r0_ in range(0, rows_, 256):
                wconv_tok.append(P.dma("pool", lambda e, d=dst_[r0_:r0_ + 256, :], s_=src_[r0_:r0_ + 256, :]: e.dma_start(out=d, in_=s_), slot_t=t_wconv))
        KTh = [A.alloc([SEQ], BF16) for _ in range(2)]
        Vh = [A.alloc([NT, 130], BF16) for _ in range(2)]
        for b in Vh:
            memset("pool", b, b.ap[:, :, 128:129], 1.0)
        BS = min(4, NOWN)
        NBLK = NOWN // BS
        TS = 4 * BS
        PTw = [A.alloc([BS * 128], BF16) for _ in range(4)]
        onesrow = A.alloc([128], BF16)
        memset("dve", onesrow, onesrow.ap, 0.0)
        memset("dve", onesrow, onesrow.ap[0:1, :], 1.0)
        drow2 = [A.alloc([BS * 128], BF16) for _ in range(2)]
        for d_ in drow2:
            memset("dve", d_, d_.ap, 0.0)
        dsl = A.alloc([BS])
        penb = A.alloc([NOWN * 4]); ld("sp", penb, c_pen.partition_broadcast(128))
        bb2 = [A.alloc([NT]) for _ in range(2)]
        bu2 = [A.alloc([4]) for _ in range(2)]
        o_st = [A.alloc([128], BF16) for _ in range(2)]
        rden = A.alloc([2])
        ucount = 0
        units = []

        def mk_bulk(h, n, t, pre):
            kth = KTh[h % 2]; vh = Vh[h % 2]
            s0 = BS * n
            drow = drow2[(h * NBLK + n) % 2]; bb = bb2[(h * NBLK + n) % 2]
            f = min(ii for ii in range(BS) if slot_lo(s0 + ii) > t)
            c0 = f * 128; Wd = BS * 128 - c0
            return dict(h=h, n=n, kind="bulk", t=t, f=f, c0=c0, Wd=Wd, pre=pre)

        for h in range(H):
            for n in range(NBLK):
                s0 = BS * n
                ntb = slot_lo(s0 + BS - 1)
                first = True
                for t in range(ntb):
                    u = mk_bulk(h, n, t, first); first = False
                    units.append(u)
                for ii in range(BS):
                    for r in range(4):
                        units.append(dict(h=h, n=n, kind="diag", ii=ii, r=r, pre=first, pre_slot=(r == 0)))
                        first = False
        started = {}

        def stage1(u, idx):
            h, n = u["h"], u["n"]
            kth = KTh[h % 2]; vh = Vh[h % 2]
            s0 = BS * n; tR = TS * (n + 1) - 1
            drow = drow2[(h * NBLK + n) % 2]; bb = bb2[(h * NBLK + n) % 2]
            if u["pre"]:
                if n == 0:
                    ld("sp", kth, KTs[h])
                    ld("sp", vh, Vs[h].rearrange("(t p) d -> p t d", p=128), vh.ap[:, :, 0:128])
                tt("dve", dsl.ap[0:1, :], CEsel.ap[0:1, s0:s0 + BS, h], CE.ap[0:1, tR, h:h + 1].to_broadcast([1, BS]), ALU.subtract, [CEsel, CE], [dsl])
                ts("dve", drow.ap[0:1, :].rearrange("p (a b) -> p a b", a=BS), dsl.ap[0:1, :].unsqueeze(2).to_broadcast([1, BS, 128]), 1.0 / SCALE, None, ALU.mult, None, [dsl], [drow])
                ntb = slot_lo(s0 + BS - 1)
                if ntb > 0:
                    ts("dve", bb.ap[:, 0:ntb], NFK.ap[:, 0:ntb, h], CE.ap[:, tR, h:h + 1], None, ALU.add, None, [NFK, CE], [bb])
            psb = PB[4 + idx % 4]; pt = PTw[idx % 4]
            if u["kind"] == "bulk":
                t, f, c0, Wd = u["t"], u["f"], u["c0"], u["Wd"]
                mm(psb.ap[:, c0:c0 + Wd], psb, kth.ap[:, t * 128:(t + 1) * 128], kth, QT.ap[:, h, (s0 + f) * 128:(s0 + BS) * 128], QT, True, False)
                mm(psb.ap[:, c0:c0 + Wd], psb, onesrow.ap, onesrow, drow.ap[:, c0:c0 + Wd], drow, False, True)
                act(pt.ap[:, c0:c0 + Wd], psb.ap[:, c0:c0 + Wd], AF.Exp, [psb, bb], [pt], scale=SCALE, bias=bb.ap[:, t:t + 1])
            else:
                ii, r = u["ii"], u["r"]
                i = s0 + ii; lo = slot_lo(i); t = lo + r
                bu = bu2[(h * NOWN + i) % 2]
                if u["pre_slot"]:
                    tt("dve", bu.ap, NFK.ap[:, lo:lo + 4, h], penb.ap[:, i * 4:i * 4 + 4], ALU.add, [NFK, penb], [bu])
                    ts("dve", bu.ap, bu.ap, CE.ap[:, tR, h:h + 1], None, ALU.add, None, [bu, CE], [bu])
                mm(psb.ap[:, 0:128], psb, kth.ap[:, t * 128:(t + 1) * 128], kth, QT.ap[:, h, i * 128:(i + 1) * 128], QT, True, False)
                mm(psb.ap[:, 0:128], psb, onesrow.ap, onesrow, drow.ap[:, ii * 128:(ii + 1) * 128], drow, False, True)
                act(pt.ap[:, 0:128], psb.ap[:, 0:128], AF.Exp, [psb, bu], [pt], scale=SCALE, bias=bu.ap[:, r:r + 1])
                tt("pool", pt.ap[:, 0:128], pt.ap[:, 0:128], mkb.ap[:, i * 4 + r, :], ALU.mult, [pt, mkb], [pt])

        def stage2(u, idx):
            h, n = u["h"], u["n"]
            vh = Vh[h % 2]
            s0 = BS * n
            pt = PTw[idx % 4]
            if u["kind"] == "bulk":
                t, f = u["t"], u["f"]
                for ii in range(f, BS):
                    key = (h, n, ii)
                    mm(PB[ii].ap[:, 0:129], PB[ii], pt.ap[:, ii * 128:(ii + 1) * 128], pt, vh.ap[:, t, 0:129], vh, key not in started, False)
                    started[key] = True
            else:
                ii, r = u["ii"], u["r"]
                i = s0 + ii; lo = slot_lo(i); t = lo + r
                key = (h, n, ii)
                mm(PB[ii].ap[:, 0:129], PB[ii], pt.ap[:, 0:128], pt, vh.ap[:, t, 0:129], vh, key not in started, r == 3)
                started[key] = True
                if r == 3:
                    po = PB[ii]
                    ob = o_st[(h * NOWN + i) % 2]
                    dve(lambda e, o=rden.ap[:, 0:1], i_=po.ap[:, 128:129]: e.reciprocal(out=o, in_=i_), [po], [rden])
                    ts("dve", ob.ap, po.ap[:, 0:128], rden.ap[:, 0:1], None, ALU.mult, None, [po, rden], [ob])
                    st("sp", OAs[i * 128:(i + 1) * 128, h * 128:(h + 1) * 128], ob)

        SKEW = 2
        for idx in range(len(units) + SKEW):
            if idx < len(units):
                stage1(units[idx], idx)
            if idx - SKEW >= 0:
                stage2(units[idx - SKEW], idx - SKEW)
        P.barrier()
        A.release(m_wa)
        ckf = [A.alloc([NPT, 128]) for _ in range(2)]
        KTc = [A.alloc([NPT * 128], BF16) for _ in range(2)]
        Vc = [A.alloc([NPT, 130], BF16) for _ in range(2)]
        for b in Vc:
            memset("pool", b, b.ap[:, :, 128:129], 1.0)
        clf_sb = A.alloc([2, NPT, 8])
        nfk_c = A.alloc([2, NPT, 8])
        bias_c = A.alloc([2, NPT, 8])
        ce_c = A.alloc([8]); fcc = A.alloc([8]); cend = A.alloc([8])
        PTs = [A.alloc([16], BF16) for _ in range(4)]
        oas_st = A.alloc([1024], BF16)
        memset("pool", oas_st, oas_st.ap, 0.0)
        rden2 = A.alloc([2])
        for s in range(2):
            ld("sp", clf_sb, clf[s].rearrange("(t p) h -> p t h", p=128), clf_sb.ap[:, s, :, :])
            memset("dve", ce_c, ce_c.ap, 0.0)
            for t in range(NPT):
                mm(PB[1].ap[:, 0:8], PB[1], tri_f.ap, tri_f, clf_sb.ap[:, s, t, :], clf_sb, True, False)
                mm(PB[1].ap[:, 0:8], PB[1], e127.ap, e127, ce_c.ap, ce_c, False, True)
                cp("dve", fcc.ap, PB[1].ap[:, 0:8], [PB[1]], [fcc])
                ts("dve", nfk_c.ap[:, s, t, :], fcc.ap, -1.0, None, ALU.mult, None, [fcc], [nfk_c])
                cp("dve", ce_c.ap, fcc.ap, [fcc], [ce_c])
            mm(PB[1].ap[:, 8:16], PB[1], e127.ap, e127, fcc.ap, fcc, True, True)
            cp("dve", cend.ap, PB[1].ap[:, 8:16], [PB[1]], [cend])
            tt("dve", bias_c.ap[:, s, :, :], nfk_c.ap[:, s, :, :], cend.ap.unsqueeze(1).to_broadcast([128, NPT, 8]), ALU.add, [nfk_c, cend], [bias_c])
        ucount = 0
        for s in range(2):
            r0 = 32 * s
            for h in range(H):
                u = s * H + h
                cf = ckf[u % 2]; ktc = KTc[u % 2]; vc = Vc[u % 2]
                ld("sp", cf, ck[s].rearrange("(t p) (h d) -> p t h d", p=128, h=H)[:, :, h, :])
                ld("pool", vc, cv[s].rearrange("(t p) (h d) -> p t h d", p=128, h=H)[:, :, h, :], vc.ap[:, :, 0:128])
                for g4 in range((NPT + 3) // 4):
                    pb = PB[2 + g4 % 2]
                    n4 = min(4, NPT - g4 * 4)
                    for c in range(n4):
                        t = g4 * 4 + c
                        tp(pb.ap[:, c * 128:(c + 1) * 128], pb, cf.ap[:, t, :], cf, ident_f)
                    if g4 % 2 == 0:
                        cp("dve", ktc.ap[:, g4 * 512:g4 * 512 + n4 * 128], pb.ap[:, 0:n4 * 128], [pb], [ktc])
                    else:
                        act(ktc.ap[:, g4 * 512:g4 * 512 + n4 * 128], pb.ap[:, 0:n4 * 128], AF.Copy, [pb], [ktc])
                po = PB[0]
                qT = QTs.ap[:, h, r0:r0 + 16]
                for t in range(NPT):
                    psb = PB[4 + ucount % 4]; pt = PTs[ucount % 4]; ucount += 1
                    mm(psb.ap[:, 0:16], psb, ktc.ap[:, t * 128:(t + 1) * 128], ktc, qT, QTs, True, True)
                    act(pt.ap, psb.ap[:, 0:16], AF.Exp, [psb, bias_c], [pt], scale=SCALE, bias=bias_c.ap[:, s, t, h:h + 1])
                    mm(po.ap[r0:r0 + 16, 0:129], po, pt.ap, pt, vc.ap[:, t, 0:129], vc, t == 0, False)
                psb = PB[4 + ucount % 4]; pt = PTs[ucount % 4]; ucount += 1
                mm(psb.ap[r0:r0 + 16, 0:16], psb, kTs_new.ap[:, h, r0:r0 + 16], kTs_new, qT, QTs, True, True)
                act(pt.ap[r0:r0 + 16, :], psb.ap[r0:r0 + 16, 0:16], AF.Exp, [psb, nl_new], [pt], scale=SCALE, bias=nl_new.ap[r0:r0 + 16, h:h + 1])
                tt("pool", pt.ap[r0:r0 + 16, :], pt.ap[r0:r0 + 16, :], tri_b.ap[r0:r0 + 16, r0:r0 + 16], ALU.mult, [pt, tri_b], [pt])
                mm(po.ap[r0:r0 + 16, 0:129], po, pt.ap[r0:r0 + 16, :], pt, vb_new.ap[r0:r0 + 16, h, 0:129], vb_new, False, True)
                dve(lambda e, o=rden2.ap[r0:r0 + 16, 0:1], i_=po.ap[r0:r0 + 16, 128:129]: e.reciprocal(out=o, in_=i_), [po], [rden2])
                ts("dve", oas_st.ap[r0:r0 + 16, h * 128:(h + 1) * 128], po.ap[r0:r0 + 16, 0:128], rden2.ap[r0:r0 + 16, 0:1], None, ALU.mult, None, [po, rden2], [oas_st])
        st("sp", OAs[NROW:NROW + 48, :], oas_st, oas_st.ap[0:48, :])
        P.barrier()

        A.release(m_ab)
        trilWT = A.alloc([H, 128], BF16)
        bsT = A.alloc([H])
        WS_f = A.alloc([H, 48])
        WS = A.alloc([H, 48], BF16)
        bsS = A.alloc([H])
        skT = A.alloc([16, NK])
        m_c = A.mark()
        wsp_f = A.alloc([H, 128])
        sk_f = A.alloc([16, 128])
        ld("sp", wsp_f, w_spatial.rearrange("g i j -> i g j"))
        for g in range(H):
            pb = PB[4 + g % 2]
            tp(pb.ap[:, 0:128], pb, wsp_f.ap[:, g, :], wsp_f, ident_f)
            tt("dve", trilWT.ap[:, g, :], pb.ap[:, 0:128], tri_f.ap, ALU.mult, [pb, tri_f], [trilWT])
        ld_nc("sp", bsT, b_spatial.rearrange("g i -> i g"))
        memset("dve", WS_f, WS_f.ap, 0.0)
        for s_ in range(2):
            for g in range(H):
                ld_nc("sp", WS_f, w_spatial[g, 0:16, 0:16].rearrange("i j -> j i"), WS_f.ap[32 * s_:32 * s_ + 16, g, 32 * s_:32 * s_ + 16])
        tt("dve", WS.ap[0:48, :, :], WS_f.ap[0:48, :, :], tri_f.ap[0:48, 0:48].unsqueeze(1).to_broadcast([48, H, 48]), ALU.mult, [WS_f, tri_f], [WS])
        memset("dve", bsS, bsS.ap, 0.0)
        for s_ in range(2):
            ld_nc("sp", bsS, b_spatial[:, 0:16].rearrange("g i -> i g"), bsS.ap[32 * s_:32 * s_ + 16, :])
        if NK < 128:
            memset("dve", sk_f, sk_f.ap, 0.0)
        ld("sp", sk_f, sub_keys.rearrange("h two k d -> k (h two) d"), sk_f.ap[0:NK, :, :])
        for g in range(16):
            pb = PB[4 + g % 2]
            tp(pb.ap[:, 0:128], pb, sk_f.ap[:, g, :], sk_f, ident_f)
            cp("dve", skT.ap[:, g, :], pb.ap[:, 0:NK], [pb], [skT])
        P.barrier()
        A.release(m_c)

        class TB:
            pass
        tbs = []
        for g in range(GSZ):
            b = TB()
            b.xf = A.alloc([D])
            b.hT = A.alloc([KC, 128], BF16)
            r1 = A.alloc([2560])
            b.q = Buf(r1.ap[:, 0:2048]); b.q.t = r1.t
            rb = r1.ap.bitcast(BF16)
            b.y = Buf(rb[:, 0:2048]); b.y.t = r1.t
            b.us = Buf(rb[:, 2048:3072]); b.us.t = r1.t
            b.obT = Buf(rb[:, 3072:4096].rearrange("p (a b) -> p a b", a=8)); b.obT.t = r1.t
            b.oaT = Buf(rb[:, 4096:5120].rearrange("p (a b) -> p a b", a=8)); b.oaT.t = r1.t
            b.stat = A.alloc([4])
            tbs.append(b)
        m_r2 = A.mark()
        Wb = [A.alloc([KC, 512], BF16) for _ in range(3)]
        Wb0lo = Buf(Wb[0].ap[:, 0:8, :]); Wb0lo.t = Wb[0].t
        Wb0hi = Buf(Wb[0].ap[:, 8:16, :]); Wb0hi.t = Wb[0].t
        junkc = A.alloc([D], BF16)
        hbc = A.alloc([D], BF16)
        xs_sb = A.alloc([512]); sq_sb = A.alloc([512]); u_sb = A.alloc([512])
        vg = A.alloc([1024]); vsn = A.alloc([1024]); vsnb = A.alloc([1024], BF16); ob_sb = A.alloc([1024], BF16)
        oa_sb = A.alloc([1024], BF16)
        sga = A.alloc([512]); sgb = A.alloc([512]); y1 = A.alloc([512])
        A.release(m_r2)
        xn2 = [A.alloc([D], BF16) for _ in range(2)]
        qn = A.alloc([D])
        qnT = A.alloc([16, 128]); yout = Buf(qnT.ap.rearrange("p a b -> p (a b)")); yout.t = qnT.t
        sc = A.alloc([16, NK])
        wk2 = [A.alloc([NK]) for _ in range(2)]
        t16 = A.alloc([16, 16])
        i16u = A.alloc([16, 16], U32)
        i16f = A.alloc([16, 16])
        cand2 = [A.alloc([256]) for _ in range(2)]
        cwk2 = [A.alloc([256]) for _ in range(2)]
        st16 = A.alloc([H, 16]); selu = A.alloc([H, 16], U32)
        selA = A.alloc([H, 16], U32); selB = A.alloc([H, 16], U32); aF = A.alloc([H, 16]); bF = A.alloc([H, 16]); ea = A.alloc([H, 16]); eb = A.alloc([H, 16])
        eidf = A.alloc([H * 16]); eidi2 = [A.alloc([H * 16], I32) for _ in range(2)]
        gat2 = [A.alloc([H, 16]) for _ in range(2)]; zs = A.alloc([H]); ss8c = A.alloc([8]); r8c = A.alloc([8]); l8c = A.alloc([8])
        NUV = 6
        acol = [A.alloc([2]) for _ in range(NUV)]; wcol = [A.alloc([2]) for _ in range(NUV)]
        dg = [A.alloc([128], BF16) for _ in range(6)]
        GBUV = [A.alloc([2 * D], BF16) for _ in range(6)]
        gj = A.alloc([D], BF16); junkd = gj
        gcu = [0]; gcv = [0]
        _breg = {}

        def breg(e):
            if "r" not in _breg:
                _breg["r"] = e.to_reg(NE - 1)
            return _breg["r"]

        def gelu_from(pb, nr, dst_ap, dst):
            act(xs_sb.ap[0:nr, :], pb.ap[0:nr, :], AF.Copy, [pb], [xs_sb])
            tt("dve", sq_sb.ap[0:nr, :], xs_sb.ap[0:nr, :], xs_sb.ap[0:nr, :], ALU.mult, [xs_sb], [sq_sb])
            tt("dve", sq_sb.ap[0:nr, :], sq_sb.ap[0:nr, :], xs_sb.ap[0:nr, :], ALU.mult, [sq_sb, xs_sb], [sq_sb])
            stt(u_sb.ap[0:nr, :], sq_sb.ap[0:nr, :], 0.044715, xs_sb.ap[0:nr, :], ALU.mult, ALU.add, [sq_sb, xs_sb], [u_sb])
            act(u_sb.ap[0:nr, :], u_sb.ap[0:nr, :], AF.Sigmoid, [u_sb], [u_sb], scale=2.0 * GELU_C)
            tt("dve", dst_ap, xs_sb.ap[0:nr, :], u_sb.ap[0:nr, :], ALU.mult, [xs_sb, u_sb], [dst])

        def proj(pb, nr, hT, wt, kc_n, ncols=512):
            for kc in range(kc_n):
                mm(pb.ap[0:nr, 0:ncols], pb, hT.ap[:, kc, 0:nr], hT, wt.ap[:, kc, 0:ncols], wt, kc == 0, kc == kc_n - 1)

        def group(tiles):
            ng = len(tiles)
            for g, tl in enumerate(tiles):
                b = tbs[g]
                front(tl["rows"], tl["nr"], b.xf, hbc, b.hT, junkc, b.stat, gmix, [PB[4], PB[5]], zero_pad=tl["sample"])
                ld("sp", oa_sb, OAs[tl["oa_row0"]:tl["oa_row0"] + tl["nr"], :], oa_sb.ap[0:tl["nr"], :])
                transpose16(oa_sb, tl["nr"], b.oaT, [PB[6]], nchunk=8)
            for blk in range(2):
                wt = Wb[blk]
                load_wb(wt, Wc_b, D, blk * 512, 512)
                for g, tl in enumerate(tiles):
                    b = tbs[g]; nr = tl["nr"]; pb = PB[g % 4]
                    proj(pb, nr, b.hT, wt, KC)
                    gelu_from(pb, nr, b.us.ap[0:nr, blk * 512:(blk + 1) * 512], b.us)
            wv = [Wb[2], Wb[0]]
            for blk in range(2):
                load_wb(wv[blk], Wc_b, D, (O_VS - O_U) + blk * 512, 512)
            for g, tl in enumerate(tiles):
                b = tbs[g]; nr = tl["nr"]
                for blk in range(2):
                    pb = PB[blk]
                    proj(pb, nr, b.hT, wv[blk], KC)
                    gelu_from(pb, nr, vg.ap[0:nr, blk * 512:(blk + 1) * 512], vg)
                rstd_of(vg, nr, 1024, junkc, b.stat)
                stt(vsn.ap[0:nr, :], vg.ap[0:nr, :], b.stat.ap[0:nr, 2:3], gv.ap[0:nr, :], ALU.mult, ALU.mult, [vg, b.stat, gv], [vsn])
                if tl["sample"]:
                    for s in range(2):
                        st("sp", sgv_o[s], vsn, vsn.ap[32 * s:32 * s + 16, :])
                cp("pool", vsnb.ap[0:nr, :], vsn.ap[0:nr, :], [vsn], [vsnb])
                for gg in range(H):
                    pb = PB[2 + gg // 4]
                    lhs = WS.ap[0:48, gg, :] if tl["sample"] else trilWT.ap[:, gg, :]
                    lb = WS if tl["sample"] else trilWT
                    mm(pb.ap[0:nr, (gg % 4) * 128:(gg % 4 + 1) * 128], pb, lhs, lb, vsnb.ap[0:nr, gg * 128:(gg + 1) * 128], vsnb, True, True)
                bs_ = bsS if tl["sample"] else bsT
                for gg in range(H):
                    pb = PB[2 + gg // 4]
                    stt(ob_sb.ap[0:nr, gg * 128:(gg + 1) * 128], pb.ap[0:nr, (gg % 4) * 128:(gg % 4 + 1) * 128], bs_.ap[0:nr, gg:gg + 1],
                        b.us.ap[0:nr, gg * 128:(gg + 1) * 128], ALU.add, ALU.mult, [pb, bs_, b.us], [ob_sb])
                transpose16(ob_sb, nr, b.obT, [PB[6]], nchunk=8)
            for blk in range(4):
                load_wb(Wb[0], Woa_b, 1024, blk * 512, 512, Wb[0].ap[:, 0:8, :])
                load_wb(Wb[0], Wob_b, 1024, blk * 512, 512, Wb[0].ap[:, 8:16, :])
                load_wb(Wb[1], Wc_b, D, (O_GA - O_U) + blk * 512, 512)
                load_wb(Wb[2], Wc_b, D, (O_GB - O_U) + blk * 512, 512)
                for g, tl in enumerate(tiles):
                    b = tbs[g]; nr = tl["nr"]
                    proj(PB[0], nr, b.oaT, Wb0lo, 8)
                    proj(PB[1], nr, b.obT, Wb0hi, 8)
                    proj(PB[2], nr, b.hT, Wb[1], KC)
                    proj(PB[3], nr, b.hT, Wb[2], KC)
                    act(sga.ap[0:nr, :], PB[2].ap[0:nr, :], AF.Sigmoid, [PB[2]], [sga])
                    act(sgb.ap[0:nr, :], PB[3].ap[0:nr, :], AF.Sigmoid, [PB[3]], [sgb])
                    tt("dve", y1.ap[0:nr, :], PB[0].ap[0:nr, :], sga.ap[0:nr, :], ALU.mult, [PB[0], sga], [y1])
                    tt("dve", sgb.ap[0:nr, :], PB[1].ap[0:nr, :], sgb.ap[0:nr, :], ALU.mult, [PB[1], sgb], [sgb])
                    tt("dve", b.y.ap[0:nr, blk * 512:(blk + 1) * 512], y1.ap[0:nr, :], sgb.ap[0:nr, :], ALU.add, [y1, sgb], [b.y])
            for g, tl in enumerate(tiles):
                b = tbs[g]
                transpose16(b.y, tl["nr"], b.hT, [PB[4], PB[5]])
            for blk in range(4):
                wt = Wb[blk % 3]
                load_wb(wt, Wo_b, D, blk * 512, 512)
                for g, tl in enumerate(tiles):
                    b = tbs[g]; nr = tl["nr"]; pb = PB[g % 4]
                    proj(pb, nr, b.hT, wt, KC)
                    tt("dve", b.xf.ap[0:nr, blk * 512:(blk + 1) * 512], b.xf.ap[0:nr, blk * 512:(blk + 1) * 512], pb.ap[0:nr, :], ALU.add, [b.xf, pb], [b.xf])
            for g, tl in enumerate(tiles):
                b = tbs[g]; nr = tl["nr"]
                rstd_of(b.xf, nr, D, junkc, b.stat)
                stt(hbc.ap[0:nr, :], b.xf.ap[0:nr, :], b.stat.ap[0:nr, 2:3], gffn.ap[0:nr, :], ALU.mult, ALU.mult, [b.xf, b.stat, gffn], [hbc])
                transpose16(hbc, nr, b.hT, [PB[4], PB[5]])
            for blk in range(4):
                wt = Wb[blk % 3]
                load_wb(wt, Wpq_b, D, blk * 512, 512)
                for g, tl in enumerate(tiles):
                    b = tbs[g]; nr = tl["nr"]; pb = PB[g % 4]
                    proj(pb, nr, b.hT, wt, KC)
                    act(b.q.ap[0:nr, blk * 512:(blk + 1) * 512], pb.ap[0:nr, :], AF.Copy, [pb], [b.q])
            P.barrier()
            def routing(g):
                tl = tiles[g]; b = tbs[g]; nr = tl["nr"]; eidi = eidi2[g % 2]; xn = xn2[g % 2]; gat = gat2[g % 2]
                rstd_of(b.xf, nr, D, junkd, b.stat)
                stt(xn.ap[0:nr, :], b.xf.ap[0:nr, :], b.stat.ap[0:nr, 2:3], gffn.ap[0:nr, :], ALU.mult, ALU.mult, [b.xf, b.stat, gffn], [xn])
                q3 = b.q.ap[0:nr, :].rearrange("p (h d) -> p h d", h=H)
                qn3 = qn.ap[0:nr, :].rearrange("p (h d) -> p h d", h=H)
                tt("dve", qn.ap[0:nr, :], b.q.ap[0:nr, :], b.q.ap[0:nr, :], ALU.mult, [b.q], [qn])
                dve(lambda e, o=ss8c.ap[0:nr, :], i_=qn3: e.tensor_reduce(out=o, in_=i_, axis=AX.X, op=ALU.add), [qn], [ss8c])
                act(l8c.ap[0:nr, :], ss8c.ap[0:nr, :], AF.Ln, [ss8c, eps_b], [l8c], scale=1.0 / 256, bias=eps_b.ap[0:nr, 0:1])
                act(r8c.ap[0:nr, :], l8c.ap[0:nr, :], AF.Exp, [l8c], [r8c], scale=-0.5)
                tt("dve", qn3, q3, r8c.ap[0:nr, :].unsqueeze(2).to_broadcast([nr, H, 256]), ALU.mult, [b.q, r8c], [qn])
                tt("dve", qn3, qn3, gpq.ap[0:nr, :].unsqueeze(1).to_broadcast([nr, H, 256]), ALU.mult, [qn, gpq], [qn])
                yield
                for half in range(4):
                    pb = PB[4 + half % 2]
                    for c in range(4):
                        gg = half * 4 + c
                        tp(pb.ap[:, c * 128:c * 128 + nr], pb, qn.ap[0:nr, gg * 128:(gg + 1) * 128], qn, ident_f, nr)
                    dst = qnT.ap[:, half * 4:half * 4 + 4, 0:nr]
                    src = pb.ap.rearrange("p (a b) -> p a b", a=4)[:, :, 0:nr]
                    if half % 2 == 0:
                        cp("dve", dst, src, [pb], [qnT])
                    else:
                        act(dst, src, AF.Copy, [pb], [qnT])
                    yield
                gpb = 512 // NK
                for gg in range(16):
                    pb = PB[6 + (gg // gpb) % 2]
                    col = (gg % gpb) * NK
                    mm(pb.ap[0:nr, col:col + NK], pb, qnT.ap[:, gg, 0:nr], qnT, skT.ap[:, gg, :], skT, True, True)
                    if gg % gpb == gpb - 1 or gg == 15:
                        g0 = (gg // gpb) * gpb
                        ng_ = gg - g0 + 1
                        cp("dve", sc.ap[0:nr, g0:g0 + ng_, :].rearrange("p a b -> p (a b)"), pb.ap[0:nr, 0:ng_ * NK], [pb], [sc])
                        yield
                for gg in range(16):
                    wk = wk2[gg % 2]
                    dve(lambda e, o=t16.ap[0:nr, gg, 0:8], i_=sc.ap[0:nr, gg, :]: e.max(out=o, in_=i_), [sc], [t16])
                    dve(lambda e, o=i16u.ap[0:nr, gg, 0:8], m=t16.ap[0:nr, gg, 0:8], v=sc.ap[0:nr, gg, :]: e.max_index(out=o, in_max=m, in_values=v), [sc, t16], [i16u])
                    dve(lambda e, o=wk.ap[0:nr, :], m=t16.ap[0:nr, gg, 0:8], v=sc.ap[0:nr, gg, :]: e.match_replace(out=o, in_to_replace=m, in_values=v, imm_value=NEG), [sc, t16], [wk])
                    dve(lambda e, o=t16.ap[0:nr, gg, 8:16], i_=wk.ap[0:nr, :]: e.max(out=o, in_=i_), [wk], [t16])
                    dve(lambda e, o=i16u.ap[0:nr, gg, 8:16], m=t16.ap[0:nr, gg, 8:16], v=wk.ap[0:nr, :]: e.max_index(out=o, in_max=m, in_values=v), [wk, t16], [i16u])
                    yield
                cp("dve", i16f.ap[0:nr, :, :], i16u.ap[0:nr, :, :], [i16u], [i16f])
                t4 = t16.ap[0:nr, :, :].rearrange("p (h two) k -> p h two k", two=2)
                i4 = i16f.ap[0:nr, :, :].rearrange("p (h two) k -> p h two k", two=2)
                for hh in range(H):
                    cand = cand2[hh % 2]; cwk = cwk2[hh % 2]
                    c3 = cand.ap[0:nr, :].rearrange("p (a b) -> p a b", a=16)
                    tt("dve", c3, t4[:, hh, 0, :].unsqueeze(2).to_broadcast([nr, 16, 16]), t4[:, hh, 1, :].unsqueeze(1).to_broadcast([nr, 16, 16]), ALU.add, [t16], [cand])
                    dve(lambda e, o=st16.ap[0:nr, hh, 0:8], i_=cand.ap[0:nr, :]: e.max(out=o, in_=i_), [cand], [st16])
                    dve(lambda e, o=selu.ap[0:nr, hh, 0:8], m=st16.ap[0:nr, hh, 0:8], v=cand.ap[0:nr, :]: e.max_index(out=o, in_max=m, in_values=v), [cand, st16], [selu])
                    dve(lambda e, o=cwk.ap[0:nr, :], m=st16.ap[0:nr, hh, 0:8], v=cand.ap[0:nr, :]: e.match_replace(out=o, in_to_replace=m, in_values=v, imm_value=NEG), [cand, st16], [cwk])
                    dve(lambda e, o=st16.ap[0:nr, hh, 8:16], i_=cwk.ap[0:nr, :]: e.max(out=o, in_=i_), [cwk], [st16])
                    dve(lambda e, o=selu.ap[0:nr, hh, 8:16], m=st16.ap[0:nr, hh, 8:16], v=cwk.ap[0:nr, :]: e.max_index(out=o, in_max=m, in_values=v), [cwk, st16], [selu])
                    yield
                ts("dve", selA.ap[0:nr, :, :], selu.ap[0:nr, :, :], 4, None, ALU.logical_shift_right, None, [selu], [selA])
                ts("dve", selB.ap[0:nr, :, :], selu.ap[0:nr, :, :], 15, None, ALU.bitwise_and, None, [selu], [selB])
                cp("dve", aF.ap[0:nr, :, :], selA.ap[0:nr, :, :], [selA], [aF])
                cp("dve", bF.ap[0:nr, :, :], selB.ap[0:nr, :, :], [selB], [bF])
                yield
                oh4 = qn.ap[0:nr, :].rearrange("p (h r c) -> p h r c", h=H, r=16)
                io4 = iota.ap[0:nr, 0:16].unsqueeze(1).unsqueeze(1).to_broadcast([nr, H, 16, 16])
                for which, (xF, eX) in enumerate(((aF, ea), (bF, eb))):
                    tt("dve", oh4, xF.ap[0:nr, :, :].unsqueeze(3).to_broadcast([nr, H, 16, 16]), io4, ALU.is_equal, [xF, iota], [qn])
                    yield
                    tt("dve", oh4, oh4, i4[:, :, which, :].unsqueeze(2).to_broadcast([nr, H, 16, 16]), ALU.mult, [qn, i16f], [qn])
                    yield
                    dve(lambda e, o=eX.ap[0:nr, :, :], i_=oh4: e.tensor_reduce(out=o, in_=i_, axis=AX.X, op=ALU.add), [qn], [eX])
                    yield
                stt(eidf.ap[0:nr, :], ea.ap[0:nr, :, :].rearrange("p h k -> p (h k)"), float(NK), eb.ap[0:nr, :, :].rearrange("p h k -> p (h k)"), ALU.mult, ALU.add, [ea, eb], [eidf])
                cp("dve", eidi.ap[0:nr, :], eidf.ap[0:nr, :], [eidf], [eidi])
                tt("dve", gat.ap[0:nr, :, :], st16.ap[0:nr, :, :], st16.ap[0:nr, :, 0:1].to_broadcast([nr, H, 16]), ALU.subtract, [st16], [gat])
                act(gat.ap[0:nr, :, :], gat.ap[0:nr, :, :], AF.Exp, [gat], [gat])
                dve(lambda e, o=zs.ap[0:nr, :], i_=gat.ap[0:nr, :, :]: e.tensor_reduce(out=o, in_=i_, axis=AX.X, op=ALU.add), [gat], [zs])
                dve(lambda e, o=zs.ap[0:nr, :], i_=zs.ap[0:nr, :]: e.reciprocal(out=o, in_=i_), [zs], [zs])
                tt("dve", gat.ap[0:nr, :, :], gat.ap[0:nr, :, :], zs.ap[0:nr, :].unsqueeze(2).to_broadcast([nr, H, 16]), ALU.mult, [gat, zs], [gat])
            def slots(g, s_from, s_to):
                tl = tiles[g]; b = tbs[g]; nr = tl["nr"]; eidi = eidi2[g % 2]; xn = xn2[g % 2]; gat = gat2[g % 2]
                gflat = gat.ap[0:nr, :, :].rearrange("p h k -> p (h k)")
                for sl in range(s_from, s_to):
                    k = gcu[0] % 6; gcu[0] += 1
                    gb = GBUV[k]; ac = acol[k]; wc = wcol[k]; d_ = dg[k]
                    P.dma("pool", lambda e, o=gb.ap[0:nr, :], ix=eidi.ap[0:nr, sl:sl + 1]: e.indirect_dma_start(
                        out=o, out_offset=None, in_=UVb, in_offset=bass.IndirectOffsetOnAxis(ap=ix, axis=0), bounds_check=breg(e), oob_is_err=False),
                        reads=[eidi.t], writes=[gb.t], extra=conv_tok[-1:])
                    stt(gj.ap[0:nr, :], gb.ap[0:nr, 0:D], 1.0, xn.ap[0:nr, :], ALU.mult, ALU.mult, [gb, xn], [gj, ac], accum=ac.ap[0:nr, 0:1])
                    act(wc.ap[0:nr, 0:1], ac.ap[0:nr, 0:1], AF.Square, [ac], [wc])
                    act(wc.ap[0:nr, 0:1], wc.ap[0:nr, 0:1], AF.Identity, [wc, one_b], [wc], scale=0.044715, bias=one_b.ap[0:nr, 0:1])
                    act(wc.ap[0:nr, 0:1], wc.ap[0:nr, 0:1], AF.Copy, [wc, ac], [wc], scale=ac.ap[0:nr, 0:1])
                    act(wc.ap[0:nr, 0:1], wc.ap[0:nr, 0:1], AF.Sigmoid, [wc], [wc], scale=2.0 * GELU_C)
                    act(wc.ap[0:nr, 0:1], wc.ap[0:nr, 0:1], AF.Copy, [wc, ac], [wc], scale=ac.ap[0:nr, 0:1])
                    act(wc.ap[0:nr, 1:2], wc.ap[0:nr, 0:1], AF.Copy, [wc, gat], [wc], scale=gflat[:, sl:sl + 1])
                    act(d_.ap[0:nr, 0:nr], ident_b.ap[0:nr, 0:nr], AF.Copy, [ident_b, wc], [d_], scale=wc.ap[0:nr, 1:2])
                    for blk in range(4):
                        mm(PB[blk].ap[0:nr, :], PB[blk], d_.ap[0:nr, 0:nr], d_, gb.ap[0:nr, D + blk * 512:D + (blk + 1) * 512], gb, sl == 0, sl == H * 16 - 1)
                if s_to == H * 16:
                    for blk in range(4):
                        tt("dve", yout.ap[0:nr, blk * 512:(blk + 1) * 512], b.xf.ap[0:nr, blk * 512:(blk + 1) * 512], PB[blk].ap[0:nr, :], ALU.add, [b.xf, PB[blk]], [yout])
                    for (dst, p0, n) in tl["outs"]:
                        st("sp", dst, yout, yout.ap[p0:p0 + n, :])

            for _ in routing(0):
                pass
            for g in range(ng):
                rgen = routing(g + 1) if g + 1 < ng else None
                for sl in range(H * 16):
                    slots(g, sl, sl + 1)
                    if rgen is not None and sl % 2 == 1:
                        try:
                            next(rgen)
                        except StopIteration:
                            rgen = None
                if rgen is not None:
                    for _ in rgen:
                        pass
            P.barrier()

        alltiles = []
        for i in range(NOWN):
            alltiles.append(dict(rows=[(xo[i * 128:(i + 1) * 128, :], 0, 128)], nr=128, oa_row0=i * 128, sample=False,
                                 outs=[(y_o[i * 128:(i + 1) * 128, :], 0, 128)]))
        for g0 in range(0, NOWN, GSZ):
            group(alltiles[g0:g0 + GSZ])
        group([dict(rows=[(xs[0], 0, 16), (xs[1], 32, 16)], nr=48, oa_row0=NROW, sample=True,
                    outs=[(ys_o[0], 0, 16), (ys_o[1], 32, 16)])])
        P.emit()
    return nc


_NC_CACHE = {}


def _consts(cfg, j):
    ident = np.eye(128, dtype=np.float32)
    k = np.arange(128)
    tri = (k[:, None] <= k[None, :]).astype(np.float32)
    e127 = np.zeros((128, 128), np.float32); e127[127, :] = 1.0
    triblk = tri * ((k[:, None] // 32) == (k[None, :] // 32)).astype(np.float32)
    iota = np.tile(np.arange(256, dtype=np.float32), (128, 1))
    own = own_tiles(cfg, j)
    sel = np.zeros((cfg.NOWN, cfg.NT), np.float32)
    mk = np.zeros((128, cfg.NOWN * 4, 128), np.float32)
    pen = np.zeros((1, cfg.NOWN * 4), np.float32)
    for i, Tt in enumerate(own):
        sel[i, Tt] = 1.0
        lo = slot_lo(i)
        for r in range(4):
            t = lo + r
            if t < Tt:
                mk[:, i * 4 + r, :] = 1.0
            elif t == Tt:
                mk[:, i * 4 + r, :] = tri
            else:
                pen[0, i * 4 + r] = NEG
    return dict(c_ident=ident, c_tri=tri, c_e127=e127, c_triblk=triblk, c_iota=iota,
                c_sel=sel.reshape(1, -1), c_mk=np.ascontiguousarray(mk.reshape(128, -1)), c_pen=pen)


def kernel(x_prompt, x_sample, cache_k, cache_v, cache_logf, norm_mix_g, w_in, b_forget,
           q_norm_g, k_norm_g, v_norm_g, w_spatial, b_spatial, w_out_a, w_out_b, w_out,
           norm_ffn_g, w_peer_q, peer_q_norm_g, peer_sub_keys, peer_u, peer_v):
    f = lambda a: np.ascontiguousarray(np.asarray(a, dtype=np.float32))
    x_prompt = f(x_prompt); x_sample = f(x_sample)
    B, SEQ, _ = x_prompt.shape
    DB, DS, _ = x_sample.shape
    PAST = cache_k.shape[2]
    NK = peer_sub_keys.shape[3]
    assert B == 2 and DB == 16 and DS == 16
    cfg = Cfg(SEQ, PAST, NK)
    key = (SEQ, PAST, NK)
    if key not in _NC_CACHE:
        _NC_CACHE[key] = build(cfg)
    nc = _NC_CACHE[key]
    ck = f(cache_k)[0].reshape(16, PAST, 1024)
    cv = f(cache_v)[0].reshape(16, PAST, 1024)
    clf = f(cache_logf)[0]
    shared = dict(
        norm_mix_g=f(norm_mix_g).reshape(1, D), w_in=f(w_in)[0], b_forget=f(b_forget).reshape(1, 8),
        q_norm_g=f(q_norm_g).reshape(1, 128), k_norm_g=f(k_norm_g).reshape(1, 128), v_norm_g=f(v_norm_g).reshape(1, 1024),
        w_spatial=f(w_spatial)[0], b_spatial=f(b_spatial)[0], w_out_a=f(w_out_a)[0], w_out_b=f(w_out_b)[0],
        w_out=f(w_out)[0], norm_ffn_g=f(norm_ffn_g).reshape(1, D), w_peer_q=f(w_peer_q)[0],
        peer_q_norm_g=f(peer_q_norm_g).reshape(1, 256), peer_sub_keys=f(peer_sub_keys)[0],
        peer_u=f(peer_u)[0], peer_v=f(peer_v)[0])
    in_maps = []
    owns = []
    for c in range(8):
        b, j = divmod(c, 4)
        own = own_tiles(cfg, j)
        owns.append(own)
        xo = np.concatenate([x_prompt[b, t * 128:(t + 1) * 128] for t in own], axis=0)
        m = dict(shared)
        m.update(xb=x_prompt[b], xo=xo, xs=x_sample[2 * c:2 * c + 2], ck=ck[2 * c:2 * c + 2], cv=cv[2 * c:2 * c + 2],
                 clf=clf[2 * c:2 * c + 2])
        m.update(_consts(cfg, j))
        in_maps.append(m)
    res = run_bass_kernel_spmd(nc, in_maps, core_ids=list(range(8))).results
    y_p = np.zeros((2, SEQ, D), np.float32)
    k_p = np.zeros((1, 2, SEQ, 8, 128), np.float32)
    v_p = np.zeros((1, 2, SEQ, 8, 128), np.float32)
    lf_p = np.zeros((1, 2, SEQ, 8), np.float32)
    y_s = np.zeros((16, 16, D), np.float32)
    k_s = np.zeros((1, 16, 16, 8, 128), np.float32)
    v_s = np.zeros((1, 16, 16, 8, 128), np.float32)
    lf_s = np.zeros((1, 16, 16, 8), np.float32)
    sg_s = np.zeros((1, 16, 16, 1024), np.float32)
    for c in range(8):
        b, j = divmod(c, 4)
        r = res[c]
        for i, t in enumerate(owns[c]):
            sl = slice(t * 128, (t + 1) * 128)
            y_p[b, sl] = r["y"][i * 128:(i + 1) * 128]
            k_p[0, b, sl] = r["ko"][i * 128:(i + 1) * 128].reshape(128, 8, 128)
            v_p[0, b, sl] = r["vo"][i * 128:(i + 1) * 128].reshape(128, 8, 128)
            lf_p[0, b, sl] = r["lfo"][i * 128:(i + 1) * 128]
        y_s[2 * c:2 * c + 2] = r["ys"]
        k_s[0, 2 * c:2 * c + 2] = r["kso"].reshape(2, 16, 8, 128)
        v_s[0, 2 * c:2 * c + 2] = r["vso"].reshape(2, 16, 8, 128)
        lf_s[0, 2 * c:2 * c + 2] = r["lfs"]
        sg_s[0, 2 * c:2 * c + 2] = r["sgv"]
    return (y_p, y_s, k_p, v_p, lf_p, k_s, v_s, lf_s, sg_s)
```

```python
import numpy as np
from contextlib import ExitStack
import concourse.bass as bass
import concourse.mybir as mybir
from concourse.bass_utils import run_bass_kernel_spmd

F32 = mybir.dt.float32
BF16 = mybir.dt.bfloat16
I32 = mybir.dt.int32
U32 = mybir.dt.uint32
ALU = mybir.AluOpType
AF = mybir.ActivationFunctionType
AX = mybir.AxisListType

D = 2048
KC = 16
H = 8
HD = 128
O_Q, O_K, O_V, O_F, O_U, O_VS, O_GA, O_GB = 0, 1024, 2048, 3072, 3080, 4104, 5128, 7176
INW = 9224
RMS_EPS = 1e-6
SCALE = HD ** -0.5
GELU_C = 0.7978845608028654
TOPK = 16
NEG = -1e30
ENGS = ("pe", "act", "dve", "pool", "sp")
SAME_ENG_SYNC = True
GSZ = 3
NGB = 6


class T:
    __slots__ = ("lw", "rd", "slot")

    def __init__(self):
        self.lw = None
        self.rd = []
        self.slot = None


class Prog:
    def __init__(self, nc, es):
        self.nc = nc
        self.es = es
        self.ops = {e: [] for e in ENGS}
        self.cnt = {e: 0 for e in ENGS}
        self.esem = {e: es.enter_context(nc.semaphore("sem_" + e)) for e in ENGS}
        self.slots = []
        self.nobar = set()

    def slot_of(self, t):
        if t.slot is None:
            sem = self.es.enter_context(self.nc.semaphore("ds%d" % len(self.slots)))
            t.slot = [sem, 0]
            self.slots.append(t.slot)
        return t.slot

    def _deps(self, reads, writes, extra):
        toks = list(extra)
        for t in reads:
            if t.lw is not None:
                toks.append(t.lw)
        for t in writes:
            if t.lw is not None:
                toks.append(t.lw)
            toks.extend(t.rd)
        return toks

    def _upd(self, reads, writes, tok):
        for t in reads:
            t.rd.append(tok)
        for t in writes:
            t.lw = tok
            t.rd = []

    def op(self, eng, fn, reads=(), writes=(), extra=()):
        toks = self._deps(reads, writes, extra)
        self.cnt[eng] += 1
        tok = ("e", eng, self.cnt[eng])
        self.ops[eng].append((toks, fn, self.esem[eng], 1))
        self._upd(reads, writes, tok)
        return tok

    def dma(self, q, fn, reads=(), writes=(), slot_t=None, extra=()):
        toks = self._deps(reads, writes, extra)
        st = slot_t if slot_t is not None else (writes[0] if writes else reads[0])
        slot = self.slot_of(st)
        slot[1] += 16
        tok = ("s", slot[0], slot[1])
        self.ops[q].append((toks, fn, slot[0], 16))
        self._upd(reads, writes, tok)
        return tok

    def barrier(self, final=False):
        toks = [("e", e, self.cnt[e]) for e in ENGS if self.cnt[e] > 0]
        toks += [("s", s[0], s[1]) for s in self.slots if s[1] > 0 and (final or id(s) not in self.nobar)]
        for e in ENGS:
            self.ops[e].append((list(toks), None, None, 0))

    def emit(self):
        nc = self.nc
        self.barrier(final=True)
        with nc.Block() as blk:
            def replay(engname, e):
                waited = {}
                for toks, fn, sem, inc in self.ops[engname]:
                    for tk in toks:
                        if tk[0] == "e":
                            if tk[1] == engname and (engname == "pe" or not SAME_ENG_SYNC):
                                continue
                            s, v = self.esem[tk[1]], tk[2]
                        else:
                            s, v = tk[1], tk[2]
                        key = id(s)
                        if waited.get(key, 0) >= v:
                            continue
                        waited[key] = v
                        e.wait_ge(s, v)
                    if fn is not None:
                        fn(e).then_inc(sem, inc)

            @blk.tensor
            def _(e):
                replay("pe", e)

            @blk.scalar
            def _(e):
                replay("act", e)

            @blk.vector
            def _(e):
                replay("dve", e)

            @blk.gpsimd
            def _(e):
                replay("pool", e)

            @blk.sync
            def _(e):
                replay("sp", e)


class Buf:
    __slots__ = ("ap", "t")

    def __init__(self, ap):
        self.ap = ap
        self.t = T()


class Arena:
    def __init__(self, ap_f32, nfl):
        self.base = ap_f32
        self.n = nfl
        self.off = 0

    def mark(self):
        return self.off

    def release(self, m):
        self.off = m

    def alloc(self, shape, dt=F32):
        ne = int(np.prod(shape))
        esz = {F32: 4, BF16: 2, I32: 4, U32: 4}[dt]
        nfl = (ne * esz + 3) // 4
        nfl = (nfl + 1) // 2 * 2
        assert self.off + nfl <= self.n, "arena overflow: need %d have %d" % (self.off + nfl, self.n)
        v = self.base[:, self.off:self.off + nfl]
        self.off += nfl
        if dt != F32:
            v = v.bitcast(dt)
        v = v[:, 0:ne]
        if len(shape) == 2:
            v = v.rearrange("p (a b) -> p a b", a=shape[0])
        elif len(shape) == 3:
            v = v.rearrange("p (a b c) -> p a b c", a=shape[0], b=shape[1])
        return Buf(v)


class Cfg:
    def __init__(self, seq, past, nk):
        self.SEQ = seq
        self.PAST = past
        self.NK = nk
        self.NE = nk * nk
        self.NT = seq // 128
        assert self.NT % 8 == 0
        self.NOWN = self.NT // 4
        self.NPT = past // 128
        assert past % 128 == 0


def own_tiles(cfg, j):
    out = []
    for m in range(cfg.NT // 8):
        out += [8 * m + j, 8 * m + 7 - j]
    return out


def slot_lo(i):
    m, e = divmod(i, 2)
    return 8 * m + (0 if e == 0 else 4)


def build(cfg):
    nc = bass.Bass("TRN2", target_bir_lowering=False)
    SEQ, PAST, NK, NE, NT, NOWN, NPT = cfg.SEQ, cfg.PAST, cfg.NK, cfg.NE, cfg.NT, cfg.NOWN, cfg.NPT
    NROW = NOWN * 128
    es = ExitStack()
    with es:
        def din(n, s, d=F32):
            return nc.dram_tensor(n, list(s), d, kind="ExternalInput").ap()

        def dout(n, s, d=F32):
            return nc.dram_tensor(n, list(s), d, kind="ExternalOutput").ap()

        xb = din("xb", [SEQ, D])
        xo = din("xo", [NROW, D])
        xs = din("xs", [2, 16, D])
        ck = din("ck", [2, PAST, 1024])
        cv = din("cv", [2, PAST, 1024])
        clf = din("clf", [2, PAST, 8])
        norm_mix_g = din("norm_mix_g", [1, D])
        w_in = din("w_in", [D, INW])
        b_forget = din("b_forget", [1, 8])
        q_norm_g = din("q_norm_g", [1, 128])
        k_norm_g = din("k_norm_g", [1, 128])
        v_norm_g = din("v_norm_g", [1, 1024])
        w_spatial = din("w_spatial", [8, 128, 128])
        b_spatial = din("b_spatial", [8, 128])
        w_out_a = din("w_out_a", [1024, D])
        w_out_b = din("w_out_b", [1024, D])
        w_out = din("w_out", [D, D])
        norm_ffn_g = din("norm_ffn_g", [1, D])
        w_peer_q = din("w_peer_q", [D, D])
        peer_q_norm_g = din("peer_q_norm_g", [1, 256])
        sub_keys = din("peer_sub_keys", [8, 2, NK, 128])
        peer_u = din("peer_u", [NE, D])
        peer_v = din("peer_v", [NE, D])
        c_ident = din("c_ident", [128, 128])
        c_tri = din("c_tri", [128, 128])
        c_e127 = din("c_e127", [128, 128])
        c_triblk = din("c_triblk", [128, 128])
        c_iota = din("c_iota", [128, 256])
        c_sel = din("c_sel", [1, NOWN * NT])
        c_mk = din("c_mk", [128, NOWN * 4 * 128])
        c_pen = din("c_pen", [1, NOWN * 4])

        y_o = dout("y", [NROW, D])
        ys_o = dout("ys", [2, 16, D])
        ko_o = dout("ko", [NROW, 1024])
        vo_o = dout("vo", [NROW, 1024])
        lfo_o = dout("lfo", [NROW, 8])
        kso_o = dout("kso", [2, 16, 1024])
        vso_o = dout("vso", [2, 16, 1024])
        lfs_o = dout("lfs", [2, 16, 8])
        sgv_o = dout("sgv", [2, 16, 1024])

        KTs = nc.dram_tensor("KTs", [H, 128, SEQ], BF16, kind="Internal").ap()
        Vs = nc.dram_tensor("Vs", [H, SEQ, 128], BF16, kind="Internal").ap()
        OAs = nc.dram_tensor("OAs", [NROW + 128, 1024], BF16, kind="Internal").ap()
        UVb = nc.dram_tensor("UVb", [NE, 2 * D], BF16, kind="Internal").ap()
        Ub = UVb[:, 0:D]
        Vb = UVb[:, D:2 * D]
        t_conv = T()
        Wc_b = nc.dram_tensor("Wc_b", [D, INW - O_U], BF16, kind="Internal").ap()
        Woa_b = nc.dram_tensor("Woa_b", [1024, D], BF16, kind="Internal").ap()
        Wob_b = nc.dram_tensor("Wob_b", [1024, D], BF16, kind="Internal").ap()
        Wo_b = nc.dram_tensor("Wo_b", [D, D], BF16, kind="Internal").ap()
        Wpq_b = nc.dram_tensor("Wpq_b", [D, D], BF16, kind="Internal").ap()
        t_wconv = T()
        wconv_tok = []

        NFL = 52500
        arena_t = es.enter_context(nc.sbuf_tensor("arena", [128, NFL], F32))
        A = Arena(arena_t[:, :], NFL)
        pacc = es.enter_context(nc.psum_tensor("pacc", [128, 2048], F32))
        pw = es.enter_context(nc.psum_tensor("pw", [128, 2048], F32))
        PB = [Buf(pacc[:, b * 512:(b + 1) * 512]) for b in range(4)] + [Buf(pw[:, b * 512:(b + 1) * 512]) for b in range(4)]
        P = Prog(nc, es)

        def ld(q, dst, src_ap, dst_ap=None, extra=()):
            d_ap = dst.ap if dst_ap is None else dst_ap
            return P.dma(q, lambda e, d_ap=d_ap, s=src_ap: e.dma_start(out=d_ap, in_=s), writes=[dst.t], extra=extra)

        def ld_nc(q, dst, src_ap, dst_ap=None):
            d_ap = dst.ap if dst_ap is None else dst_ap
            return P.dma(q, lambda e, d_ap=d_ap, s=src_ap: e.dma_start(out=d_ap, in_=s, allow_slow_non_contiguous=True), writes=[dst.t])

        def st(q, dst_ap, src, src_ap=None, dst_t=None, extra=()):
            s_ap = src.ap if src_ap is None else src_ap
            w = [dst_t] if dst_t is not None else []
            return P.dma(q, lambda e, d=dst_ap, s=s_ap: e.dma_start(out=d, in_=s), reads=[src.t], writes=w, slot_t=src.t, extra=extra)

        def mm(out_ap, out_b, lhsT_ap, lhsT_b, rhs_ap, rhs_b, start, stop):
            return P.op("pe", lambda e, o=out_ap, l=lhsT_ap, r=rhs_ap, s=start, p=stop: e.matmul(out=o, lhsT=l, rhs=r, start=s, stop=p),
                        reads=[lhsT_b.t, rhs_b.t], writes=[out_b.t])

        def tp(out_ap, out_b, in_ap, in_b, ident, n=128):
            return P.op("pe", lambda e, o=out_ap, i=in_ap, idn=ident.ap[0:n, 0:n]: e.transpose(out=o, in_=i, identity=idn),
                        reads=[in_b.t, ident.t], writes=[out_b.t])

        def act(out_ap, in_ap, func, reads, writes, **kw):
            return P.op("act", lambda e, o=out_ap, i=in_ap, f=func, kw=kw: e.activation(out=o, in_=i, func=f, **kw),
                        reads=[b.t for b in reads], writes=[b.t for b in writes])

        def dve(fn, reads, writes):
            return P.op("dve", fn, reads=[b.t for b in reads], writes=[b.t for b in writes])

        def pool(fn, reads, writes):
            return P.op("pool", fn, reads=[b.t for b in reads], writes=[b.t for b in writes])

        def tt(eng, out_ap, in0, in1, op, reads, writes):
            return P.op(eng, lambda e, o=out_ap, a=in0, b=in1, op=op: e.tensor_tensor(out=o, in0=a, in1=b, op=op),
                        reads=[b.t for b in reads], writes=[b.t for b in writes])

        def ts(eng, out_ap, in0, s1, s2, op0, op1, reads, writes):
            if op1 is None:
                return P.op(eng, lambda e, o=out_ap, a=in0, s1=s1, op0=op0: e.tensor_scalar(out=o, in0=a, scalar1=s1, scalar2=None, op0=op0),
                            reads=[b.t for b in reads], writes=[b.t for b in writes])
            return P.op(eng, lambda e, o=out_ap, a=in0, s1=s1, s2=s2, op0=op0, op1=op1: e.tensor_scalar(out=o, in0=a, scalar1=s1, scalar2=s2, op0=op0, op1=op1),
                        reads=[b.t for b in reads], writes=[b.t for b in writes])

        def stt(out_ap, in0, scalar, in1, op0, op1, reads, writes, accum=None):
            if accum is None:
                return dve(lambda e, o=out_ap, a=in0, s=scalar, b=in1, op0=op0, op1=op1: e.scalar_tensor_tensor(out=o, in0=a, scalar=s, in1=b, op0=op0, op1=op1), reads, writes)
            return dve(lambda e, o=out_ap, a=in0, s=scalar, b=in1, op0=op0, op1=op1, ac=accum: e.scalar_tensor_tensor(out=o, in0=a, scalar=s, in1=b, op0=op0, op1=op1, accum_out=ac), reads, writes)

        def cp(eng, out_ap, in_ap, reads, writes):
            return P.op(eng, lambda e, o=out_ap, i=in_ap: e.tensor_copy(out=o, in_=i), reads=[b.t for b in reads], writes=[b.t for b in writes])

        def memset(eng, buf, ap, val):
            return P.op(eng, lambda e, a=ap, v=val: e.memset(a, v), writes=[buf.t])

        ident_f = A.alloc([128]); ld("sp", ident_f, c_ident)
        ident_b = A.alloc([128], BF16); ld("pool", ident_b, c_ident)
        tri_f = A.alloc([128]); ld("sp", tri_f, c_tri)
        tri_b = A.alloc([128], BF16); ld("pool", tri_b, c_tri)
        e127 = A.alloc([128]); ld("sp", e127, c_e127)
        triblk = A.alloc([128]); ld("sp", triblk, c_triblk)
        iota = A.alloc([256]); ld("sp", iota, c_iota)
        gmix = A.alloc([D]); ld("sp", gmix, norm_mix_g.partition_broadcast(128))
        gffn = A.alloc([D]); ld("sp", gffn, norm_ffn_g.partition_broadcast(128))
        gq = A.alloc([128]); ld("sp", gq, q_norm_g.partition_broadcast(128))
        gk = A.alloc([128]); ld("sp", gk, k_norm_g.partition_broadcast(128))
        gv = A.alloc([1024]); ld("sp", gv, v_norm_g.partition_broadcast(128))
        bfg = A.alloc([8]); ld("sp", bfg, b_forget.partition_broadcast(128))
        gpq = A.alloc([256]); ld("sp", gpq, peer_q_norm_g.partition_broadcast(128))
        zero1 = A.alloc([2]); memset("dve", zero1, zero1.ap, 0.0)
        m_persist = A.mark()

        def front(rows, nr, xf, hb, hT, junk, stat, gain, pbs, zero_pad=False):
            if zero_pad:
                memset("pool", xf, xf.ap[0:nr, :], 0.0)
            for (src, p0, n) in rows:
                ld("pool", xf, src, xf.ap[p0:p0 + n, :])
            rstd_of(xf, nr, D, junk, stat)
            stt(hb.ap[0:nr, :], xf.ap[0:nr, :], stat.ap[0:nr, 2:3], gain.ap[0:nr, :], ALU.mult, ALU.mult, [xf, stat, gain], [hb])
            transpose16(hb, nr, hT, pbs)

        def rstd_of(xf, nr, n, junk, stat):
            act(junk.ap[0:nr, 0:n], xf.ap[0:nr, 0:n], AF.Square, [xf], [junk, stat], accum_out=stat.ap[0:nr, 0:1])
            act(stat.ap[0:nr, 1:2], stat.ap[0:nr, 0:1], AF.Ln, [stat, eps_b], [stat], scale=1.0 / n, bias=eps_b.ap[0:nr, 0:1])
            act(stat.ap[0:nr, 2:3], stat.ap[0:nr, 1:2], AF.Exp, [stat], [stat], scale=-0.5)

        def transpose16(hb, nr, hT, pbs, nchunk=KC):
            for half in range((nchunk + 7) // 8):
                pb = pbs[half % len(pbs)]
                pv = pb.ap.bitcast(BF16)
                n8 = min(8, nchunk - half * 8)
                for c in range(n8):
                    kc = half * 8 + c
                    tp(pv[:, c * 128:c * 128 + nr], pb, hb.ap[0:nr, kc * 128:(kc + 1) * 128], hb, ident_b, nr)
                src = pv[:, 0:n8 * 128].rearrange("p (a b) -> p a b", a=n8)[:, :, 0:nr]
                dst = hT.ap[:, half * 8:half * 8 + n8, 0:nr]
                if half % 2 == 0:
                    cp("dve", dst, src, [pb], [hT])
                else:
                    act(dst, src, AF.Copy, [pb], [hT])

        eps_b = A.alloc([2]); memset("dve", eps_b, eps_b.ap, RMS_EPS)
        one_b = A.alloc([2]); memset("dve", one_b, one_b.ap, 1.0)
        m_persist = A.mark()

        def load_w(wt, w_dram, k_rows, c0, ncols, wap=None):
            kc = k_rows // 128
            src = w_dram[:, c0:c0 + ncols].rearrange("(kc p) n -> p kc n", p=128)
            dst = wt.ap[:, 0:kc, 0:ncols] if wap is None else wap
            return ld("pool", wt, src, dst)

        def load_wb(wt, w_bf, k_rows, c0, ncols, wap=None):
            kc = k_rows // 128
            src = w_bf[:, c0:c0 + ncols].rearrange("(kc p) n -> p kc n", p=128)
            dst = wt.ap[:, 0:kc, 0:ncols] if wap is None else wap
            return ld("sp", wt, src, dst, extra=wconv_tok[-1:])

        m_ab = A.mark()
        NFK = A.alloc([NT, 8])
        CE = A.alloc([NT, 8])
        mkb = A.alloc([NOWN * 4, 128], BF16); ld("pool", mkb, c_mk, mkb.ap.rearrange("p a b -> p (a b)"))
        CEsel = A.alloc([NOWN, 8])
        QT = A.alloc([H, NROW], BF16)
        QTs = A.alloc([H, 48], BF16)
        kTs_new = A.alloc([H, 48], BF16)
        vb_new = A.alloc([H, 132], BF16)
        nl_new = A.alloc([8])
        m_wa = A.mark()
        WA = A.alloc([KC, 2056], BF16)
        load_w(WA, w_in, D, O_K, 2056)

        xf1 = A.alloc([D])
        xf2 = [xf1, xf1]
        hb = A.alloc([D], BF16)
        hT2 = [A.alloc([KC, 128], BF16) for _ in range(2)]
        stat = A.alloc([4])
        k_sb = A.alloc([1024])
        tmpf = A.alloc([1024])
        junk = Buf(tmpf.ap.bitcast(BF16)); junk.t = tmpf.t
        kn = A.alloc([1024])
        v_sb = A.alloc([1024])
        kT_st = [A.alloc([H, 128], BF16) for _ in range(2)]
        vb_st = [A.alloc([1024], BF16) for _ in range(2)]
        ss8 = A.alloc([8]); r8 = A.alloc([8]); l8 = A.alloc([8])
        f_sb = A.alloc([8]); lf_sb = A.alloc([8]); fc_sb = A.alloc([8]); ce_prev = A.alloc([8])
        memset("dve", ce_prev, ce_prev.ap, 0.0)
        selb = A.alloc([NOWN, NT]); ld("sp", selb, c_sel.partition_broadcast(128), selb.ap.rearrange("p a b -> p (a b)"))

        def qknorm(src, nr, gain, dst, scr):
            s3 = src.ap[0:nr, :].rearrange("p (h d) -> p h d", h=H)
            tt("dve", scr.ap[0:nr, :], src.ap[0:nr, :], src.ap[0:nr, :], ALU.mult, [src], [scr])
            dve(lambda e, o=ss8.ap[0:nr, :], i=scr.ap[0:nr, :].rearrange("p (h d) -> p h d", h=H): e.tensor_reduce(out=o, in_=i, axis=AX.X, op=ALU.add), [scr], [ss8])
            act(l8.ap[0:nr, :], ss8.ap[0:nr, :], AF.Ln, [ss8, eps_b], [l8], scale=1.0 / HD, bias=eps_b.ap[0:nr, 0:1])
            act(r8.ap[0:nr, :], l8.ap[0:nr, :], AF.Exp, [l8], [r8], scale=-0.5)
            tt("dve", scr.ap[0:nr, :].rearrange("p (h d) -> p h d", h=H), s3, r8.ap[0:nr, :].unsqueeze(2).to_broadcast([nr, H, HD]), ALU.mult, [src, r8], [scr])
            tt("dve", dst.ap[0:nr, :].rearrange("p (h d) -> p h d", h=H), scr.ap[0:nr, :].rearrange("p (h d) -> p h d", h=H),
               gain.ap[0:nr, :].unsqueeze(1).to_broadcast([nr, H, HD]), ALU.mult, [scr, gain], [dst])

        def logf_of(nr):
            tt("dve", f_sb.ap[0:nr, :], f_sb.ap[0:nr, :], bfg.ap[0:nr, :], ALU.add, [f_sb, bfg], [f_sb])
            act(lf_sb.ap[0:nr, :], f_sb.ap[0:nr, :], AF.Exp, [f_sb], [lf_sb], scale=-1.0)
            ts("dve", lf_sb.ap[0:nr, :], lf_sb.ap[0:nr, :], 1.0, None, ALU.add, None, [lf_sb], [lf_sb])
            act(lf_sb.ap[0:nr, :], lf_sb.ap[0:nr, :], AF.Ln, [lf_sb], [lf_sb])
            ts("dve", lf_sb.ap[0:nr, :], lf_sb.ap[0:nr, :], -1.0, None, ALU.mult, None, [lf_sb], [lf_sb])

        kn2 = [kn, A.alloc([1024])]
        lf2 = [lf_sb, A.alloc([8])]

        def logf_of2(nr, lf):
            tt("dve", f_sb.ap[0:nr, :], f_sb.ap[0:nr, :], bfg.ap[0:nr, :], ALU.add, [f_sb, bfg], [f_sb])
            act(lf.ap[0:nr, :], f_sb.ap[0:nr, :], AF.Exp, [f_sb], [lf], scale=-1.0)
            ts("dve", lf.ap[0:nr, :], lf.ap[0:nr, :], 1.0, None, ALU.add, None, [lf], [lf])
            act(lf.ap[0:nr, :], lf.ap[0:nr, :], AF.Ln, [lf], [lf])
            ts("dve", lf.ap[0:nr, :], lf.ap[0:nr, :], -1.0, None, ALU.mult, None, [lf], [lf])

        def kvf_tile(it, rows, nr, owned, out_row0=None, sample=False):
            xf = xf2[it % 2]; hT = hT2[it % 2]
            knc = kn2[it % 2]; lf = lf2[it % 2]
            front(rows, nr, xf, hb, hT, junk, stat, gmix, [PB[4], PB[5]], zero_pad=sample)
            for blk in range(4):
                for kc in range(KC):
                    mm(PB[blk].ap[0:nr, :], PB[blk], hT.ap[:, kc, 0:nr], hT, WA.ap[:, kc, blk * 512:(blk + 1) * 512], WA, kc == 0, kc == KC - 1)
            for kc in range(KC):
                mm(PB[6].ap[0:nr, 0:8], PB[6], hT.ap[:, kc, 0:nr], hT, WA.ap[:, kc, 2048:2056], WA, kc == 0, kc == KC - 1)
            act(k_sb.ap[0:nr, 0:512], PB[0].ap[0:nr, :], AF.Copy, [PB[0]], [k_sb])
            act(k_sb.ap[0:nr, 512:1024], PB[1].ap[0:nr, :], AF.Copy, [PB[1]], [k_sb])
            cp("dve", f_sb.ap[0:nr, :], PB[6].ap[0:nr, 0:8], [PB[6]], [f_sb])
            kT = kT_st[it % 2]; vb = vb_st[it % 2]
            if owned:
                act(v_sb.ap[0:nr, 0:512], PB[2].ap[0:nr, :], AF.Copy, [PB[2]], [v_sb])
                act(v_sb.ap[0:nr, 512:1024], PB[3].ap[0:nr, :], AF.Copy, [PB[3]], [v_sb])
            else:
                act(vb.ap[:, 0:512], PB[2].ap[:, :], AF.Copy, [PB[2]], [vb])
                act(vb.ap[:, 512:1024], PB[3].ap[:, :], AF.Copy, [PB[3]], [vb])
            qknorm(k_sb, nr, gk, knc, tmpf)
            logf_of2(nr, lf)
            t0 = it * 128
            if owned:
                if not sample:
                    st("sp", ko_o[out_row0:out_row0 + 128, :], knc)
                    st("sp", vo_o[out_row0:out_row0 + 128, :], v_sb)
                    st("sp", lfo_o[out_row0:out_row0 + 128, :], lf)
                else:
                    for s in range(2):
                        st("sp", kso_o[s], knc, knc.ap[32 * s:32 * s + 16, :])
                        st("sp", vso_o[s], v_sb, v_sb.ap[32 * s:32 * s + 16, :])
                        st("sp", lfs_o[s], lf, lf.ap[32 * s:32 * s + 16, :])
                    memset("pool", vb_new, vb_new.ap[0:48, :, 128:129], 1.0)
                    cp("dve", vb_new.ap[0:48, :, 0:128], v_sb.ap[0:48, :].rearrange("p (h d) -> p h d", h=H), [v_sb], [vb_new])
            else:
                st("sp", Vs[:, t0:t0 + 128, :].rearrange("h t d -> t h d"), vb, vb.ap.rearrange("p (h d) -> p h d", h=H))

            def stage2():
                if owned:
                    if sample:
                        for h in range(H):
                            tp(PB[7].ap[:, h * 48:h * 48 + 48], PB[7], knc.ap[0:48, h * 128:(h + 1) * 128], knc, ident_f, 48)
                        cp("dve", kTs_new.ap.rearrange("p h t -> p (h t)"), PB[7].ap[:, 0:H * 48], [PB[7]], [kTs_new])
                        mm(PB[6].ap[0:48, 8:16], PB[6], triblk.ap[0:48, 0:48], triblk, lf.ap[0:48, :], lf, True, True)
                        ts("dve", nl_new.ap[0:48, :], PB[6].ap[0:48, 8:16], -1.0, None, ALU.mult, None, [PB[6]], [nl_new])
                    return
                for half in range(2):
                    for c in range(4):
                        h = half * 4 + c
                        tp(PB[7].ap[:, c * 128:(c + 1) * 128], PB[7], knc.ap[:, h * 128:(h + 1) * 128], knc, ident_f)
                    cp("dve", kT.ap[:, half * 4:half * 4 + 4, :].rearrange("p h t -> p (h t)"), PB[7].ap[:, :], [PB[7]], [kT])
                st("sp", KTs[:, :, t0:t0 + 128].rearrange("h d t -> d h t"), kT)
                mm(PB[6].ap[:, 8:16], PB[6], tri_f.ap, tri_f, lf.ap, lf, True, False)
                mm(PB[6].ap[:, 8:16], PB[6], e127.ap, e127, ce_prev.ap, ce_prev, False, True)
                cp("dve", fc_sb.ap, PB[6].ap[:, 8:16], [PB[6]], [fc_sb])
                ts("dve", NFK.ap[:, it, :], fc_sb.ap, -1.0, None, ALU.mult, None, [fc_sb], [NFK])
                mm(PB[6].ap[:, 16:24], PB[6], e127.ap, e127, fc_sb.ap, fc_sb, True, True)
                cp("dve", CE.ap[:, it, :], PB[6].ap[:, 16:24], [PB[6]], [CE])
                cp("dve", ce_prev.ap, fc_sb.ap, [fc_sb], [ce_prev])
            return stage2

        conv_tok = []
        P.nobar.add(id(P.slot_of(t_conv)))
        CR = NE // NT
        prev2 = None
        ce_toks = []
        for t in range(NT):
            s2 = kvf_tile(t, [(xb[t * 128:(t + 1) * 128, :], 0, 128)], 128, owned=False)
            if prev2 is not None:
                prev2()
            prev2 = s2
            for (dst_, src_) in ((Ub, peer_u), (Vb, peer_v)):
                conv_tok.append(P.dma("pool", lambda e, d=dst_[t * CR:(t + 1) * CR, :], s_=src_[t * CR:(t + 1) * CR, :]: e.dma_start(out=d, in_=s_), slot_t=t_conv,
                                      extra=[ce_toks[-3]] if len(ce_toks) >= 3 else []))
            if CE.t.lw is not None:
                ce_toks.append(CE.t.lw)
        prev2()
        selscr = A.alloc([8, NT])
        for i in range(NOWN):
            tt("dve", selscr.ap, CE.ap.rearrange("p t h -> p h t"), selb.ap[:, i, :].unsqueeze(1).to_broadcast([128, 8, NT]), ALU.mult, [CE, selb], [selscr])
            dve(lambda e, o=CEsel.ap[:, i, :], i_=selscr.ap: e.tensor_reduce(out=o, in_=i_, axis=AX.X, op=ALU.add), [selscr], [CEsel])
        for i in range(NOWN):
            kvf_tile(i, [(xo[i * 128:(i + 1) * 128, :], 0, 128)], 128, owned=True, out_row0=i * 128)()
        kvf_tile(NOWN, [(xs[0], 0, 16), (xs[1], 32, 16)], 48, owned=True, sample=True)()
        P.barrier()
        load_w(WA, w_in, D, O_Q, 1024, WA.ap[:, :, 0:1024])
        for i in range(NOWN + 1):
            sample = (i == NOWN)
            nr = 48 if sample else 128
            rows = [(xs[0], 0, 16), (xs[1], 32, 16)] if sample else [(xo[i * 128:(i + 1) * 128, :], 0, 128)]
            xf = xf2[i % 2]; hT = hT2[i % 2]
            front(rows, nr, xf, hb, hT, junk, stat, gmix, [PB[4], PB[5]], zero_pad=sample)
            for blk in range(2):
                for kc in range(KC):
                    mm(PB[blk].ap[0:nr, :], PB[blk], hT.ap[:, kc, 0:nr], hT, WA.ap[:, kc, blk * 512:(blk + 1) * 512], WA, kc == 0, kc == KC - 1)
            act(k_sb.ap[0:nr, 0:512], PB[0].ap[0:nr, :], AF.Copy, [PB[0]], [k_sb])
            act(k_sb.ap[0:nr, 512:1024], PB[1].ap[0:nr, :], AF.Copy, [PB[1]], [k_sb])
            qknorm(k_sb, nr, gq, kn, tmpf)
            if sample:
                for h in range(H):
                    tp(PB[7].ap[:, h * 48:h * 48 + 48], PB[7], kn.ap[0:48, h * 128:(h + 1) * 128], kn, ident_f, 48)
                cp("dve", QTs.ap.rearrange("p h t -> p (h t)"), PB[7].ap[:, 0:H * 48], [PB[7]], [QTs])
            else:
                for half in range(2):
                    pb = PB[6 + half]
                    for c in range(4):
                        h = half * 4 + c
                        tp(pb.ap[:, c * 128:(c + 1) * 128], pb, kn.ap[:, h * 128:(h + 1) * 128], kn, ident_f)
                    dst = QT.ap[:, half * 4:half * 4 + 4, i * 128:(i + 1) * 128]
                    src = pb.ap.rearrange("p (h t) -> p h t", h=4)
                    if half == 0:
                        cp("dve", dst, src, [pb], [QT])
                    else:
                        act(dst, src, AF.Copy, [pb], [QT])
        P.barrier()
        A.release(m_wa)
        P.nobar.add(id(P.slot_of(t_wconv)))
        for (dst_, src_, rows_) in ((Wc_b, w_in[:, O_U:INW], D), (Woa_b, w_out_a, 1024), (Wob_b, w_out_b, 1024), (Wo_b, w_out, D), (Wpq_b, w_peer_q, D)):
            for r0_ in range(0, rows_, 256):
                wconv_tok.append(P.dma("pool", lambda e, d=dst_[r0_:r0_ + 256, :], s_=src_[r0_:r0_ + 256, :]: e.dma_start(out=d, in_=s_), slot_t=t_wconv))
        KTh = [A.alloc([SEQ], BF16) for _ in range(2)]
        Vh = [A.alloc([NT, 130], BF16) for _ in range(2)]
        for b in Vh:
            memset("pool", b, b.ap[:, :, 128:129], 1.0)
        BS = min(4, NOWN)
        NBLK = NOWN // BS
        TS = 4 * BS
        PTw = [A.alloc([BS * 128], BF16) for _ in range(4)]
        onesrow = A.alloc([128], BF16)
        memset("dve", onesrow, onesrow.ap, 0.0)
        memset("dve", onesrow, onesrow.ap[0:1, :], 1.0)
        drow2 = [A.alloc([BS * 128], BF16) for _ in range(2)]
        for d_ in drow2:
            memset("dve", d_, d_.ap, 0.0)
        dsl = A.alloc([BS])
        penb = A.alloc([NOWN * 4]); ld("sp", penb, c_pen.partition_broadcast(128))
        bb2 = [A.alloc([NT]) for _ in range(2)]
        bu2 = [A.alloc([4]) for _ in range(2)]
        o_st = [A.alloc([128], BF16) for _ in range(2)]
        rden = A.alloc([2])
        ucount = 0
        units = []

        def mk_bulk(h, n, t, pre):
            kth = KTh[h % 2]; vh = Vh[h % 2]
            s0 = BS * n
            drow = drow2[(h * NBLK + n) % 2]; bb = bb2[(h * NBLK + n) % 2]
            f = min(ii for ii in range(BS) if slot_lo(s0 + ii) > t)
            c0 = f * 128; Wd = BS * 128 - c0
            return dict(h=h, n=n, kind="bulk", t=t, f=f, c0=c0, Wd=Wd, pre=pre)

        for h in range(H):
            for n in range(NBLK):
                s0 = BS * n
                ntb = slot_lo(s0 + BS - 1)
                first = True
                for t in range(ntb):
                    u = mk_bulk(h, n, t, first); first = False
                    units.append(u)
                for ii in range(BS):
                    for r in range(4):
                        units.append(dict(h=h, n=n, kind="diag", ii=ii, r=r, pre=first, pre_slot=(r == 0)))
                        first = False
        started = {}

        def stage1(u, idx):
            h, n = u["h"], u["n"]
            kth = KTh[h % 2]; vh = Vh[h % 2]
            s0 = BS * n; tR = TS * (n + 1) - 1
            drow = drow2[(h * NBLK + n) % 2]; bb = bb2[(h * NBLK + n) % 2]
            if u["pre"]:
                if n == 0:
                    ld("sp", kth, KTs[h])
                    ld("sp", vh, Vs[h].rearrange("(t p) d -> p t d", p=128), vh.ap[:, :, 0:128])
                tt("dve", dsl.ap[0:1, :], CEsel.ap[0:1, s0:s0 + BS, h], CE.ap[0:1, tR, h:h + 1].to_broadcast([1, BS]), ALU.subtract, [CEsel, CE], [dsl])
                ts("dve", drow.ap[0:1, :].rearrange("p (a b) -> p a b", a=BS), dsl.ap[0:1, :].unsqueeze(2).to_broadcast([1, BS, 128]), 1.0 / SCALE, None, ALU.mult, None, [dsl], [drow])
                ntb = slot_lo(s0 + BS - 1)
                if ntb > 0:
                    ts("dve", bb.ap[:, 0:ntb], NFK.ap[:, 0:ntb, h], CE.ap[:, tR, h:h + 1], None, ALU.add, None, [NFK, CE], [bb])
            psb = PB[4 + idx % 4]; pt = PTw[idx % 4]
            if u["kind"] == "bulk":
                t, f, c0, Wd = u["t"], u["f"], u["c0"], u["Wd"]
                mm(psb.ap[:, c0:c0 + Wd], psb, kth.ap[:, t * 128:(t + 1) * 128], kth, QT.ap[:, h, (s0 + f) * 128:(s0 + BS) * 128], QT, True, False)
                mm(psb.ap[:, c0:c0 + Wd], psb, onesrow.ap, onesrow, drow.ap[:, c0:c0 + Wd], drow, False, True)
                act(pt.ap[:, c0:c0 + Wd], psb.ap[:, c0:c0 + Wd], AF.Exp, [psb, bb], [pt], scale=SCALE, bias=bb.ap[:, t:t + 1])
            else:
                ii, r = u["ii"], u["r"]
                i = s0 + ii; lo = slot_lo(i); t = lo + r
                bu = bu2[(h * NOWN + i) % 2]
                if u["pre_slot"]:
                    tt("dve", bu.ap, NFK.ap[:, lo:lo + 4, h], penb.ap[:, i * 4:i * 4 + 4], ALU.add, [NFK, penb], [bu])
                    ts("dve", bu.ap, bu.ap, CE.ap[:, tR, h:h + 1], None, ALU.add, None, [bu, CE], [bu])
                mm(psb.ap[:, 0:128], psb, kth.ap[:, t * 128:(t + 1) * 128], kth, QT.ap[:, h, i * 128:(i + 1) * 128], QT, True, False)
                mm(psb.ap[:, 0:128], psb, onesrow.ap, onesrow, drow.ap[:, ii * 128:(ii + 1) * 128], drow, False, True)
                act(pt.ap[:, 0:128], psb.ap[:, 0:128], AF.Exp, [psb, bu], [pt], scale=SCALE, bias=bu.ap[:, r:r + 1])
                tt("pool", pt.ap[:, 0:128], pt.ap[:, 0:128], mkb.ap[:, i * 4 + r, :], ALU.mult, [pt, mkb], [pt])

        def stage2(u, idx):
            h, n = u["h"], u["n"]
            vh = Vh[h % 2]
            s0 = BS * n
            pt = PTw[idx % 4]
            if u["kind"] == "bulk":
                t, f = u["t"], u["f"]
                for ii in range(f, BS):
                    key = (h, n, ii)
                    mm(PB[ii].ap[:, 0:129], PB[ii], pt.ap[:, ii * 128:(ii + 1) * 128], pt, vh.ap[:, t, 0:129], vh, key not in started, False)
                    started[key] = True
            else:
                ii, r = u["ii"], u["r"]
                i = s0 + ii; lo = slot_lo(i); t = lo + r
                key = (h, n, ii)
                mm(PB[ii].ap[:, 0:129], PB[ii], pt.ap[:, 0:128], pt, vh.ap[:, t, 0:129], vh, key not in started, r == 3)
                started[key] = True
                if r == 3:
                    po = PB[ii]
                    ob = o_st[(h * NOWN + i) % 2]
                    dve(lambda e, o=rden.ap[:, 0:1], i_=po.ap[:, 128:129]: e.reciprocal(out=o, in_=i_), [po], [rden])
                    ts("dve", ob.ap, po.ap[:, 0:128], rden.ap[:, 0:1], None, ALU.mult, None, [po, rden], [ob])
                    st("sp", OAs[i * 128:(i + 1) * 128, h * 128:(h + 1) * 128], ob)

        SKEW = 2
        for idx in range(len(units) + SKEW):
            if idx < len(units):
                stage1(units[idx], idx)
            if idx - SKEW >= 0:
                stage2(units[idx - SKEW], idx - SKEW)
        P.barrier()
        A.release(m_wa)
        ckf = [A.alloc([NPT, 128]) for _ in range(2)]
        KTc = [A.alloc([NPT * 128], BF16) for _ in range(2)]
        Vc = [A.alloc([NPT, 130], BF16) for _ in range(2)]
        for b in Vc:
            memset("pool", b, b.ap[:, :, 128:129], 1.0)
        clf_sb = A.alloc([2, NPT, 8])
        nfk_c = A.alloc([2, NPT, 8])
        bias_c = A.alloc([2, NPT, 8])
        ce_c = A.alloc([8]); fcc = A.alloc([8]); cend = A.alloc([8])
        PTs = [A.alloc([16], BF16) for _ in range(4)]
        oas_st = A.alloc([1024], BF16)
        memset("pool", oas_st, oas_st.ap, 0.0)
        rden2 = A.alloc([2])
        for s in range(2):
            ld("sp", clf_sb, clf[s].rearrange("(t p) h -> p t h", p=128), clf_sb.ap[:, s, :, :])
            memset("dve", ce_c, ce_c.ap, 0.0)
            for t in range(NPT):
                mm(PB[1].ap[:, 0:8], PB[1], tri_f.ap, tri_f, clf_sb.ap[:, s, t, :], clf_sb, True, False)
                mm(PB[1].ap[:, 0:8], PB[1], e127.ap, e127, ce_c.ap, ce_c, False, True)
                cp("dve", fcc.ap, PB[1].ap[:, 0:8], [PB[1]], [fcc])
                ts("dve", nfk_c.ap[:, s, t, :], fcc.ap, -1.0, None, ALU.mult, None, [fcc], [nfk_c])
                cp("dve", ce_c.ap, fcc.ap, [fcc], [ce_c])
            mm(PB[1].ap[:, 8:16], PB[1], e127.ap, e127, fcc.ap, fcc, True, True)
            cp("dve", cend.ap, PB[1].ap[:, 8:16], [PB[1]], [cend])
            tt("dve", bias_c.ap[:, s, :, :], nfk_c.ap[:, s, :, :], cend.ap.unsqueeze(1).to_broadcast([128, NPT, 8]), ALU.add, [nfk_c, cend], [bias_c])
        def prep_load(u):
            s_, h = divmod(u, H)
            cf = ckf[u % 2]; vc = Vc[u % 2]
            ld("sp", cf, ck[s_].rearrange("(t p) (h d) -> p t h d", p=128, h=H)[:, :, h, :])
            ld("pool", vc, cv[s_].rearrange("(t p) (h d) -> p t h d", p=128, h=H)[:, :, h, :], vc.ap[:, :, 0:128])

        def prep_tp(u):
            cf = ckf[u % 2]; ktc = KTc[u % 2]
            for g4 in range((NPT + 3) // 4):
                pb = PB[2 + g4 % 2]
                n4 = min(4, NPT - g4 * 4)
                for c in range(n4):
                    t = g4 * 4 + c
                    tp(pb.ap[:, c * 128:(c + 1) * 128], pb, cf.ap[:, t, :], cf, ident_f)
                if g4 % 2 == 0:
                    cp("dve", ktc.ap[:, g4 * 512:g4 * 512 + n4 * 128], pb.ap[:, 0:n4 * 128], [pb], [ktc])
                else:
                    act(ktc.ap[:, g4 * 512:g4 * 512 + n4 * 128], pb.ap[:, 0:n4 * 128], AF.Copy, [pb], [ktc])

        sunits = [(u, t) for u in range(2 * H) for t in range(NPT + 1)]
        t_load = min(2, NPT)
        t_mid = max(t_load, NPT - 6)

        def sstage1(unit, idx):
            u, t = unit
            s_, h = divmod(u, H); r0 = 32 * s_
            ktc = KTc[u % 2]
            qT = QTs.ap[:, h, r0:r0 + 16]
            psb = PB[4 + idx % 4]; pt = PTs[idx % 4]
            if t == 0 and u == 0:
                prep_load(0); prep_tp(0)
            if t == t_load and u + 1 < 2 * H:
                prep_load(u + 1)
            if t == t_mid and u + 1 < 2 * H:
                prep_tp(u + 1)
            if t < NPT:
                mm(psb.ap[:, 0:16], psb, ktc.ap[:, t * 128:(t + 1) * 128], ktc, qT, QTs, True, True)
                act(pt.ap, psb.ap[:, 0:16], AF.Exp, [psb, bias_c], [pt], scale=SCALE, bias=bias_c.ap[:, s_, t, h:h + 1])
            else:
                mm(psb.ap[r0:r0 + 16, 0:16], psb, kTs_new.ap[:, h, r0:r0 + 16], kTs_new, qT, QTs, True, True)
                act(pt.ap[r0:r0 + 16, :], psb.ap[r0:r0 + 16, 0:16], AF.Exp, [psb, nl_new], [pt], scale=SCALE, bias=nl_new.ap[r0:r0 + 16, h:h + 1])
                tt("pool", pt.ap[r0:r0 + 16, :], pt.ap[r0:r0 + 16, :], tri_b.ap[r0:r0 + 16, r0:r0 + 16], ALU.mult, [pt, tri_b], [pt])

        def sstage2(unit, idx):
            u, t = unit
            s_, h = divmod(u, H); r0 = 32 * s_
            vc = Vc[u % 2]; po = PB[u % 2]; pt = PTs[idx % 4]
            if t < NPT:
                mm(po.ap[r0:r0 + 16, 0:129], po, pt.ap, pt, vc.ap[:, t, 0:129], vc, t == 0, False)
            else:
                mm(po.ap[r0:r0 + 16, 0:129], po, pt.ap[r0:r0 + 16, :], pt, vb_new.ap[r0:r0 + 16, h, 0:129], vb_new, False, True)
                dve(lambda e, o=rden2.ap[r0:r0 + 16, 0:1], i_=po.ap[r0:r0 + 16, 128:129]: e.reciprocal(out=o, in_=i_), [po], [rden2])
                ts("dve", oas_st.ap[r0:r0 + 16, h * 128:(h + 1) * 128], po.ap[r0:r0 + 16, 0:128], rden2.ap[r0:r0 + 16, 0:1], None, ALU.mult, None, [po, rden2], [oas_st])

        for idx in range(len(sunits) + 2):
            if idx < len(sunits):
                sstage1(sunits[idx], idx)
            if idx - 2 >= 0:
                sstage2(sunits[idx - 2], idx - 2)
        st("sp", OAs[NROW:NROW + 48, :], oas_st, oas_st.ap[0:48, :])
        P.barrier()

        A.release(m_ab)
        trilWT = A.alloc([H, 128], BF16)
        bsT = A.alloc([H])
        WS_f = A.alloc([H, 48])
        WS = A.alloc([H, 48], BF16)
        bsS = A.alloc([H])
        skT = A.alloc([16, NK])
        m_c = A.mark()
        wsp_f = A.alloc([H, 128])
        sk_f = A.alloc([16, 128])
        ld("sp", wsp_f, w_spatial.rearrange("g i j -> i g j"))
        for g in range(H):
            pb = PB[4 + g % 2]
            tp(pb.ap[:, 0:128], pb, wsp_f.ap[:, g, :], wsp_f, ident_f)
            tt("dve", trilWT.ap[:, g, :], pb.ap[:, 0:128], tri_f.ap, ALU.mult, [pb, tri_f], [trilWT])
        ld_nc("sp", bsT, b_spatial.rearrange("g i -> i g"))
        memset("dve", WS_f, WS_f.ap, 0.0)
        for s_ in range(2):
            for g in range(H):
                ld_nc("sp", WS_f, w_spatial[g, 0:16, 0:16].rearrange("i j -> j i"), WS_f.ap[32 * s_:32 * s_ + 16, g, 32 * s_:32 * s_ + 16])
        tt("dve", WS.ap[0:48, :, :], WS_f.ap[0:48, :, :], tri_f.ap[0:48, 0:48].unsqueeze(1).to_broadcast([48, H, 48]), ALU.mult, [WS_f, tri_f], [WS])
        memset("dve", bsS, bsS.ap, 0.0)
        for s_ in range(2):
            ld_nc("sp", bsS, b_spatial[:, 0:16].rearrange("g i -> i g"), bsS.ap[32 * s_:32 * s_ + 16, :])
        if NK < 128:
            memset("dve", sk_f, sk_f.ap, 0.0)
        ld("sp", sk_f, sub_keys.rearrange("h two k d -> k (h two) d"), sk_f.ap[0:NK, :, :])
        for g in range(16):
            pb = PB[4 + g % 2]
            tp(pb.ap[:, 0:128], pb, sk_f.ap[:, g, :], sk_f, ident_f)
            cp("dve", skT.ap[:, g, :], pb.ap[:, 0:NK], [pb], [skT])
        P.barrier()
        A.release(m_c)

        class TB:
            pass
        tbs = []
        for g in range(GSZ):
            b = TB()
            b.xf = A.alloc([D])
            b.hT = A.alloc([KC, 128], BF16)
            r1 = A.alloc([2560])
            b.q = Buf(r1.ap[:, 0:2048]); b.q.t = r1.t
            rb = r1.ap.bitcast(BF16)
            b.y = Buf(rb[:, 0:2048]); b.y.t = r1.t
            b.us = Buf(rb[:, 2048:3072]); b.us.t = r1.t
            b.obT = Buf(rb[:, 3072:4096].rearrange("p (a b) -> p a b", a=8)); b.obT.t = r1.t
            b.oaT = Buf(rb[:, 4096:5120].rearrange("p (a b) -> p a b", a=8)); b.oaT.t = r1.t
            b.stat = A.alloc([4])
            tbs.append(b)
        m_r2 = A.mark()
        Wb = [A.alloc([KC, 512], BF16) for _ in range(3)]
        Wb0lo = Buf(Wb[0].ap[:, 0:8, :]); Wb0lo.t = Wb[0].t
        Wb0hi = Buf(Wb[0].ap[:, 8:16, :]); Wb0hi.t = Wb[0].t
        junkc = A.alloc([D], BF16)
        hbc = A.alloc([D], BF16)
        xs_sb = A.alloc([512]); sq_sb = A.alloc([512]); u_sb = A.alloc([512])
        vg = A.alloc([1024]); vsn = A.alloc([1024]); vsnb = A.alloc([1024], BF16); ob_sb = A.alloc([1024], BF16)
        oa_sb = A.alloc([1024], BF16)
        sga = A.alloc([512]); sgb = A.alloc([512]); y1 = A.alloc([512])
        A.release(m_r2)
        xn2 = [A.alloc([D], BF16) for _ in range(2)]
        qn = A.alloc([D])
        qnT = A.alloc([16, 128]); yout = Buf(qnT.ap.rearrange("p a b -> p (a b)")); yout.t = qnT.t
        sc = A.alloc([16, NK])
        wk2 = [A.alloc([NK]) for _ in range(2)]
        t16 = A.alloc([16, 16])
        i16u = A.alloc([16, 16], U32)
        i16f = A.alloc([16, 16])
        cand2 = [A.alloc([256]) for _ in range(2)]
        cwk2 = [A.alloc([256]) for _ in range(2)]
        st16 = A.alloc([H, 16]); selu = A.alloc([H, 16], U32)
        selA = A.alloc([H, 16], U32); selB = A.alloc([H, 16], U32); aF = A.alloc([H, 16]); bF = A.alloc([H, 16]); ea = A.alloc([H, 16]); eb = A.alloc([H, 16])
        eidf = A.alloc([H * 16]); eidi2 = [A.alloc([H * 16], I32) for _ in range(2)]
        gat2 = [A.alloc([H, 16]) for _ in range(2)]; zs = A.alloc([H]); ss8c = A.alloc([8]); r8c = A.alloc([8]); l8c = A.alloc([8])
        NUV = 6
        acol = [A.alloc([2]) for _ in range(NUV)]; wcol = [A.alloc([2]) for _ in range(NUV)]
        dg = [A.alloc([128], BF16) for _ in range(6)]
        GBUV = [A.alloc([2 * D], BF16) for _ in range(6)]
        gj = A.alloc([D], BF16); junkd = gj
        gcu = [0]; gcv = [0]
        _breg = {}

        def breg(e):
            if "r" not in _breg:
                _breg["r"] = e.to_reg(NE - 1)
            return _breg["r"]

        def gelu_from(pb, nr, dst_ap, dst):
            act(xs_sb.ap[0:nr, :], pb.ap[0:nr, :], AF.Copy, [pb], [xs_sb])
            tt("dve", sq_sb.ap[0:nr, :], xs_sb.ap[0:nr, :], xs_sb.ap[0:nr, :], ALU.mult, [xs_sb], [sq_sb])
            tt("dve", sq_sb.ap[0:nr, :], sq_sb.ap[0:nr, :], xs_sb.ap[0:nr, :], ALU.mult, [sq_sb, xs_sb], [sq_sb])
            stt(u_sb.ap[0:nr, :], sq_sb.ap[0:nr, :], 0.044715, xs_sb.ap[0:nr, :], ALU.mult, ALU.add, [sq_sb, xs_sb], [u_sb])
            act(u_sb.ap[0:nr, :], u_sb.ap[0:nr, :], AF.Sigmoid, [u_sb], [u_sb], scale=2.0 * GELU_C)
            tt("dve", dst_ap, xs_sb.ap[0:nr, :], u_sb.ap[0:nr, :], ALU.mult, [xs_sb, u_sb], [dst])

        def proj(pb, nr, hT, wt, kc_n, ncols=512):
            for kc in range(kc_n):
                mm(pb.ap[0:nr, 0:ncols], pb, hT.ap[:, kc, 0:nr], hT, wt.ap[:, kc, 0:ncols], wt, kc == 0, kc == kc_n - 1)

        def group(tiles):
            ng = len(tiles)
            for g, tl in enumerate(tiles):
                b = tbs[g]
                front(tl["rows"], tl["nr"], b.xf, hbc, b.hT, junkc, b.stat, gmix, [PB[4], PB[5]], zero_pad=tl["sample"])
                ld("sp", oa_sb, OAs[tl["oa_row0"]:tl["oa_row0"] + tl["nr"], :], oa_sb.ap[0:tl["nr"], :])
                transpose16(oa_sb, tl["nr"], b.oaT, [PB[6]], nchunk=8)
            for blk in range(2):
                wt = Wb[blk]
                load_wb(wt, Wc_b, D, blk * 512, 512)
                for g, tl in enumerate(tiles):
                    b = tbs[g]; nr = tl["nr"]; pb = PB[g % 4]
                    proj(pb, nr, b.hT, wt, KC)
                    gelu_from(pb, nr, b.us.ap[0:nr, blk * 512:(blk + 1) * 512], b.us)
            wv = [Wb[2], Wb[0]]
            for blk in range(2):
                load_wb(wv[blk], Wc_b, D, (O_VS - O_U) + blk * 512, 512)
            for g, tl in enumerate(tiles):
                b = tbs[g]; nr = tl["nr"]
                for blk in range(2):
                    pb = PB[blk]
                    proj(pb, nr, b.hT, wv[blk], KC)
                    gelu_from(pb, nr, vg.ap[0:nr, blk * 512:(blk + 1) * 512], vg)
                rstd_of(vg, nr, 1024, junkc, b.stat)
                stt(vsn.ap[0:nr, :], vg.ap[0:nr, :], b.stat.ap[0:nr, 2:3], gv.ap[0:nr, :], ALU.mult, ALU.mult, [vg, b.stat, gv], [vsn])
                if tl["sample"]:
                    for s in range(2):
                        st("sp", sgv_o[s], vsn, vsn.ap[32 * s:32 * s + 16, :])
                cp("pool", vsnb.ap[0:nr, :], vsn.ap[0:nr, :], [vsn], [vsnb])
                for gg in range(H):
                    pb = PB[2 + gg // 4]
                    lhs = WS.ap[0:48, gg, :] if tl["sample"] else trilWT.ap[:, gg, :]
                    lb = WS if tl["sample"] else trilWT
                    mm(pb.ap[0:nr, (gg % 4) * 128:(gg % 4 + 1) * 128], pb, lhs, lb, vsnb.ap[0:nr, gg * 128:(gg + 1) * 128], vsnb, True, True)
                bs_ = bsS if tl["sample"] else bsT
                for gg in range(H):
                    pb = PB[2 + gg // 4]
                    stt(ob_sb.ap[0:nr, gg * 128:(gg + 1) * 128], pb.ap[0:nr, (gg % 4) * 128:(gg % 4 + 1) * 128], bs_.ap[0:nr, gg:gg + 1],
                        b.us.ap[0:nr, gg * 128:(gg + 1) * 128], ALU.add, ALU.mult, [pb, bs_, b.us], [ob_sb])
                transpose16(ob_sb, nr, b.obT, [PB[6]], nchunk=8)
            for blk in range(4):
                load_wb(Wb[0], Woa_b, 1024, blk * 512, 512, Wb[0].ap[:, 0:8, :])
                load_wb(Wb[0], Wob_b, 1024, blk * 512, 512, Wb[0].ap[:, 8:16, :])
                load_wb(Wb[1], Wc_b, D, (O_GA - O_U) + blk * 512, 512)
                load_wb(Wb[2], Wc_b, D, (O_GB - O_U) + blk * 512, 512)
                for g, tl in enumerate(tiles):
                    b = tbs[g]; nr = tl["nr"]
                    proj(PB[0], nr, b.oaT, Wb0lo, 8)
                    proj(PB[1], nr, b.obT, Wb0hi, 8)
                    proj(PB[2], nr, b.hT, Wb[1], KC)
                    proj(PB[3], nr, b.hT, Wb[2], KC)
                    act(sga.ap[0:nr, :], PB[2].ap[0:nr, :], AF.Sigmoid, [PB[2]], [sga])
                    act(sgb.ap[0:nr, :], PB[3].ap[0:nr, :], AF.Sigmoid, [PB[3]], [sgb])
                    tt("dve", y1.ap[0:nr, :], PB[0].ap[0:nr, :], sga.ap[0:nr, :], ALU.mult, [PB[0], sga], [y1])
                    tt("dve", sgb.ap[0:nr, :], PB[1].ap[0:nr, :], sgb.ap[0:nr, :], ALU.mult, [PB[1], sgb], [sgb])
                    tt("dve", b.y.ap[0:nr, blk * 512:(blk + 1) * 512], y1.ap[0:nr, :], sgb.ap[0:nr, :], ALU.add, [y1, sgb], [b.y])
            for g, tl in enumerate(tiles):
                b = tbs[g]
                transpose16(b.y, tl["nr"], b.hT, [PB[4], PB[5]])
            for blk in range(4):
                wt = Wb[blk % 3]
                load_wb(wt, Wo_b, D, blk * 512, 512)
                for g, tl in enumerate(tiles):
                    b = tbs[g]; nr = tl["nr"]; pb = PB[g % 4]
                    proj(pb, nr, b.hT, wt, KC)
                    tt("dve", b.xf.ap[0:nr, blk * 512:(blk + 1) * 512], b.xf.ap[0:nr, blk * 512:(blk + 1) * 512], pb.ap[0:nr, :], ALU.add, [b.xf, pb], [b.xf])
            for g, tl in enumerate(tiles):
                b = tbs[g]; nr = tl["nr"]
                rstd_of(b.xf, nr, D, junkc, b.stat)
                stt(hbc.ap[0:nr, :], b.xf.ap[0:nr, :], b.stat.ap[0:nr, 2:3], gffn.ap[0:nr, :], ALU.mult, ALU.mult, [b.xf, b.stat, gffn], [hbc])
                transpose16(hbc, nr, b.hT, [PB[4], PB[5]])
            for blk in range(4):
                wt = Wb[blk % 3]
                load_wb(wt, Wpq_b, D, blk * 512, 512)
                for g, tl in enumerate(tiles):
                    b = tbs[g]; nr = tl["nr"]; pb = PB[g % 4]
                    proj(pb, nr, b.hT, wt, KC)
                    act(b.q.ap[0:nr, blk * 512:(blk + 1) * 512], pb.ap[0:nr, :], AF.Copy, [pb], [b.q])
            P.barrier()
            def routing(g):
                tl = tiles[g]; b = tbs[g]; nr = tl["nr"]; eidi = eidi2[g % 2]; xn = xn2[g % 2]; gat = gat2[g % 2]
                rstd_of(b.xf, nr, D, junkd, b.stat)
                stt(xn.ap[0:nr, :], b.xf.ap[0:nr, :], b.stat.ap[0:nr, 2:3], gffn.ap[0:nr, :], ALU.mult, ALU.mult, [b.xf, b.stat, gffn], [xn])
                q3 = b.q.ap[0:nr, :].rearrange("p (h d) -> p h d", h=H)
                qn3 = qn.ap[0:nr, :].rearrange("p (h d) -> p h d", h=H)
                tt("dve", qn.ap[0:nr, :], b.q.ap[0:nr, :], b.q.ap[0:nr, :], ALU.mult, [b.q], [qn])
                dve(lambda e, o=ss8c.ap[0:nr, :], i_=qn3: e.tensor_reduce(out=o, in_=i_, axis=AX.X, op=ALU.add), [qn], [ss8c])
                act(l8c.ap[0:nr, :], ss8c.ap[0:nr, :], AF.Ln, [ss8c, eps_b], [l8c], scale=1.0 / 256, bias=eps_b.ap[0:nr, 0:1])
                act(r8c.ap[0:nr, :], l8c.ap[0:nr, :], AF.Exp, [l8c], [r8c], scale=-0.5)
                tt("dve", qn3, q3, r8c.ap[0:nr, :].unsqueeze(2).to_broadcast([nr, H, 256]), ALU.mult, [b.q, r8c], [qn])
                tt("dve", qn3, qn3, gpq.ap[0:nr, :].unsqueeze(1).to_broadcast([nr, H, 256]), ALU.mult, [qn, gpq], [qn])
                yield
                for half in range(4):
                    pb = PB[4 + half % 2]
                    for c in range(4):
                        gg = half * 4 + c
                        tp(pb.ap[:, c * 128:c * 128 + nr], pb, qn.ap[0:nr, gg * 128:(gg + 1) * 128], qn, ident_f, nr)
                    dst = qnT.ap[:, half * 4:half * 4 + 4, 0:nr]
                    src = pb.ap.rearrange("p (a b) -> p a b", a=4)[:, :, 0:nr]
                    if half % 2 == 0:
                        cp("dve", dst, src, [pb], [qnT])
                    else:
                        act(dst, src, AF.Copy, [pb], [qnT])
                    yield
                gpb = 512 // NK
                for gg in range(16):
                    pb = PB[6 + (gg // gpb) % 2]
                    col = (gg % gpb) * NK
                    mm(pb.ap[0:nr, col:col + NK], pb, qnT.ap[:, gg, 0:nr], qnT, skT.ap[:, gg, :], skT, True, True)
                    if gg % gpb == gpb - 1 or gg == 15:
                        g0 = (gg // gpb) * gpb
                        ng_ = gg - g0 + 1
                        cp("dve", sc.ap[0:nr, g0:g0 + ng_, :].rearrange("p a b -> p (a b)"), pb.ap[0:nr, 0:ng_ * NK], [pb], [sc])
                        yield
                for gg in range(16):
                    wk = wk2[gg % 2]
                    dve(lambda e, o=t16.ap[0:nr, gg, 0:8], i_=sc.ap[0:nr, gg, :]: e.max(out=o, in_=i_), [sc], [t16])
                    dve(lambda e, o=i16u.ap[0:nr, gg, 0:8], m=t16.ap[0:nr, gg, 0:8], v=sc.ap[0:nr, gg, :]: e.max_index(out=o, in_max=m, in_values=v), [sc, t16], [i16u])
                    dve(lambda e, o=wk.ap[0:nr, :], m=t16.ap[0:nr, gg, 0:8], v=sc.ap[0:nr, gg, :]: e.match_replace(out=o, in_to_replace=m, in_values=v, imm_value=NEG), [sc, t16], [wk])
                    dve(lambda e, o=t16.ap[0:nr, gg, 8:16], i_=wk.ap[0:nr, :]: e.max(out=o, in_=i_), [wk], [t16])
                    dve(lambda e, o=i16u.ap[0:nr, gg, 8:16], m=t16.ap[0:nr, gg, 8:16], v=wk.ap[0:nr, :]: e.max_index(out=o, in_max=m, in_values=v), [wk, t16], [i16u])
                    yield
                cp("dve", i16f.ap[0:nr, :, :], i16u.ap[0:nr, :, :], [i16u], [i16f])
                t4 = t16.ap[0:nr, :, :].rearrange("p (h two) k -> p h two k", two=2)
                i4 = i16f.ap[0:nr, :, :].rearrange("p (h two) k -> p h two k", two=2)
                for hh in range(H):
                    cand = cand2[hh % 2]; cwk = cwk2[hh % 2]
                    c3 = cand.ap[0:nr, :].rearrange("p (a b) -> p a b", a=16)
                    tt("dve", c3, t4[:, hh, 0, :].unsqueeze(2).to_broadcast([nr, 16, 16]), t4[:, hh, 1, :].unsqueeze(1).to_broadcast([nr, 16, 16]), ALU.add, [t16], [cand])
                    dve(lambda e, o=st16.ap[0:nr, hh, 0:8], i_=cand.ap[0:nr, :]: e.max(out=o, in_=i_), [cand], [st16])
                    dve(lambda e, o=selu.ap[0:nr, hh, 0:8], m=st16.ap[0:nr, hh, 0:8], v=cand.ap[0:nr, :]: e.max_index(out=o, in_max=m, in_values=v), [cand, st16], [selu])
                    dve(lambda e, o=cwk.ap[0:nr, :], m=st16.ap[0:nr, hh, 0:8], v=cand.ap[0:nr, :]: e.match_replace(out=o, in_to_replace=m, in_values=v, imm_value=NEG), [cand, st16], [cwk])
                    dve(lambda e, o=st16.ap[0:nr, hh, 8:16], i_=cwk.ap[0:nr, :]: e.max(out=o, in_=i_), [cwk], [st16])
                    dve(lambda e, o=selu.ap[0:nr, hh, 8:16], m=st16.ap[0:nr, hh, 8:16], v=cwk.ap[0:nr, :]: e.max_index(out=o, in_max=m, in_values=v), [cwk, st16], [selu])
                    yield
                ts("dve", selA.ap[0:nr, :, :], selu.ap[0:nr, :, :], 4, None, ALU.logical_shift_right, None, [selu], [selA])
                ts("dve", selB.ap[0:nr, :, :], selu.ap[0:nr, :, :], 15, None, ALU.bitwise_and, None, [selu], [selB])
                cp("dve", aF.ap[0:nr, :, :], selA.ap[0:nr, :, :], [selA], [aF])
                cp("dve", bF.ap[0:nr, :, :], selB.ap[0:nr, :, :], [selB], [bF])
                yield
                oh4 = qn.ap[0:nr, :].rearrange("p (h r c) -> p h r c", h=H, r=16)
                io4 = iota.ap[0:nr, 0:16].unsqueeze(1).unsqueeze(1).to_broadcast([nr, H, 16, 16])
                for which, (xF, eX) in enumerate(((aF, ea), (bF, eb))):
                    tt("dve", oh4, xF.ap[0:nr, :, :].unsqueeze(3).to_broadcast([nr, H, 16, 16]), io4, ALU.is_equal, [xF, iota], [qn])
                    yield
                    tt("dve", oh4, oh4, i4[:, :, which, :].unsqueeze(2).to_broadcast([nr, H, 16, 16]), ALU.mult, [qn, i16f], [qn])
                    yield
                    dve(lambda e, o=eX.ap[0:nr, :, :], i_=oh4: e.tensor_reduce(out=o, in_=i_, axis=AX.X, op=ALU.add), [qn], [eX])
                    yield
                stt(eidf.ap[0:nr, :], ea.ap[0:nr, :, :].rearrange("p h k -> p (h k)"), float(NK), eb.ap[0:nr, :, :].rearrange("p h k -> p (h k)"), ALU.mult, ALU.add, [ea, eb], [eidf])
                cp("dve", eidi.ap[0:nr, :], eidf.ap[0:nr, :], [eidf], [eidi])
                tt("dve", gat.ap[0:nr, :, :], st16.ap[0:nr, :, :], st16.ap[0:nr, :, 0:1].to_broadcast([nr, H, 16]), ALU.subtract, [st16], [gat])
                act(gat.ap[0:nr, :, :], gat.ap[0:nr, :, :], AF.Exp, [gat], [gat])
                dve(lambda e, o=zs.ap[0:nr, :], i_=gat.ap[0:nr, :, :]: e.tensor_reduce(out=o, in_=i_, axis=AX.X, op=ALU.add), [gat], [zs])
                dve(lambda e, o=zs.ap[0:nr, :], i_=zs.ap[0:nr, :]: e.reciprocal(out=o, in_=i_), [zs], [zs])
                tt("dve", gat.ap[0:nr, :, :], gat.ap[0:nr, :, :], zs.ap[0:nr, :].unsqueeze(2).to_broadcast([nr, H, 16]), ALU.mult, [gat, zs], [gat])
            def slots(g, s_from, s_to):
                tl = tiles[g]; b = tbs[g]; nr = tl["nr"]; eidi = eidi2[g % 2]; xn = xn2[g % 2]; gat = gat2[g % 2]
                gflat = gat.ap[0:nr, :, :].rearrange("p h k -> p (h k)")
                for sl in range(s_from, s_to):
                    k = gcu[0] % 6; gcu[0] += 1
                    gb = GBUV[k]; ac = acol[k]; wc = wcol[k]; d_ = dg[k]
                    P.dma("pool", lambda e, o=gb.ap[0:nr, :], ix=eidi.ap[0:nr, sl:sl + 1]: e.indirect_dma_start(
                        out=o, out_offset=None, in_=UVb, in_offset=bass.IndirectOffsetOnAxis(ap=ix, axis=0), bounds_check=breg(e), oob_is_err=False),
                        reads=[eidi.t], writes=[gb.t], extra=conv_tok[-1:])
                    stt(gj.ap[0:nr, :], gb.ap[0:nr, 0:D], 1.0, xn.ap[0:nr, :], ALU.mult, ALU.mult, [gb, xn], [gj, ac], accum=ac.ap[0:nr, 0:1])
                    act(wc.ap[0:nr, 0:1], ac.ap[0:nr, 0:1], AF.Square, [ac], [wc])
                    act(wc.ap[0:nr, 0:1], wc.ap[0:nr, 0:1], AF.Identity, [wc, one_b], [wc], scale=0.044715, bias=one_b.ap[0:nr, 0:1])
                    act(wc.ap[0:nr, 0:1], wc.ap[0:nr, 0:1], AF.Copy, [wc, ac], [wc], scale=ac.ap[0:nr, 0:1])
                    act(wc.ap[0:nr, 0:1], wc.ap[0:nr, 0:1], AF.Sigmoid, [wc], [wc], scale=2.0 * GELU_C)
                    act(wc.ap[0:nr, 0:1], wc.ap[0:nr, 0:1], AF.Copy, [wc, ac], [wc], scale=ac.ap[0:nr, 0:1])
                    act(wc.ap[0:nr, 1:2], wc.ap[0:nr, 0:1], AF.Copy, [wc, gat], [wc], scale=gflat[:, sl:sl + 1])
                    act(d_.ap[0:nr, 0:nr], ident_b.ap[0:nr, 0:nr], AF.Copy, [ident_b, wc], [d_], scale=wc.ap[0:nr, 1:2])
                    for blk in range(4):
                        mm(PB[blk].ap[0:nr, :], PB[blk], d_.ap[0:nr, 0:nr], d_, gb.ap[0:nr, D + blk * 512:D + (blk + 1) * 512], gb, sl == 0, sl == H * 16 - 1)
                if s_to == H * 16:
                    for blk in range(4):
                        tt("dve", yout.ap[0:nr, blk * 512:(blk + 1) * 512], b.xf.ap[0:nr, blk * 512:(blk + 1) * 512], PB[blk].ap[0:nr, :], ALU.add, [b.xf, PB[blk]], [yout])
                    for (dst, p0, n) in tl["outs"]:
                        st("sp", dst, yout, yout.ap[p0:p0 + n, :])

            for _ in routing(0):
                pass
            for g in range(ng):
                rgen = routing(g + 1) if g + 1 < ng else None
                for sl in range(H * 16):
                    slots(g, sl, sl + 1)
                    if rgen is not None and sl % 2 == 1:
                        try:
                            next(rgen)
                        except StopIteration:
                            rgen = None
                if rgen is not None:
                    for _ in rgen:
                        pass
            P.barrier()

        alltiles = []
        for i in range(NOWN):
            alltiles.append(dict(rows=[(xo[i * 128:(i + 1) * 128, :], 0, 128)], nr=128, oa_row0=i * 128, sample=False,
                                 outs=[(y_o[i * 128:(i + 1) * 128, :], 0, 128)]))
        for g0 in range(0, NOWN, GSZ):
            group(alltiles[g0:g0 + GSZ])
        group([dict(rows=[(xs[0], 0, 16), (xs[1], 32, 16)], nr=48, oa_row0=NROW, sample=True,
                    outs=[(ys_o[0], 0, 16), (ys_o[1], 32, 16)])])
        P.emit()
    return nc


_NC_CACHE = {}


def _consts(cfg, j):
    ident = np.eye(128, dtype=np.float32)
    k = np.arange(128)
    tri = (k[:, None] <= k[None, :]).astype(np.float32)
    e127 = np.zeros((128, 128), np.float32); e127[127, :] = 1.0
    triblk = tri * ((k[:, None] // 32) == (k[None, :] // 32)).astype(np.float32)
    iota = np.tile(np.arange(256, dtype=np.float32), (128, 1))
    own = own_tiles(cfg, j)
    sel = np.zeros((cfg.NOWN, cfg.NT), np.float32)
    mk = np.zeros((128, cfg.NOWN * 4, 128), np.float32)
    pen = np.zeros((1, cfg.NOWN * 4), np.float32)
    for i, Tt in enumerate(own):
        sel[i, Tt] = 1.0
        lo = slot_lo(i)
        for r in range(4):
            t = lo + r
            if t < Tt:
                mk[:, i * 4 + r, :] = 1.0
            elif t == Tt:
                mk[:, i * 4 + r, :] = tri
            else:
                pen[0, i * 4 + r] = NEG
    return dict(c_ident=ident, c_tri=tri, c_e127=e127, c_triblk=triblk, c_iota=iota,
                c_sel=sel.reshape(1, -1), c_mk=np.ascontiguousarray(mk.reshape(128, -1)), c_pen=pen)


def kernel(x_prompt, x_sample, cache_k, cache_v, cache_logf, norm_mix_g, w_in, b_forget,
           q_norm_g, k_norm_g, v_norm_g, w_spatial, b_spatial, w_out_a, w_out_b, w_out,
           norm_ffn_g, w_peer_q, peer_q_norm_g, peer_sub_keys, peer_u, peer_v):
    f = lambda a: np.ascontiguousarray(np.asarray(a, dtype=np.float32))
    x_prompt = f(x_prompt); x_sample = f(x_sample)
    B, SEQ, _ = x_prompt.shape
    DB, DS, _ = x_sample.shape
    PAST = cache_k.shape[2]
    NK = peer_sub_keys.shape[3]
    assert B == 2 and DB == 16 and DS == 16
    cfg = Cfg(SEQ, PAST, NK)
    key = (SEQ, PAST, NK)
    if key not in _NC_CACHE:
        _NC_CACHE[key] = build(cfg)
    nc = _NC_CACHE[key]
    ck = f(cache_k)[0].reshape(16, PAST, 1024)
    cv = f(cache_v)[0].reshape(16, PAST, 1024)
    clf = f(cache_logf)[0]
    shared = dict(
        norm_mix_g=f(norm_mix_g).reshape(1, D), w_in=f(w_in)[0], b_forget=f(b_forget).reshape(1, 8),
        q_norm_g=f(q_norm_g).reshape(1, 128), k_norm_g=f(k_norm_g).reshape(1, 128), v_norm_g=f(v_norm_g).reshape(1, 1024),
        w_spatial=f(w_spatial)[0], b_spatial=f(b_spatial)[0], w_out_a=f(w_out_a)[0], w_out_b=f(w_out_b)[0],
        w_out=f(w_out)[0], norm_ffn_g=f(norm_ffn_g).reshape(1, D), w_peer_q=f(w_peer_q)[0],
        peer_q_norm_g=f(peer_q_norm_g).reshape(1, 256), peer_sub_keys=f(peer_sub_keys)[0],
        peer_u=f(peer_u)[0], peer_v=f(peer_v)[0])
    in_maps = []
    owns = []
    for c in range(8):
        b, j = divmod(c, 4)
        own = own_tiles(cfg, j)
        owns.append(own)
        xo = np.concatenate([x_prompt[b, t * 128:(t + 1) * 128] for t in own], axis=0)
        m = dict(shared)
        m.update(xb=x_prompt[b], xo=xo, xs=x_sample[2 * c:2 * c + 2], ck=ck[2 * c:2 * c + 2], cv=cv[2 * c:2 * c + 2],
                 clf=clf[2 * c:2 * c + 2])
        m.update(_consts(cfg, j))
        in_maps.append(m)
    res = run_bass_kernel_spmd(nc, in_maps, core_ids=list(range(8))).results
    y_p = np.zeros((2, SEQ, D), np.float32)
    k_p = np.zeros((1, 2, SEQ, 8, 128), np.float32)
    v_p = np.zeros((1, 2, SEQ, 8, 128), np.float32)
    lf_p = np.zeros((1, 2, SEQ, 8), np.float32)
    y_s = np.zeros((16, 16, D), np.float32)
    k_s = np.zeros((1, 16, 16, 8, 128), np.float32)
    v_s = np.zeros((1, 16, 16, 8, 128), np.float32)
    lf_s = np.zeros((1, 16, 16, 8), np.float32)
    sg_s = np.zeros((1, 16, 16, 1024), np.float32)
    for c in range(8):
        b, j = divmod(c, 4)
        r = res[c]
        for i, t in enumerate(owns[c]):
            sl = slice(t * 128, (t + 1) * 128)
            y_p[b, sl] = r["y"][i * 128:(i + 1) * 128]
            k_p[0, b, sl] = r["ko"][i * 128:(i + 1) * 128].reshape(128, 8, 128)
            v_p[0, b, sl] = r["vo"][i * 128:(i + 1) * 128].reshape(128, 8, 128)
            lf_p[0, b, sl] = r["lfo"][i * 128:(i + 1) * 128]
        y_s[2 * c:2 * c + 2] = r["ys"]
        k_s[0, 2 * c:2 * c + 2] = r["kso"].reshape(2, 16, 8, 128)
        v_s[0, 2 * c:2 * c + 2] = r["vso"].reshape(2, 16, 8, 128)
        lf_s[0, 2 * c:2 * c + 2] = r["lfs"]
        sg_s[0, 2 * c:2 * c + 2] = r["sgv"]
    return (y_p, y_s, k_p, v_p, lf_p, k_s, v_s, lf_s, sg_s)
```
